# Optimizing a Trainium2 kernel written in Bass

```python
import math
import jax, jax.numpy as jnp
from jax import lax
import numpy as np

D_MODEL = 2048
BATCH = 4
SEQ = 2048
DEPTH = 2
DEC_BATCH = 128
DEC_SEQ = 1
PAST_LEN = 16384
PAGE_SIZE = 128

DN_HEADS = 8
DN_DK = 128
DN_DV = 128
DN_QK = DN_HEADS * DN_DK
DN_V = DN_HEADS * DN_DV
CONV_W = 4
CONV_CH = 2 * DN_QK + DN_V
CHUNK = 64
S5_CH = D_MODEL // 2
S5_GROUP = 16
S5_GROUPS = S5_CH // S5_GROUP
S5_STATE = 64
FFN_HIDDEN = -((-8 * D_MODEL) // (3 * 256)) * 256
IN_SIZES = (DN_QK, DN_QK, DN_V, DN_V, DN_HEADS, DN_HEADS, S5_CH, D_MODEL, D_MODEL)
IN_DIM = DN_QK + DN_QK + DN_V + DN_V + DN_HEADS + DN_HEADS + S5_CH + D_MODEL + D_MODEL
NORM_EPS = 1e-6
L2_EPS = 1e-6

kernel_name = "hybrid_gdn_s5_decoder_step"


def rms_norm(x, w):
    xf = x.astype(jnp.float32)
    y = xf * lax.rsqrt(jnp.mean(xf * xf, axis=-1, keepdims=True) + NORM_EPS)
    return (y * w.astype(jnp.float32)).astype(x.dtype)


def l2norm(x):
    return x * lax.rsqrt(jnp.sum(x * x, axis=-1, keepdims=True) + L2_EPS)


def causal_conv(x, buf, w):
    L = x.shape[1]
    xp = jnp.concatenate([buf.astype(x.dtype), x], axis=1)
    y = sum(xp[:, i:i + L] * w[i] for i in range(CONV_W))
    return jax.nn.silu(y), xp[:, L:]


def gated_delta_chunked(q, k, v, beta, g, s0):
    b, L, h, _ = q.shape
    dv = v.shape[-1]
    n = L // CHUNK

    def blk(t):
        t = t.reshape((b, n, CHUNK, h) + t.shape[3:])
        return jnp.moveaxis(jnp.moveaxis(t, 1, 0), 2, 3)

    q, k, v, beta, g = (blk(t) for t in (q, k, v, beta, g))
    gc = jnp.cumsum(g, axis=-1)
    idx = jnp.arange(CHUNK)
    causal = idx[:, None] >= idx[None, :]
    strict = idx[:, None] > idx[None, :]
    gamma = jnp.exp(jnp.where(causal, gc[..., :, None] - gc[..., None, :], -jnp.inf))
    kb = k * beta[..., None]
    vb = v * beta[..., None]
    a_mat = jnp.where(strict, jnp.einsum('nbhid,nbhjd->nbhij', kb, k) * gamma, 0.0)
    eye = jnp.eye(CHUNK, dtype=q.dtype)
    t_inv = lax.linalg.triangular_solve(eye + a_mat, jnp.broadcast_to(eye, a_mat.shape),
                                        left_side=True, lower=True)
    u = t_inv @ vb
    w = t_inv @ (kb * jnp.exp(gc)[..., None])
    qk = jnp.einsum('nbhid,nbhjd->nbhij', q, k) * gamma
    q_dec = q * jnp.exp(gc)[..., None]
    g_last = gc[..., -1]
    k_dec = k * jnp.exp(g_last[..., None] - gc)[..., None]

    def step(S, inp):
        u_c, w_c, qk_c, qd_c, kd_c, gl_c = inp
        v_new = u_c - w_c @ S
        o = qd_c @ S + qk_c @ v_new
        S = S * jnp.exp(gl_c)[..., None, None] + jnp.swapaxes(kd_c, -1, -2) @ v_new
        return S, o

    S, o = lax.scan(step, s0, (u, w, qk, q_dec, k_dec, g_last))
    o = jnp.swapaxes(jnp.moveaxis(o, 0, 1), 2, 3).reshape(b, L, h, dv)
    return o, S


def gated_delta_recurrent(q, k, v, beta, g, s0):
    def step(S, inp):
        q_t, k_t, v_t, b_t, g_t = inp
        S = S * jnp.exp(g_t)[..., None, None]
        v_new = (v_t - jnp.einsum('bhk,bhkv->bhv', k_t, S)) * b_t[..., None]
        S = S + jnp.einsum('bhk,bhv->bhkv', k_t, v_new)
        return S, jnp.einsum('bhk,bhkv->bhv', q_t, S)

    xs = tuple(jnp.moveaxis(t, 1, 0) for t in (q, k, v, beta, g))
    S, o = lax.scan(step, s0, xs)
    return jnp.moveaxis(o, 0, 1), S


def s5_scan(u, x0_re, x0_im, lam_re, lam_im, log_dt, b_re, b_im, c_re, c_im, d_skip):
    f32 = jnp.float32
    lam_re, lam_im, log_dt = lam_re.astype(f32), lam_im.astype(f32), log_dt.astype(f32)
    b_re, b_im, c_re, c_im = b_re.astype(f32), b_im.astype(f32), c_re.astype(f32), c_im.astype(f32)
    bsz, L, _ = u.shape
    ug = u.reshape(bsz, L, S5_GROUPS, S5_GROUP)
    dt = jnp.exp(log_dt)[:, None]
    mag = jnp.exp(lam_re * dt)
    ar = mag * jnp.cos(lam_im * dt)
    ai = mag * jnp.sin(lam_im * dt)
    nr = ar - 1.0
    den = lam_re * lam_re + lam_im * lam_im
    fr = (nr * lam_re + ai * lam_im) / den
    fi = (ai * lam_re - nr * lam_im) / den
    bbar_re = fr[..., None] * b_re - fi[..., None] * b_im
    bbar_im = fr[..., None] * b_im + fi[..., None] * b_re
    bu_re = jnp.einsum('gpc,blgc->blgp', bbar_re, ug)
    bu_im = jnp.einsum('gpc,blgc->blgp', bbar_im, ug)
    a_re = jnp.broadcast_to(ar, bu_re.shape)
    a_im = jnp.broadcast_to(ai, bu_im.shape)

    def combine(e1, e2):
        a1r, a1i, b1r, b1i = e1
        a2r, a2i, b2r, b2i = e2
        return (a2r * a1r - a2i * a1i, a2r * a1i + a2i * a1r,
                a2r * b1r - a2i * b1i + b2r, a2r * b1i + a2i * b1r + b2i)

    pr, pim, sr, si = lax.associative_scan(combine, (a_re, a_im, bu_re, bu_im), axis=1)
    x0r = x0_re.astype(f32)[:, None]
    x0i = x0_im.astype(f32)[:, None]
    x_re = pr * x0r - pim * x0i + sr
    x_im = pr * x0i + pim * x0r + si
    y = jnp.einsum('gcp,blgp->blgc', c_re, x_re) - jnp.einsum('gcp,blgp->blgc', c_im, x_im)
    y = y.reshape(bsz, L, S5_CH) + d_skip.astype(f32) * u
    return y, x_re[:, -1], x_im[:, -1]


def decoder_layer(x, conv_buf, dn_s, s5r, s5i, lw, chunked):
    (n1, w_in, conv_w, a_log, dt_bias, dn_nw, w_br_dn, lam_re, lam_im, log_dt,
     b_re, b_im, c_re, c_im, d_s, w_glu, w_br_s5, w_out, n2, wg, wu, wd) = lw
    f32 = jnp.float32
    bsz, L, _ = x.shape
    h = rms_norm(x, n1)
    proj = h @ w_in
    offs = np.cumsum(np.array(IN_SIZES))[:-1].tolist()
    q, k, v, z, bt, a, s5u, gd, gs = jnp.split(proj, offs, axis=-1)

    qkv, conv_new = causal_conv(jnp.concatenate([q, k, v], axis=-1), conv_buf, conv_w)
    qkv = qkv.astype(f32)
    q = l2norm(qkv[..., :DN_QK].reshape(bsz, L, DN_HEADS, DN_DK)) * (DN_DK ** -0.5)
    k = l2norm(qkv[..., DN_QK:2 * DN_QK].reshape(bsz, L, DN_HEADS, DN_DK))
    v = qkv[..., 2 * DN_QK:].reshape(bsz, L, DN_HEADS, DN_DV)
    beta = jax.nn.sigmoid(bt.astype(f32))
    g = -jnp.exp(a_log.astype(f32)) * jax.nn.softplus(a.astype(f32) + dt_bias.astype(f32))
    s0 = dn_s.astype(f32)
    if chunked:
        o, dn_new = gated_delta_chunked(q, k, v, beta, g, s0)
    else:
        o, dn_new = gated_delta_recurrent(q, k, v, beta, g, s0)
    zf = jax.nn.silu(z.astype(f32)).reshape(bsz, L, DN_HEADS, DN_DV)
    o = o * lax.rsqrt(jnp.mean(o * o, axis=-1, keepdims=True) + NORM_EPS) * dn_nw.astype(f32) * zf
    br_dn = o.reshape(bsz, L, DN_V).astype(x.dtype) @ w_br_dn

    y5, s5r_new, s5i_new = s5_scan(s5u.astype(f32), s5r, s5i, lam_re, lam_im, log_dt,
                                   b_re, b_im, c_re, c_im, d_s)
    g5 = jax.nn.gelu(y5)
    g5 = g5 * jax.nn.sigmoid(g5 @ w_glu.astype(f32))
    br_s5 = g5.astype(x.dtype) @ w_br_s5

    merged = jax.nn.sigmoid(gd) * br_dn + jax.nn.sigmoid(gs) * br_s5
    x = x + merged @ w_out

    h2 = rms_norm(x, n2)
    x = x + (jax.nn.silu(h2 @ wg) * (h2 @ wu)) @ wd
    return x, conv_new, dn_new, s5r_new, s5i_new


def setup_inputs(seed: int = 0) -> dict:
    key = jax.random.key(seed)
    ks = iter(jax.random.split(key, 40))
    f32 = jnp.float32

    def nrm(shape, scale):
        return jax.random.normal(next(ks), shape, f32) * scale

    def unif(shape, lo, hi):
        return jax.random.uniform(next(ks), shape, f32, lo, hi)

    x_prompt = nrm((BATCH, SEQ, D_MODEL), 1.0)
    x_sample = nrm((DEC_BATCH, DEC_SEQ, D_MODEL), 1.0)
    state_dn_conv = nrm((DEPTH, DEC_BATCH, CONV_W - 1, CONV_CH), 1.0)
    state_dn_ssm = nrm((DEPTH, DEC_BATCH, DN_HEADS, DN_DK, DN_DV), 0.1)
    state_s5_re = nrm((DEPTH, DEC_BATCH, S5_GROUPS, S5_STATE), 0.3)
    state_s5_im = nrm((DEPTH, DEC_BATCH, S5_GROUPS, S5_STATE), 0.3)

    norm1 = 1.0 + nrm((DEPTH, D_MODEL), 0.01)
    w_in = nrm((DEPTH, D_MODEL, IN_DIM), D_MODEL ** -0.5)
    dn_conv_w = nrm((DEPTH, CONV_W, CONV_CH), CONV_W ** -0.5)
    dn_a_log = jnp.log(unif((DEPTH, DN_HEADS), 1.0, 16.0))
    dt = jnp.exp(unif((DEPTH, DN_HEADS), math.log(1e-3), math.log(1e-1)))
    dn_dt_bias = dt + jnp.log(-jnp.expm1(-dt))
    dn_norm_w = 1.0 + nrm((DEPTH, DN_DV), 0.01)
    w_br_dn = nrm((DEPTH, DN_V, D_MODEL), DN_V ** -0.5)
    s5_lam_re = -0.5 + nrm((DEPTH, S5_GROUPS, S5_STATE), 0.01)
    s5_lam_im = jnp.broadcast_to(math.pi * jnp.arange(S5_STATE, dtype=f32), (DEPTH, S5_GROUPS, S5_STATE)) + 0.0
    s5_log_dt = unif((DEPTH, S5_GROUPS), math.log(1e-3), math.log(1e-1))
    s5_b_re = nrm((DEPTH, S5_GROUPS, S5_STATE, S5_GROUP), (2 * S5_GROUP) ** -0.5)
    s5_b_im = nrm((DEPTH, S5_GROUPS, S5_STATE, S5_GROUP), (2 * S5_GROUP) ** -0.5)
    s5_c_re = nrm((DEPTH, S5_GROUPS, S5_GROUP, S5_STATE), S5_STATE ** -0.5)
    s5_c_im = nrm((DEPTH, S5_GROUPS, S5_GROUP, S5_STATE), S5_STATE ** -0.5)
    s5_d = nrm((DEPTH, S5_CH), 1.0)
    w_glu = nrm((DEPTH, S5_CH, S5_CH), S5_CH ** -0.5)
    w_br_s5 = nrm((DEPTH, S5_CH, D_MODEL), S5_CH ** -0.5)
    w_out = nrm((DEPTH, D_MODEL, D_MODEL), D_MODEL ** -0.5)
    norm2 = 1.0 + nrm((DEPTH, D_MODEL), 0.01)
    w_ffn_gate = nrm((DEPTH, D_MODEL, FFN_HIDDEN), D_MODEL ** -0.5)
    w_ffn_up = nrm((DEPTH, D_MODEL, FFN_HIDDEN), D_MODEL ** -0.5)
    w_ffn_down = nrm((DEPTH, FFN_HIDDEN, D_MODEL), FFN_HIDDEN ** -0.5)
    norm_f = 1.0 + nrm((D_MODEL,), 0.01)
    return {
        "x_prompt": x_prompt, "x_sample": x_sample,
        "state_dn_conv": state_dn_conv, "state_dn_ssm": state_dn_ssm,
        "state_s5_re": state_s5_re, "state_s5_im": state_s5_im,
        "norm1": norm1, "w_in": w_in, "dn_conv_w": dn_conv_w, "dn_a_log": dn_a_log,
        "dn_dt_bias": dn_dt_bias, "dn_norm_w": dn_norm_w, "w_br_dn": w_br_dn,
        "s5_lam_re": s5_lam_re, "s5_lam_im": s5_lam_im, "s5_log_dt": s5_log_dt,
        "s5_b_re": s5_b_re, "s5_b_im": s5_b_im, "s5_c_re": s5_c_re, "s5_c_im": s5_c_im,
        "s5_d": s5_d, "w_glu": w_glu, "w_br_s5": w_br_s5, "w_out": w_out, "norm2": norm2,
        "w_ffn_gate": w_ffn_gate, "w_ffn_up": w_ffn_up, "w_ffn_down": w_ffn_down,
        "norm_f": norm_f,
    }


def reference(x_prompt, x_sample, state_dn_conv, state_dn_ssm, state_s5_re, state_s5_im,
              norm1, w_in, dn_conv_w, dn_a_log, dn_dt_bias, dn_norm_w, w_br_dn,
              s5_lam_re, s5_lam_im, s5_log_dt, s5_b_re, s5_b_im, s5_c_re, s5_c_im, s5_d,
              w_glu, w_br_s5, w_out, norm2, w_ffn_gate, w_ffn_up, w_ffn_down, norm_f):
    weights = (norm1, w_in, dn_conv_w, dn_a_log, dn_dt_bias, dn_norm_w, w_br_dn,
               s5_lam_re, s5_lam_im, s5_log_dt, s5_b_re, s5_b_im, s5_c_re, s5_c_im, s5_d,
               w_glu, w_br_s5, w_out, norm2, w_ffn_gate, w_ffn_up, w_ffn_down)

    def run(x, conv, ssm, sre, sim, chunked):
        conv_l, ssm_l, sre_l, sim_l = [], [], [], []
        for i in range(DEPTH):
            lw = tuple(w[i] for w in weights)
            x, c_new, d_new, r_new, m_new = decoder_layer(x, conv[i], ssm[i], sre[i], sim[i], lw, chunked)
            conv_l.append(c_new)
            ssm_l.append(d_new)
            sre_l.append(r_new)
            sim_l.append(m_new)
        return (rms_norm(x, norm_f), jnp.stack(conv_l), jnp.stack(ssm_l),
                jnp.stack(sre_l), jnp.stack(sim_l))

    f32 = jnp.float32
    p_conv0 = jnp.zeros((DEPTH, BATCH, CONV_W - 1, CONV_CH), x_prompt.dtype)
    p_ssm0 = jnp.zeros((DEPTH, BATCH, DN_HEADS, DN_DK, DN_DV), f32)
    p_s50 = jnp.zeros((DEPTH, BATCH, S5_GROUPS, S5_STATE), f32)
    y_prompt, p_dn_conv, p_dn_ssm, p_s5_re, p_s5_im = run(x_prompt, p_conv0, p_ssm0, p_s50, p_s50, True)
    y_sample, s_dn_conv, s_dn_ssm, s_s5_re, s_s5_im = run(
        x_sample, state_dn_conv, state_dn_ssm, state_s5_re, state_s5_im, False)
    return (y_prompt, y_sample, p_dn_conv, p_dn_ssm, p_s5_re, p_s5_im,
            s_dn_conv, s_dn_ssm, s_s5_re, s_s5_im)
```

```python
import math
import numpy as np
import concourse.bass as bass
import concourse.mybir as mybir
from concourse.bass_utils import run_bass_kernel_spmd
from contextlib import ExitStack

F32 = mybir.dt.float32
BF16 = mybir.dt.bfloat16
AF = mybir.ActivationFunctionType
ALU = mybir.AluOpType
AX = mybir.AxisListType

SAME_ENG_SYNC = True
D = 2048
NH = 8
FFN = 5632
IN_DIM = 9232
BIG = 30000.0
PI = math.pi


class Res:
    __slots__ = ("name", "last_writer", "readers", "dsem")

    def __init__(self, name):
        self.name = name
        self.last_writer = None
        self.readers = []
        self.dsem = None


class Buf:
    def __init__(self, t, name):
        self.t = t
        self.res = Res(name)


class Op:
    __slots__ = ("eng", "fn", "deps", "signaled", "seq", "is_dma", "dsem", "dval")

    def __init__(self, eng, fn, is_dma=False):
        self.eng = eng
        self.fn = fn
        self.deps = []
        self.signaled = False
        self.seq = None
        self.is_dma = is_dma
        self.dsem = None
        self.dval = None


class Prog:
    ENGS = ("pe", "act", "dve", "pool", "sp")

    def __init__(self, nc):
        self.nc = nc
        self.ops = []
        self.stack = ExitStack()
        self.dsem_count = {}
        self.n_dsem = 0
        self.sb_bytes = 0

    def sbuf(self, name, shape, dtype):
        t = self.stack.enter_context(self.nc.sbuf_tensor(name, list(shape), dtype))
        n = 1
        for s in shape[1:]:
            n *= s
        self.sb_bytes += n * mybir.dt.size(dtype)
        return Buf(t, name)

    def psum(self, name, shape, dtype=F32):
        return Buf(self.stack.enter_context(self.nc.psum_tensor(name, list(shape), dtype)), name)

    def op(self, eng, fn, reads=(), writes=(), dma=False):
        o = Op(eng, fn, dma)
        deps = set()
        for r in reads:
            excl = getattr(r, "excl", False)
            r = r.res
            if r.last_writer is not None:
                deps.add(r.last_writer)
            if excl:
                for rd in r.readers:
                    if rd.eng != eng:
                        deps.add(rd)
        for w in writes:
            w = w.res
            if w.last_writer is not None:
                deps.add(w.last_writer)
            for rd in w.readers:
                deps.add(rd)
        for d in deps:
            if d.eng == eng and not d.is_dma:
                if eng == "pe" or not SAME_ENG_SYNC:
                    continue
            d.signaled = True
            o.deps.append(d)
        for r in reads:
            r.res.readers.append(o)
        for w in writes:
            w.res.last_writer = o
            w.res.readers = []
        if dma:
            key = writes[0].res
            if key.dsem is None:
                key.dsem = ("d", self.n_dsem)
                self.n_dsem += 1
                self.dsem_count[key.dsem] = 0
            self.dsem_count[key.dsem] += 1
            o.dsem = key.dsem
            o.dval = 16 * self.dsem_count[key.dsem]
            o.signaled = True
        self.ops.append(o)
        return o

    def emit(self):
        nc = self.nc
        cnt = {e: 0 for e in self.ENGS}
        for o in self.ops:
            if o.is_dma:
                continue
            if o.signaled:
                cnt[o.eng] += 1
                o.seq = cnt[o.eng]
        sems = {e: self.stack.enter_context(nc.semaphore("s_" + e)) for e in self.ENGS}
        dsems = {k: self.stack.enter_context(nc.semaphore("sd%d" % k[1])) for k in self.dsem_count}
        per_eng = {e: [] for e in self.ENGS}
        for o in self.ops:
            per_eng[o.eng].append(o)
        final_waits = [(dsems[k], 16 * v) for k, v in self.dsem_count.items()]

        def make(e, ops):
            def body(engobj):
                known = {}
                for o in ops:
                    need = {}
                    for d in o.deps:
                        k, v = (d.dsem, d.dval) if d.is_dma else (d.eng, d.seq)
                        if known.get(k, 0) >= v:
                            continue
                        if need.get(k, 0) < v:
                            need[k] = v
                    for k, v in need.items():
                        engobj.wait_ge(dsems[k] if isinstance(k, tuple) else sems[k], v)
                        known[k] = v
                    ins = o.fn(engobj)
                    if o.is_dma:
                        ins.then_inc(dsems[o.dsem], 16)
                    elif o.signaled:
                        ins.then_inc(sems[o.eng], 1)
                if e == "sp":
                    for s, v in final_waits:
                        engobj.wait_ge(s, v)
            return body

        with nc.Block() as block:
            deco = {"pe": block.tensor, "act": block.scalar, "dve": block.vector,
                    "pool": block.gpsimd, "sp": block.sync}
            for e in self.ENGS:
                if per_eng[e] or e == "sp":
                    deco[e](make(e, per_eng[e]))


def host_consts():
    c = {}
    c["c_ident"] = np.eye(128, dtype=np.float32)
    m = np.zeros((64, 3, 64), np.float32)
    p = np.arange(64)[:, None]
    f = np.arange(64)[None, :]
    m[:, 0, :] = np.where(f > p, BIG, 0.0)
    m[:, 1, :] = np.where(f < p, -BIG, 0.0)
    m[:, 2, :] = np.where(f < p, 1.0, 0.0)
    c["c_mask"] = m
    sel = np.zeros((8, 8, 128), np.float32)
    for h in range(8):
        sel[h, h, :] = 1.0
    c["c_sel"] = sel
    c["c_iota"] = np.broadcast_to(np.arange(1, 513, dtype=np.float32), (128, 512)).copy()
    mE = np.zeros((128, 4, 8), np.float32)
    for gl2 in range(2):
        for mm in range(4):
            mE[gl2 * 64:(gl2 + 1) * 64, mm, 2 * mm + gl2] = 1.0
    c["c_maskE"] = mE
    return c


class Ctx:
    pass


def build(n_ptiles=4, n_layers=2, do_sample=True, dbg=None):
    nc = bass.Bass("TRN2", target_bir_lowering=False)
    P = Prog(nc)
    dbg = dbg or []

    def din(name, shape):
        return nc.dram_tensor(name, list(shape), F32, kind="ExternalInput").ap()

    def dout(name, shape):
        return nc.dram_tensor(name, list(shape), F32, kind="ExternalOutput").ap()

    I = {}
    I["x_prompt"] = din("x_prompt", [2048, D])
    I["x_sample"] = din("x_sample", [16, D])
    I["state_dn_conv"] = din("state_dn_conv", [2, 16, 3, 3072])
    I["state_dn_ssm"] = din("state_dn_ssm", [2, 16, 8, 128, 128])
    I["state_s5_re"] = din("state_s5_re", [2, 16, 64, 64])
    I["state_s5_im"] = din("state_s5_im", [2, 16, 64, 64])
    for nm, sh in [("norm1", [2, D]), ("w_in", [2, D, IN_DIM]), ("dn_conv_w", [2, 4, 3072]), ("dn_a_log", [2, 8]),
                   ("dn_dt_bias", [2, 8]), ("dn_norm_w", [2, 128]), ("w_br_dn", [2, 1024, D]),
                   ("s5_lam_re", [2, 64, 64]), ("s5_lam_im", [2, 64, 64]), ("s5_log_dt", [2, 64]),
                   ("s5_b_re", [2, 64, 64, 16]), ("s5_b_im", [2, 64, 64, 16]), ("s5_c_re", [2, 64, 16, 64]),
                   ("s5_c_im", [2, 64, 16, 64]), ("s5_d", [2, 1024]), ("w_glu", [2, 1024, 1024]),
                   ("w_br_s5", [2, 1024, D]), ("w_out", [2, D, D]), ("norm2", [2, D]),
                   ("w_ffn_gate", [2, D, FFN]), ("w_ffn_up", [2, D, FFN]), ("w_ffn_down", [2, FFN, D]),
                   ("norm_f", [D])]:
        I[nm] = din(nm, sh)
    hc = host_consts()
    for k, v in hc.items():
        I[k] = din(k, v.shape)
    O = {}
    O["y_prompt"] = dout("y_prompt", [2048, D])
    O["y_sample"] = dout("y_sample", [16, D])
    O["p_dn_conv"] = dout("p_dn_conv", [2, 3, 3072])
    O["p_dn_ssm"] = dout("p_dn_ssm", [2, 8, 128, 128])
    O["p_s5_re"] = dout("p_s5_re", [2, 64, 64])
    O["p_s5_im"] = dout("p_s5_im", [2, 64, 64])
    O["s_dn_conv"] = dout("s_dn_conv", [2, 16, 3, 3072])
    O["s_dn_ssm"] = dout("s_dn_ssm", [2, 16, 8, 128, 128])
    O["s_s5_re"] = dout("s_s5_re", [2, 16, 64, 64])
    O["s_s5_im"] = dout("s_s5_im", [2, 16, 64, 64])
    DBG = {}
    for nm, sh in dbg:
        DBG[nm] = dout("dbg_" + nm, sh)
    tabd = nc.dram_tensor("tabd", [2, 32, 128, 2, 512], F32, kind="Internal").ap()
    blkd = nc.dram_tensor("blkd", [2, 8, 128, 16, 128], BF16, kind="Internal").ap()

    k = Ctx()
    k.nc, k.P, k.I, k.O, k.DBG = nc, P, I, O, DBG
    k.WC_NA, k.WC_NB = 220, 176
    k.wcacheA = nc.dram_tensor("wcacheA", [k.WC_NA, 128, 2048], BF16, kind="Internal").ap()
    k.wcacheB = nc.dram_tensor("wcacheB", [k.WC_NB, 128, 4096], BF16, kind="Internal").ap()
    k.wc_na, k.wc_nb = 0, 0
    k.wc_idx, k.wc_res, k.wc_last = {}, {}, {}
    k.wc_bufs = [Buf(None, "wcs%d" % i) for i in range(4)]
    k.wc_n = 0

    def wc_sem():
        k.wc_n += 1
        return k.wc_bufs[k.wc_n % 4]
    k.wc_sem = wc_sem
    k.tabd, k.blkd = tabd, blkd
    k.blkd_res = Buf(None, 'blkd_res')
    k.tabd_res = Buf(None, 'tabd_res')
    alloc_buffers(k)
    setup_consts(k)
    for l in range(n_layers):
        setup_layer(k, l)
    for l in range(n_layers):
        setup_s5(k, l)
    for ti in range(n_ptiles):
        run_prompt_tile(k, ti, n_ptiles, n_layers)
    if do_sample:
        run_sample(k, n_layers)
    P.emit()
    k.sb_bytes = P.sb_bytes
    P.stack.close()
    return nc, k


def alloc_buffers(k):
    P = k.P
    k.xres = P.sbuf("xres", [128, 4, D], F32)
    k.hT = P.sbuf("hT", [128, 16, 512], BF16)
    k.NSLOT = 4
    k.wring = [P.sbuf("wr%d" % i, [128, 4096], BF16) for i in range(k.NSLOT)]
    k.wnext = 0
    k.big16 = P.sbuf("big16", [128, 16, 512], BF16)
    k.oT = P.sbuf("oT", [128, 8, 512], BF16)
    k.g5T = P.sbuf("g5T", [128, 8, 512], BF16)
    k.NPOOL = 24
    pool_t = P.stack.enter_context(k.nc.sbuf_tensor("pool", [128, k.NPOOL, 512], F32))
    P.sb_bytes += k.NPOOL * 2048
    k.pool = []
    for i in range(k.NPOOL):
        b = Buf(pool_t[:, i, :], "pool%d" % i)
        k.pool.append(b)
    k.pool_t = pool_t
    k.pre = P.sbuf("pre", [128, 3, 516], F32)
    k.onb = P.sbuf("onb", [64, 8, 128], BF16)
    k.oraw = P.sbuf("oraw", [64, 8, 128], F32)
    k.blk = [P.sbuf("blk%d" % i, [128, 16, 128], BF16) for i in range(2)]
    k.tab = [P.sbuf("tab0", [128, 2, 512], F32)] * 2
    k.small = P.sbuf("small", [128, 64], F32)
    k.ps = [P.psum("ps%d" % i, [128, 512], F32) for i in range(8)]
    for b in k.ps:
        b.excl = True
    k.ident = P.sbuf("ident", [128, 128], F32)
    k.identb = P.sbuf("identb", [128, 128], BF16)
    k.ones = P.sbuf("ones", [128, 128], F32)
    k.mask = P.sbuf("mask", [64, 3, 64], F32)
    k.maskE = P.sbuf("maskE", [128, 4, 8], F32)
    k.n1w = P.sbuf("n1w", [128, 2, 16], F32)
    k.n2w = P.sbuf("n2w", [128, 2, 16], F32)
    k.convw = P.sbuf("convw", [128, 2, 24, 4], F32)
    k.alog = P.sbuf("alog", [8, 2, 2], F32)
    k.dnw = P.sbuf("dnw", [128, 2], F32)
    k.s5d = P.sbuf("s5d", [128, 2, 8], F32)
    k.convhist = P.sbuf("convhist", [128, 2, 24, 3], F32)
    k.S = P.sbuf("S", [128, 2, 8, 128], F32)
    k.Sb = P.sbuf("Sb", [128, 2, 8, 128], BF16)
    k.s5x = P.sbuf("s5x", [128, 2, 2, 32], F32)
    k.s5m = P.sbuf("s5m", [128, 2, 3, 32], F32)
    k.gcol = P.sbuf("gcol", [64, 4, 8, 8], F32)
    k.xn = Buf(k.pre.t[:].rearrange("p a b -> p (a b)").bitcast(BF16)[:, 0:D], "xn")
    k.xn.res = k.pre.res
    k.epsc = P.sbuf("epsc", [128, 1], F32)
    k.onec = P.sbuf("onec", [128, 1], F32)
    k.vnewb = [P.sbuf("vnew%d" % i, [64, 128], BF16) for i in range(2)]
    k.c48 = P.sbuf("c48", [48, 3, 128], F32)
    k.cacc = P.sbuf("cacc", [128, 3, 16], F32)
    k.nr = P.sbuf("nr", [128, 24, 16], F32)
    k.SS = Buf(k.S.t[:, 0, :, :], "SS")
    k.SS.res = k.S.res
    k.SSb = Buf(k.Sb.t[:, 0, :, :], "SSb")
    k.SSb.res = k.Sb.res
    k.sg = P.sbuf("sg", [8, 2, 16], F32)
    k.zs16 = P.sbuf("zs16", [128, 16], BF16)
    k.outs = [Buf(None, "out%d" % i) for i in range(8)]
    k.outn = 0

    def onext():
        k.outn += 1
        return k.outs[k.outn % 8]
    k.onext = onext


def setup_consts(k):
    P, I = k.P, k.I
    P.op("sp", lambda e: e.dma_start(out=k.ident.t[:], in_=I["c_ident"]), writes=[k.ident], dma=True)
    P.op("sp", lambda e: e.dma_start(out=k.mask.t[:], in_=I["c_mask"]), writes=[k.mask], dma=True)
    P.op("sp", lambda e: e.dma_start(out=k.maskE.t[:], in_=I["c_maskE"]), writes=[k.maskE], dma=True)
    P.op("dve", lambda e: e.tensor_copy(out=k.identb.t[:], in_=k.ident.t[:]), reads=[k.ident], writes=[k.identb])
    P.op("dve", lambda e: e.memset(k.ones.t[:], 1.0), writes=[k.ones])
    P.op("dve", lambda e: e.memset(k.convhist.t[:], 0.0), writes=[k.convhist])
    P.op("dve", lambda e: e.memset(k.S.t[:], 0.0), writes=[k.S])
    P.op("dve", lambda e: e.memset(k.s5x.t[:], 0.0), writes=[k.s5x])
    P.op("dve", lambda e: e.memset(k.Sb.t[:], 0.0), writes=[k.Sb])
    P.op("dve", lambda e: e.memset(k.epsc.t[:], 1e-6), writes=[k.epsc])
    P.op("dve", lambda e: e.memset(k.onec.t[:], 1.0), writes=[k.onec])


def setup_layer(k, l):
    P, I = k.P, k.I
    P.op("sp", lambda e: e.dma_start(out=k.n1w.t[:, l, :], in_=I["norm1"][l].rearrange("(c p) -> p c", p=128),
                                     allow_slow_non_contiguous=True), writes=[k.n1w], dma=True)
    P.op("sp", lambda e: e.dma_start(out=k.n2w.t[:, l, :], in_=I["norm2"][l].rearrange("(c p) -> p c", p=128),
                                     allow_slow_non_contiguous=True), writes=[k.n2w], dma=True)
    for i in range(4):
        P.op("sp", lambda e, i=i: e.dma_start(out=k.convw.t[:, l, :, i], in_=I["dn_conv_w"][l, i].rearrange("(c p) -> p c", p=128),
                                              allow_slow_non_contiguous=True), writes=[k.convw], dma=True)
    P.op("sp", lambda e: e.dma_start(out=k.alog.t[:, l, 0:1], in_=I["dn_a_log"][l].rearrange("(h o) -> h o", o=1)),
         writes=[k.alog], dma=True)
    P.op("sp", lambda e: e.dma_start(out=k.alog.t[:, l, 1:2], in_=I["dn_dt_bias"][l].rearrange("(h o) -> h o", o=1)),
         writes=[k.alog], dma=True)
    P.op("act", lambda e: e.activation(out=k.alog.t[:, l, 0:1], in_=k.alog.t[:, l, 0:1], func=AF.Exp),
         reads=[k.alog], writes=[k.alog])
    P.op("dve", lambda e: e.tensor_scalar_mul(out=k.alog.t[:, l, 0:1], in0=k.alog.t[:, l, 0:1], scalar1=-1.0),
         reads=[k.alog], writes=[k.alog])
    P.op("sp", lambda e: e.dma_start(out=k.dnw.t[:, l:l + 1], in_=I["dn_norm_w"][l].rearrange("(p o) -> p o", o=1)),
         writes=[k.dnw], dma=True)
    P.op("sp", lambda e: e.dma_start(out=k.s5d.t[:, l, :], in_=I["s5_d"][l].rearrange("(c p) -> p c", p=128),
                                     allow_slow_non_contiguous=True), writes=[k.s5d], dma=True)


def bcast_mid(ap2d, n):
    p, a = ap2d.shape
    return ap2d.unsqueeze(2).to_broadcast([p, a, n])


def wload(k, src2d, r0, nk, c0, ncols):
    slot = k.wring[k.wnext]
    k.wnext = (k.wnext + 1) % k.NSLOT
    n = nk * ncols
    assert n <= 4096
    flat = slot.t[:, 0:n]
    view = flat.rearrange("p (k n) -> p k n", k=nk)
    key = (src2d.tensor.name, int(src2d.offset), r0, nk, c0, ncols)
    idx = k.wc_idx.get(key)
    if idx is None:
        if n <= 2048:
            idx = ("A", k.wc_na)
            k.wc_na += 1
            assert k.wc_na <= k.WC_NA
        else:
            idx = ("B", k.wc_nb)
            k.wc_nb += 1
            assert k.wc_nb <= k.WC_NB
        k.wc_idx[key] = idx
        src = src2d[r0:r0 + nk * 128, c0:c0 + ncols].rearrange("(k p) n -> p k n", p=128)
        k.P.op("pool", lambda e: e.dma_start(out=view, in_=src), writes=[slot], dma=True)
        k.P.op("sp", lambda e: e.dma_start(out=(k.wcacheA if idx[0] == "A" else k.wcacheB)[idx[1], :, 0:n], in_=flat), reads=[slot], writes=[k.wc_sem()], dma=True)
        k.wc_last[idx] = k.P.ops[-1]
    else:
        st = k.wc_last[idx]
        op = k.P.op("sp", lambda e: e.dma_start(out=flat, in_=(k.wcacheA if idx[0] == "A" else k.wcacheB)[idx[1], :, 0:n]), writes=[slot], dma=True)
        if st is not None and st not in op.deps:
            op.deps.append(st)
    return slot, view


def mm(k, psb, out_ap, pairs, reads):
    def fn(e):
        n = len(pairs)
        ins = None
        for i, (l_, r_) in enumerate(pairs):
            ins = e.matmul(out_ap, l_, r_, start=(i == 0), stop=(i == n - 1))
        return ins
    k.P.op("pe", fn, reads=reads, writes=[psb])


def tr(k, psb, out_ap, in_ap, ident_ap, reads):
    k.P.op("pe", lambda e: e.transpose(out_ap, in_ap, ident_ap), reads=reads, writes=[psb])


def psbf(psb):
    return psb.t[:].bitcast(BF16)


def tokv(k, q):
    return k.pool_t[0:64, 8 + q, :].bitcast(BF16).rearrange("p (c d) -> p c d", d=128)


def bfview(buf):
    return buf.t.bitcast(BF16)


def setup_s5(k, l):
    P, I, pl = k.P, k.I, k.pool

    def v(i):
        return pl[i].t[:, 0:32]
    V, A_, Gp = "dve", "act", "pool"
    P.op("sp", lambda e: e.dma_start(out=v(0), in_=I["s5_lam_re"][l].rearrange("g p -> (g p)").rearrange("(m q) -> q m", q=128),
                                     allow_slow_non_contiguous=True), writes=[pl[0]], dma=True)
    P.op("sp", lambda e: e.dma_start(out=v(1), in_=I["s5_lam_im"][l].rearrange("g p -> (g p)").rearrange("(m q) -> q m", q=128),
                                     allow_slow_non_contiguous=True), writes=[pl[1]], dma=True)
    ldt = I["s5_log_dt"]
    for gl2 in range(2):
        src = bass.AP(ldt.tensor, l * 64 + gl2, [[0, 64], [2, 32]])
        P.op("sp", lambda e, src=src, gl2=gl2: e.dma_start(out=pl[2].t[gl2 * 64:(gl2 + 1) * 64, 0:32], in_=src,
                                                          allow_slow_non_contiguous=True), writes=[pl[2]], dma=True)
    P.op(A_, lambda e: e.activation(out=v(2), in_=v(2), func=AF.Exp), reads=[pl[2]], writes=[pl[2]])
    P.op(V, lambda e: e.tensor_mul(out=v(3), in0=v(0), in1=v(2)), reads=[pl[0], pl[2]], writes=[pl[3]])
    P.op(V, lambda e: e.tensor_mul(out=v(4), in0=v(1), in1=v(2)), reads=[pl[1], pl[2]], writes=[pl[4]])
    P.op(A_, lambda e: e.activation(out=v(5), in_=v(3), func=AF.Exp), reads=[pl[3]], writes=[pl[5]])
    P.op(A_, lambda e: e.activation(out=v(8), in_=v(4), func=AF.Sin, scale=0.125), reads=[pl[4]], writes=[pl[8]])
    P.op(A_, lambda e: e.activation(out=v(6), in_=v(4), func=AF.Sin, scale=0.0625), reads=[pl[4]], writes=[pl[6]])
    P.op(V, lambda e: e.tensor_mul(out=v(6), in0=v(6), in1=v(6)), reads=[pl[6]], writes=[pl[6]])
    P.op(V, lambda e: e.tensor_scalar(out=v(7), in0=v(6), scalar1=-2.0, scalar2=1.0, op0=ALU.mult, op1=ALU.add), reads=[pl[6]], writes=[pl[7]])
    for _ in range(3):
        P.op(V, lambda e: e.tensor_mul(out=v(6), in0=v(7), in1=v(8)), reads=[pl[7], pl[8]], writes=[pl[6]])
        P.op(V, lambda e: e.tensor_mul(out=v(7), in0=v(7), in1=v(7)), reads=[pl[7]], writes=[pl[7]])
        P.op(V, lambda e: e.tensor_mul(out=v(8), in0=v(8), in1=v(8)), reads=[pl[8]], writes=[pl[8]])
        P.op(V, lambda e: e.tensor_sub(out=v(7), in0=v(7), in1=v(8)), reads=[pl[7], pl[8]], writes=[pl[7]])
        P.op(V, lambda e: e.tensor_scalar_mul(out=v(8), in0=v(6), scalar1=2.0), reads=[pl[6]], writes=[pl[8]])
    P.op(V, lambda e: e.tensor_mul(out=v(9), in0=v(5), in1=v(7)), reads=[pl[5], pl[7]], writes=[pl[9]])
    P.op(V, lambda e: e.tensor_mul(out=v(10), in0=v(5), in1=v(8)), reads=[pl[5], pl[8]], writes=[pl[10]])
    P.op(V, lambda e: e.tensor_scalar_add(out=v(11), in0=v(9), scalar1=-1.0), reads=[pl[9]], writes=[pl[11]])
    P.op(V, lambda e: e.tensor_mul(out=v(12), in0=v(0), in1=v(0)), reads=[pl[0]], writes=[pl[12]])
    P.op(V, lambda e: e.tensor_mul(out=v(6), in0=v(1), in1=v(1)), reads=[pl[1]], writes=[pl[6]])
    P.op(V, lambda e: e.tensor_add(out=v(12), in0=v(12), in1=v(6)), reads=[pl[12], pl[6]], writes=[pl[12]])
    P.op(V, lambda e: e.reciprocal(out=v(12), in_=v(12)), reads=[pl[12]], writes=[pl[12]])
    P.op(V, lambda e: e.tensor_mul(out=v(13), in0=v(11), in1=v(0)), reads=[pl[11], pl[0]], writes=[pl[13]])
    P.op(V, lambda e: e.tensor_mul(out=v(6), in0=v(10), in1=v(1)), reads=[pl[10], pl[1]], writes=[pl[6]])
    P.op(V, lambda e: e.tensor_add(out=v(13), in0=v(13), in1=v(6)), reads=[pl[13], pl[6]], writes=[pl[13]])
    P.op(V, lambda e: e.tensor_mul(out=v(13), in0=v(13), in1=v(12)), reads=[pl[13], pl[12]], writes=[pl[13]])
    P.op(V, lambda e: e.tensor_mul(out=v(14), in0=v(10), in1=v(0)), reads=[pl[10], pl[0]], writes=[pl[14]])
    P.op(V, lambda e: e.tensor_mul(out=v(6), in0=v(11), in1=v(1)), reads=[pl[11], pl[1]], writes=[pl[6]])
    P.op(V, lambda e: e.tensor_sub(out=v(14), in0=v(14), in1=v(6)), reads=[pl[14], pl[6]], writes=[pl[14]])
    P.op(V, lambda e: e.tensor_mul(out=v(14), in0=v(14), in1=v(12)), reads=[pl[14], pl[12]], writes=[pl[14]])
    for qi, src in enumerate((5, 9, 10)):
        P.op(V, lambda e, qi=qi, src=src: e.tensor_copy(out=k.s5m.t[:, l, qi, :], in_=v(src)), reads=[pl[src]], writes=[k.s5m])

    def v3(i):
        return pl[i].t[:, :].rearrange("p (m c) -> p m c", c=16)
    for idx, nm in ((16, "s5_b_re"), (17, "s5_b_im")):
        src = bass.AP(I[nm].tensor, l * 65536, [[16, 128], [2048, 32], [1, 16]])
        P.op("sp", lambda e, idx=idx, src=src: e.dma_start(out=v3(idx), in_=src), writes=[pl[idx]], dma=True)
    frb = bcast_mid(v(13), 16)
    fib = bcast_mid(v(14), 16)
    P.op(V, lambda e: e.tensor_mul(out=v3(18), in0=v3(16), in1=frb), reads=[pl[16], pl[13]], writes=[pl[18]])
    P.op(V, lambda e: e.tensor_mul(out=v3(19), in0=v3(17), in1=fib), reads=[pl[17], pl[14]], writes=[pl[3]])
    P.op(V, lambda e: e.tensor_sub(out=v3(18), in0=v3(18), in1=v3(19)), reads=[pl[18], pl[3]], writes=[pl[18]])
    P.op(V, lambda e: e.tensor_mul(out=v3(19), in0=v3(17), in1=frb), reads=[pl[17], pl[13]], writes=[pl[3]])
    P.op(V, lambda e: e.tensor_mul(out=v3(20), in0=v3(16), in1=fib), reads=[pl[16], pl[14]], writes=[pl[3]])
    P.op(V, lambda e: e.tensor_add(out=v3(19), in0=v3(19), in1=v3(20)), reads=[pl[3], pl[3]], writes=[pl[3]])
    cn = k.pool_t[:, 0:2, :].rearrange("p a (j q) -> p (a j) q", q=128)
    for idx, nm in ((21, "s5_c_re"), (22, "s5_c_im")):
        srcn = I[nm][l].rearrange("g c p -> (g c) p").rearrange("(j q) p -> q j p", q=128)
        for dup in range(2):
            P.op("sp", lambda e, srcn=srcn, dup=dup: e.dma_start(out=cn[:, :, dup * 64:(dup + 1) * 64], in_=srcn),
                 writes=[pl[0], pl[1]], dma=True)
        for j in range(8):
            psb = k.ps[j % 4]
            tr(k, psb, psb.t[:, 0:128], cn[:, j, :], k.ident.t[:], [pl[0], pl[1], k.ident])
            for gl2 in range(2):
                srcv = psb.t[gl2 * 64:(gl2 + 1) * 64, 0:128].rearrange("p (mm g c) -> p mm g c", g=2, c=16)[:, :, gl2, :]
                dstv = pl[idx].t[gl2 * 64:(gl2 + 1) * 64, :].rearrange("p (m c) -> p m c", c=16)[:, 4 * j:4 * j + 4, :]
                P.op(A_, lambda e, srcv=srcv, dstv=dstv: e.activation(out=dstv, in_=srcv, func=AF.Copy), reads=[psb], writes=[pl[idx]])
    for j in range(8):
        bb = k.blk[j % 2]
        for mm_ in range(4):
            mt = 4 * j + mm_
            mk = k.maskE.t[:, mm_, :].unsqueeze(2).to_broadcast([128, 8, 16])
            for kind, srcslot in ((0, 18), (1, 19)):
                srcv = v3(srcslot)[:, mt, :].unsqueeze(1).to_broadcast([128, 8, 16])
                E = pl[23]
                Ev = E.t[:, 0:128].rearrange("p (g c) -> p g c", c=16)
                P.op(V, lambda e, Ev=Ev, srcv=srcv, mk=mk: e.tensor_tensor(out=Ev, in0=srcv, in1=mk, op=ALU.mult),
                     reads=[pl[srcslot], k.maskE], writes=[E])
                psb = k.ps[(mm_ * 2 + kind) % 8]
                tr(k, psb, psb.t[:, 0:128], E.t[:, 0:128], k.ident.t[:], [E, k.ident])
                P.op(A_, lambda e, psb=psb, b=mm_ * 4 + kind, bb=bb: e.activation(out=bb.t[:, b, :], in_=psb.t[:, 0:128], func=AF.Copy),
                     reads=[psb], writes=[bb])
            for kind, srcslot, sc in ((2, 21, 1.0), (3, 22, -1.0)):
                srcv = v3(srcslot)[:, mt, :].unsqueeze(1).to_broadcast([128, 8, 16])
                outv = bb.t[:, mm_ * 4 + kind, :].rearrange("p (g c) -> p g c", c=16)
                P.op(V, lambda e, outv=outv, srcv=srcv, mk=mk, sc=sc: e.scalar_tensor_tensor(
                    out=outv, in0=srcv, scalar=sc, in1=mk, op0=ALU.mult, op1=ALU.mult),
                    reads=[pl[srcslot], k.maskE], writes=[bb])
        P.op("sp", lambda e, bb=bb, j=j: e.dma_start(out=k.blkd[l, j], in_=bb.t[:]), reads=[bb], writes=[k.blkd_res], dma=True)
    Tc = k.pool_t[:, 16:20, :]
    Ts = k.pool_t[:, 20:24, :]
    t1 = k.pool_t[:, 9:11, :].rearrange("p a (g n) -> p (a g) n", g=2)
    t2 = k.pool_t[:, 11:13, :].rearrange("p a (g n) -> p (a g) n", g=2)
    rc, rs_, rt1, rt2 = [pl[i] for i in range(16, 20)], [pl[i] for i in range(20, 24)], [pl[9], pl[10]], [pl[11], pl[12]]
    for g4 in range(8):
        m0 = 4 * g4
        P.op(V, lambda e, m0=m0: e.tensor_copy(out=Tc[:, :, 0:1], in_=pl[7].t[:, m0:m0 + 4].unsqueeze(2)), reads=[pl[7]], writes=rc)
        P.op(V, lambda e, m0=m0: e.tensor_copy(out=Ts[:, :, 0:1], in_=pl[8].t[:, m0:m0 + 4].unsqueeze(2)), reads=[pl[8]], writes=rs_)
        n = 1
        while n < 512:
            cb = Tc[:, :, n - 1:n].to_broadcast([128, 4, n])
            sb = Ts[:, :, n - 1:n].to_broadcast([128, 4, n])
            P.op(V, lambda e, n=n, sb=sb: e.tensor_tensor(out=t1[:, :, 0:n], in0=Ts[:, :, 0:n], in1=sb, op=ALU.mult), reads=rs_, writes=rt1)
            P.op(V, lambda e, n=n, sb=sb: e.tensor_tensor(out=t2[:, :, 0:n], in0=Tc[:, :, 0:n], in1=sb, op=ALU.mult), reads=rc + rs_, writes=rt2)
            P.op(V, lambda e, n=n, cb=cb: e.tensor_tensor(out=Tc[:, :, n:2 * n], in0=Tc[:, :, 0:n], in1=cb, op=ALU.mult), reads=rc, writes=rc)
            P.op(V, lambda e, n=n, cb=cb: e.tensor_tensor(out=Ts[:, :, n:2 * n], in0=Ts[:, :, 0:n], in1=cb, op=ALU.mult), reads=rs_ + rc, writes=rs_)
            P.op(V, lambda e, n=n: e.tensor_sub(out=Tc[:, :, n:2 * n], in0=Tc[:, :, n:2 * n], in1=t1[:, :, 0:n]), reads=rc + rt1, writes=rc)
            P.op(V, lambda e, n=n: e.tensor_add(out=Ts[:, :, n:2 * n], in0=Ts[:, :, n:2 * n], in1=t2[:, :, 0:n]), reads=rs_ + rt2, writes=rs_)
            n *= 2
        for i in range(4):
            P.op("sp", lambda e, i=i, m0=m0: e.dma_start(out=k.tabd[l, m0 + i, :, 0, :], in_=Tc[:, i, :]), reads=rc, writes=[k.tabd_res], dma=True)
            P.op("sp", lambda e, i=i, m0=m0: e.dma_start(out=k.tabd[l, m0 + i, :, 1, :], in_=Ts[:, i, :]), reads=rs_, writes=[k.tabd_res], dma=True)

class Grp:
    def __init__(self, T, np_, nsub, sample):
        self.T, self.np, self.nsub, self.sample = T, np_, nsub, sample


def norm_stage(k, G, wbuf, l):
    P = k.P
    np_ = G.np
    for s in range(G.nsub):
        xs = k.xres.t[0:np_, s, :]
        c0 = k.small.t[0:np_, 2 * s:2 * s + 1]
        c1 = k.small.t[0:np_, 2 * s + 1:2 * s + 2]
        xn = k.xn.t[0:np_, :]
        P.op("dve", lambda e, c0=c0: e.memset(c0, 0.0), writes=[k.small])
        P.op("act", lambda e, xn=xn, xs=xs, c0=c0: e.activation(out=xn, in_=xs, func=AF.Square, accum_out=c0),
             reads=[k.xres, k.small], writes=[k.xn, k.small])
        P.op("dve", lambda e, c0=c0, c1=c1: e.tensor_scalar(out=c1, in0=c0, scalar1=1.0 / D, scalar2=1e-6, op0=ALU.mult, op1=ALU.add),
             reads=[k.small], writes=[k.small])
        P.op("act", lambda e, c1=c1: e.activation(out=c1, in_=c1, func=AF.Sqrt), reads=[k.small], writes=[k.small])
        P.op("dve", lambda e, c1=c1: e.reciprocal(out=c1, in_=c1), reads=[k.small], writes=[k.small])
        P.op("act", lambda e, xn=xn, xs=xs, c1=c1: e.activation(out=xn, in_=xs, func=AF.Copy, scale=c1),
             reads=[k.xres, k.small], writes=[k.xn])
        for half in range(2):
            psb = k.ps[half]
            pv = psbf(psb)

            def fn(e, half=half, pv=pv):
                ins = None
                for c in range(8):
                    ins = e.transpose(pv[:, c * np_:(c + 1) * np_], k.xn.t[0:np_, (half * 8 + c) * 128:(half * 8 + c + 1) * 128],
                                      k.identb.t[0:np_, 0:np_])
                return ins
            P.op("pe", fn, reads=[k.xn, k.identb], writes=[psb])
            outv = k.hT.t[:, half * 8:(half + 1) * 8, s * np_:(s + 1) * np_]
            inv = pv[:, 0:8 * np_].rearrange("p (c t) -> p c t", t=np_)
            wv = bcast_mid(wbuf.t[:, l, half * 8:(half + 1) * 8], np_)
            P.op("dve", lambda e, outv=outv, inv=inv, wv=wv: e.tensor_tensor(out=outv, in0=inv, in1=wv, op=ALU.mult),
                 reads=[psb, wbuf], writes=[k.hT])


def proj_F(k, l, wsrc, c0, ncols, psb, T, rhs=None, nk=16, rhs_buf=None, m0=0, mcols=None):
    slot, wv = wload(k, wsrc, 0, nk, c0, ncols)
    rb = rhs_buf or k.hT
    rt = rhs if rhs is not None else k.hT.t
    mcols = mcols or ncols
    pairs = [(wv[:, kc, m0:m0 + mcols], rt[:, kc, 0:T]) for kc in range(nk)]
    mm(k, psb, psb.t[0:mcols, 0:T], pairs, [slot, rb])


def grow(k, r):
    return k.pool[19 + r].t[0:8, :]


def gbuf(k, r):
    return k.pool[19 + r]


def gates_stage(k, G, l):
    P = k.P
    T = G.T
    slot, wv = wload(k, k.I["w_in"][l], 0, 16, 4096, 16)
    mm(k, k.ps[2], k.ps[2].t[0:8, 0:T], [(wv[:, kc, 0:8], k.hT.t[:, kc, 0:T]) for kc in range(16)], [slot, k.hT])
    mm(k, k.ps[3], k.ps[3].t[0:8, 0:T], [(wv[:, kc, 8:16], k.hT.t[:, kc, 0:T]) for kc in range(16)], [slot, k.hT])
    P.op("act", lambda e: e.activation(out=grow(k, 0)[:, 0:T], in_=k.ps[2].t[0:8, 0:T], func=AF.Sigmoid), reads=[k.ps[2]], writes=[gbuf(k, 0)])
    P.op("act", lambda e: e.activation(out=grow(k, 1)[:, 0:T], in_=k.ps[3].t[0:8, 0:T], func=AF.Exp, bias=k.alog.t[:, l, 1:2]),
         reads=[k.ps[3], k.alog], writes=[gbuf(k, 1)])
    P.op("act", lambda e: e.activation(out=grow(k, 1)[:, 0:T], in_=grow(k, 1)[:, 0:T], func=AF.Ln, bias=k.onec.t[0:8, 0:1]), reads=[gbuf(k, 1), k.onec], writes=[gbuf(k, 1)])
    P.op("dve", lambda e: e.tensor_scalar_mul(out=grow(k, 1)[:, 0:T], in0=grow(k, 1)[:, 0:T], scalar1=k.alog.t[:, l, 0:1]),
         reads=[gbuf(k, 1), k.alog], writes=[gbuf(k, 1)])


def gates_derive(k, nch):
    P = k.P
    T = nch * 64
    for c in range(nch):
        P.op("dve", lambda e, c=c: e.tensor_tensor_scan(out=grow(k, 2)[:, c * 64:(c + 1) * 64], data0=k.ones.t[0:8, 0:64],
                                                         data1=grow(k, 1)[:, c * 64:(c + 1) * 64], initial=0.0, op0=ALU.mult, op1=ALU.add),
             reads=[gbuf(k, 1), k.ones], writes=[gbuf(k, 2)])
    P.op("act", lambda e: e.activation(out=grow(k, 3)[:, 0:T], in_=grow(k, 2)[:, 0:T], func=AF.Exp), reads=[gbuf(k, 2)], writes=[gbuf(k, 3)])
    P.op("dve", lambda e: e.tensor_mul(out=grow(k, 3)[:, 0:T], in0=grow(k, 0)[:, 0:T], in1=grow(k, 3)[:, 0:T]), reads=[gbuf(k, 0), gbuf(k, 3)], writes=[gbuf(k, 3)])
    g3 = grow(k, 2)[:, 0:T].rearrange("p (c t) -> p c t", t=64)
    P.op("dve", lambda e: e.tensor_tensor(out=grow(k, 4)[:, 0:T].rearrange("p (c t) -> p c t", t=64), in0=g3,
                                          in1=g3[:, :, 63:64].to_broadcast([8, nch, 64]), op=ALU.subtract),
         reads=[gbuf(k, 2)], writes=[gbuf(k, 4)])
    P.op("act", lambda e: e.activation(out=grow(k, 4)[:, 0:T], in_=grow(k, 4)[:, 0:T], func=AF.Exp, scale=-1.0), reads=[gbuf(k, 4)], writes=[gbuf(k, 4)])
    psb = k.ps[4]

    def fn(e):
        ins = None
        for qi, row in enumerate((2, 0, 3, 4)):
            for c in range(nch):
                ins = e.transpose(psb.t[0:64, (qi * 8 + c) * 8:(qi * 8 + c + 1) * 8], grow(k, row)[:, c * 64:(c + 1) * 64], k.ident.t[0:8, 0:8])
        return ins
    P.op("pe", fn, reads=[gbuf(k, 0), gbuf(k, 2), gbuf(k, 3), gbuf(k, 4), k.ident], writes=[psb])
    P.op("dve", lambda e: e.tensor_copy(out=k.gcol.t[:].rearrange("p q c h -> p (q c h)"), in_=psb.t[0:64, 0:256]),
         reads=[psb], writes=[k.gcol])


def dn_prep_head(k, l, h, nch, ident_T=False):
    P, pl = k.P, k.pool
    T = nch * 64
    V, A_ = "dve", "act"
    qT, kT = pl[0].t[:, 0:T], pl[1].t[:, 0:T]
    vTb = bfview(pl[2])[:, 0:T]
    knT, qnT = bfview(pl[4])[:, 0:T], bfview(pl[4])[:, 512:512 + T]
    qdT, nWT = bfview(pl[5])[:, 0:T], bfview(pl[5])[:, 512:512 + T]
    for src, srcb, psb, dst, scale in ((qT, pl[0], k.ps[0], qnT, 128 ** -0.5), (kT, pl[1], k.ps[1], knT, 1.0)):
        sq = pl[3].t[:, 0:T]
        P.op(A_, lambda e, sq=sq, src=src: e.activation(out=sq, in_=src, func=AF.Square), reads=[srcb], writes=[pl[3]])
        mm(k, psb, psb.t[:, 0:T], [(k.ones.t[:], sq)], [k.ones, pl[3]])
        P.op(A_, lambda e, sq=sq, psb=psb: e.activation(out=sq, in_=psb.t[:, 0:T], func=AF.Ln, bias=k.epsc.t[:, 0:1]),
             reads=[psb, k.epsc], writes=[pl[3]])
        P.op(A_, lambda e, sq=sq: e.activation(out=sq, in_=sq, func=AF.Exp, scale=-0.5), reads=[pl[3]], writes=[pl[3]])
        P.op(V, lambda e, sq=sq, src=src, dst=dst, scale=scale: e.scalar_tensor_tensor(
            out=dst, in0=src, scalar=scale, in1=sq, op0=ALU.mult, op1=ALU.mult), reads=[srcb, pl[3]], writes=[pl[4]])
    import os
    KP = int(os.environ.get("KP", "9"))
    if KP <= 1:
        return
    gsel = pl[8].t[0:8, 0:T]
    P.op(V, lambda e: e.tensor_scalar_mul(out=gsel, in0=grow(k, 2)[:, 0:T], scalar1=k.ident.t[0:8, h:h + 1]),
         reads=[gbuf(k, 2), k.ident], writes=[pl[8]])
    mm(k, k.ps[2], k.ps[2].t[:, 0:T], [(k.ones.t[0:8, :], gsel)], [k.ones, pl[8]])
    P.op(V, lambda e: e.tensor_copy(out=pl[6].t[:, 0:T], in_=k.ps[2].t[:, 0:T]), reads=[k.ps[2]], writes=[pl[6]])
    P.op(A_, lambda e: e.activation(out=pl[7].t[:, 0:T], in_=k.ps[2].t[:, 0:T], func=AF.Exp), reads=[k.ps[2]], writes=[pl[7]])
    P.op(V, lambda e: e.tensor_mul(out=qdT, in0=qnT, in1=pl[7].t[:, 0:T]), reads=[pl[4], pl[7]], writes=[pl[5]])
    if KP <= 2:
        return
    mm_chunks = []

    def fnG(e):
        if ident_T:
            return e.matmul(k.ps[3].t[0:64, 0:64], knT[:, 0:64], knT[:, 0:64], start=True, stop=True)
        ins = None
        for c in range(nch):
            ins = e.matmul(k.ps[3].t[0:64, c * 64:(c + 1) * 64], knT[:, c * 64:(c + 1) * 64], knT[:, c * 64:(c + 1) * 64], start=True, stop=True)
        return ins
    P.op("pe", fnG, reads=[pl[4]], writes=[k.ps[3]])

    def fnQ(e):
        ins = None
        for c in range(nch):
            ins = e.matmul(k.ps[4].t[0:64, c * 64:(c + 1) * 64], knT[:, c * 64:(c + 1) * 64], qnT[:, c * 64:(c + 1) * 64], start=True, stop=True)
        return ins
    P.op("pe", fnQ, reads=[pl[4]], writes=[k.ps[4]])
    v3 = lambda ap: ap.rearrange("p (c t) -> p c t", t=64)
    negD, mA, mB = pl[8].t[0:64, 0:T], pl[9].t[0:64, 0:T], pl[10].t[0:64, 0:T]
    P.op(V, lambda e: e.tensor_tensor(out=v3(negD), in0=v3(pl[6].t[0:64, 0:T]), in1=bcast_mid(k.gcol.t[:, 0, 0:nch, h], 64), op=ALU.subtract),
         reads=[pl[6], k.gcol], writes=[pl[8]])
    P.op(V, lambda e: e.tensor_tensor(out=v3(mA), in0=v3(negD), in1=k.mask.t[:, 0:1, :].to_broadcast([64, nch, 64]), op=ALU.add),
         reads=[pl[8], k.mask], writes=[pl[9]])
    P.op(A_, lambda e: e.activation(out=mA, in_=mA, func=AF.Exp, scale=-1.0), reads=[pl[9]], writes=[pl[9]])
    P.op(V, lambda e: e.tensor_tensor(out=v3(mB), in0=v3(negD), in1=k.mask.t[:, 1:2, :].to_broadcast([64, nch, 64]), op=ALU.add),
         reads=[pl[8], k.mask], writes=[pl[10]])
    P.op(A_, lambda e: e.activation(out=mB, in_=mB, func=AF.Exp), reads=[pl[10]], writes=[pl[10]])
    if not ident_T:
        A0 = pl[11].t[0:64, 0:T]
        P.op(V, lambda e: e.tensor_tensor(out=v3(A0), in0=v3(k.ps[3].t[0:64, 0:T]), in1=bcast_mid(k.gcol.t[:, 1, 0:nch, h], 64), op=ALU.mult),
             reads=[k.ps[3], k.gcol], writes=[pl[11]])
        P.op(V, lambda e: e.tensor_mul(out=A0, in0=A0, in1=mA), reads=[pl[11], pl[9]], writes=[pl[11]])
        P.op(V, lambda e: e.tensor_tensor(out=v3(A0), in0=v3(A0), in1=k.mask.t[:, 2:3, :].to_broadcast([64, nch, 64]), op=ALU.mult),
             reads=[pl[11], k.mask], writes=[pl[11]])
    QKTm = bfview(pl[18])[0:64, 512:512 + T]
    P.op(V, lambda e: e.tensor_mul(out=QKTm, in0=k.ps[4].t[0:64, 0:T], in1=mB), reads=[k.ps[4], pl[10]], writes=[pl[18]])
    if KP <= 3:
        return
    for src, srcb, psb, outs in ((knT, pl[4], k.ps[5], ((0, 2), (1, 3))), (vTb, pl[2], k.ps[6], ((2, 1),))):
        pv = psbf(psb)

        def fn(e, src=src, pv=pv):
            ins = None
            for c in range(nch):
                ins = e.transpose(pv[0:64, c * 128:(c + 1) * 128], src[:, c * 64:(c + 1) * 64], k.identb.t[:])
            return ins
        P.op("pe", fn, reads=[srcb, k.identb], writes=[psb])
        for oi, qi in outs:
            P.op(V, lambda e, oi=oi, qi=qi, pv=pv: e.tensor_tensor(
                out=tokv(k, oi)[:, 0:nch, :], in0=pv[0:64, 0:nch * 128].rearrange("p (c d) -> p c d", d=128),
                in1=bcast_mid(k.gcol.t[:, qi, 0:nch, h], 128), op=ALU.mult), reads=[psb, k.gcol], writes=[pl[8 + oi]])
    if not ident_T:
        AT0 = pl[12].t[0:64, 0:T]

        def fnT(e):
            ins = None
            for c in range(nch):
                ins = e.transpose(k.ps[0].t[0:64, c * 64:(c + 1) * 64], A0[:, c * 64:(c + 1) * 64], k.ident.t[0:64, 0:64])
            return ins
        P.op("pe", fnT, reads=[pl[11], k.ident], writes=[k.ps[0]])
        P.op(A_, lambda e: e.activation(out=AT0, in_=k.ps[0].t[0:64, 0:T], func=AF.Copy), reads=[k.ps[0]], writes=[pl[12]])
        PT = [pl[16], pl[17]]
        I3 = k.ident.t[0:64, 0:64].unsqueeze(1).to_broadcast([64, nch, 64])
        P.op(V, lambda e: e.scalar_tensor_tensor(out=v3(PT[0].t[0:64, 0:T]), in0=v3(AT0), scalar=-1.0, in1=I3, op0=ALU.mult, op1=ALU.add),
             reads=[pl[12], k.ident], writes=[PT[0]])
        X = [pl[11], pl[13]]
        XT = [pl[12], pl[14]]
        IX = pl[15]
        cur = 0
        for lev in range(5):
            nxt = 1 - cur
            lowp = lev >= 1
            sel_ = (lambda b_: bfview(b_)[0:64, 0:T]) if lowp else (lambda b_: b_.t[0:64, 0:T])
            nsel = lambda b_: bfview(b_)[0:64, 0:T]
            Xc, XTc = sel_(X[cur]), sel_(XT[cur])

            def fnX(e, Xc=Xc, XTc=XTc):
                ins = None
                for c in range(nch):
                    sl = slice(c * 64, (c + 1) * 64)
                    ins = e.matmul(k.ps[0].t[0:64, sl], XTc[:, sl], Xc[:, sl], start=True, stop=True)
                return ins
            P.op("pe", fnX, reads=[X[cur], XT[cur]], writes=[k.ps[0]])
            if lev < 4:
                def fnXT(e, Xc=Xc, XTc=XTc):
                    ins = None
                    for c in range(nch):
                        sl = slice(c * 64, (c + 1) * 64)
                        ins = e.matmul(k.ps[1].t[0:64, sl], Xc[:, sl], XTc[:, sl], start=True, stop=True)
                    return ins
                P.op("pe", fnXT, reads=[X[cur], XT[cur]], writes=[k.ps[1]])
                P.op(A_, lambda e, nxt=nxt: e.activation(out=nsel(X[nxt]), in_=k.ps[0].t[0:64, 0:T], func=AF.Copy),
                     reads=[k.ps[0]], writes=[X[nxt]])
                P.op(A_, lambda e, nxt=nxt: e.activation(out=nsel(XT[nxt]), in_=k.ps[1].t[0:64, 0:T], func=AF.Copy),
                     reads=[k.ps[1]], writes=[XT[nxt]])
            IXv = sel_(IX)
            P.op(V, lambda e, IXv=IXv: e.tensor_tensor(out=v3(IXv), in0=v3(k.ps[0].t[0:64, 0:T]), in1=I3, op=ALU.add),
                 reads=[k.ps[0], k.ident], writes=[IX])
            pin, pout = PT[lev % 2], PT[(lev + 1) % 2]
            pinv = sel_(pin)

            def fnP(e, pinv=pinv, IXv=IXv):
                ins = None
                for c in range(nch):
                    sl = slice(c * 64, (c + 1) * 64)
                    ins = e.matmul(k.ps[2].t[0:64, sl], IXv[:, sl], pinv[:, sl], start=True, stop=True)
                return ins
            P.op("pe", fnP, reads=[IX, pin], writes=[k.ps[2]])
            if lev < 4:
                P.op(V, lambda e, pout=pout: e.tensor_copy(out=nsel(pout), in_=k.ps[2].t[0:64, 0:T]), reads=[k.ps[2]], writes=[pout])
            cur = nxt
    TTb = bfview(pl[18])[0:64, 0:T]
    if ident_T:
        P.op(V, lambda e: e.tensor_copy(out=v3(TTb), in_=k.identb.t[0:64, 0:64].unsqueeze(1).to_broadcast([64, nch, 64])),
             reads=[k.identb], writes=[pl[18]])
    else:
        P.op(V, lambda e: e.tensor_copy(out=TTb, in_=k.ps[2].t[0:64, 0:T]), reads=[k.ps[2]], writes=[pl[18]])

    def fnW(e):
        ins = None
        for c in range(nch):
            ins = e.matmul(k.ps[3].t[:, c * 64:(c + 1) * 64], tokv(k, 0)[:, c, :], TTb[:, c * 64:(c + 1) * 64], start=True, stop=True)
        return ins
    P.op("pe", fnW, reads=[pl[8], pl[18]], writes=[k.ps[3]])
    P.op(A_, lambda e: e.activation(out=nWT, in_=k.ps[3].t[:, 0:T], func=AF.Copy, scale=-1.0), reads=[k.ps[3]], writes=[pl[5]])


def dn_chunk(k, c, S_ap, Sb_ap, Sres, Sbres, h, egl_ap, T_cols, fill=None):
    P, pl = k.P, k.pool
    sl = slice(c * 64, (c + 1) * 64)
    TTb = bfview(pl[18])[0:64, sl]
    QKTm = bfview(pl[18])[0:64, 512 + c * 64:512 + (c + 1) * 64]
    qdT = bfview(pl[5])[:, sl]
    nWT = bfview(pl[5])[:, 512 + c * 64:512 + (c + 1) * 64]
    psv = k.ps[4]
    mm(k, psv, psv.t[0:64, 0:128], [(TTb, tokv(k, 2)[:, c, :]), (nWT, Sb_ap)], [pl[18], pl[10], pl[5], Sbres])
    if fill:
        fill()
    vb_ = k.vnewb[c % 2]
    vnew = vb_.t[:]
    P.op("act", lambda e: e.activation(out=vnew, in_=psv.t[0:64, 0:128], func=AF.Copy), reads=[psv], writes=[vb_])
    pso = k.ps[5 + (c % 8) // 4]
    oreg = pso.t[0:64, (c % 4) * 128:(c % 4 + 1) * 128]
    mm(k, pso, oreg, [(qdT, Sb_ap), (QKTm, vnew)], [pl[5], Sbres, pl[18], vb_])
    pss = k.ps[7]
    mm(k, pss, pss.t[:, 0:128], [(tokv(k, 1)[:, c, :], vnew)], [pl[9], vb_])
    if fill:
        fill()
    P.op("dve", lambda e: e.scalar_tensor_tensor(out=Sb_ap, in0=S_ap, scalar=egl_ap, in1=pss.t[:, 0:128], op0=ALU.mult, op1=ALU.add),
         reads=[Sres, pss, pl[7]], writes=[Sbres])
    P.op("dve", lambda e: e.scalar_tensor_tensor(out=S_ap, in0=S_ap, scalar=egl_ap, in1=pss.t[:, 0:128], op0=ALU.mult, op1=ALU.add),
         reads=[Sres, pss, pl[7]], writes=[Sres])


def dn_finish_head(k, l, h, nch, zs_ap, zs_res, out_ap, out_buf, stride_tok=None):
    P, pl = k.P, k.pool
    for b in range((nch + 3) // 4):
        n = min(4, nch - 4 * b)
        P.op("dve", lambda e, b=b, n=n: e.tensor_copy(out=k.oraw.t[:, 4 * b:4 * b + n, :].rearrange("p c d -> p (c d)"),
                                                       in_=k.ps[5 + b].t[0:64, 0:n * 128]), reads=[k.ps[5 + b]], writes=[k.oraw])
    sq = pl[3].t[0:64, :]
    T2 = nch * 128
    sqv = (sq if nch <= 4 else None)
    ssq = k.small.t[0:64, 16:16 + nch]
    for b in range((nch + 3) // 4):
        n = min(4, nch - 4 * b)
        P.op("act", lambda e, b=b, n=n: e.activation(out=sq[:, 0:n * 128], in_=k.oraw.t[:, 4 * b:4 * b + n, :].rearrange("p c d -> p (c d)"),
                                                      func=AF.Square), reads=[k.oraw], writes=[pl[3]])
        P.op("dve", lambda e, b=b, n=n: e.tensor_reduce(out=k.small.t[0:64, 16 + 4 * b:16 + 4 * b + n],
                                                         in_=sq[:, 0:n * 128].rearrange("p (c d) -> p c d", d=128),
                                                         axis=AX.X, op=ALU.add), reads=[pl[3]], writes=[k.small])
    P.op("dve", lambda e: e.tensor_scalar(out=ssq, in0=ssq, scalar1=1.0 / 128, scalar2=1e-6, op0=ALU.mult, op1=ALU.add),
         reads=[k.small], writes=[k.small])
    P.op("act", lambda e: e.activation(out=ssq, in_=ssq, func=AF.Ln), reads=[k.small], writes=[k.small])
    P.op("act", lambda e: e.activation(out=ssq, in_=ssq, func=AF.Exp, scale=-0.5), reads=[k.small], writes=[k.small])
    P.op("dve", lambda e: e.tensor_tensor(out=k.onb.t[:, 0:nch, :], in0=k.oraw.t[:, 0:nch, :], in1=bcast_mid(ssq, 128), op=ALU.mult),
         reads=[k.oraw, k.small], writes=[k.onb])
    psb = k.ps[7]
    pv = psbf(psb)

    def fn(e):
        ins = None
        for c in range(nch):
            ins = e.transpose(pv[:, c * 64:(c + 1) * 64], k.onb.t[:, c, :], k.identb.t[0:64, 0:64])
        return ins
    P.op("pe", fn, reads=[k.onb, k.identb], writes=[psb])
    P.op("dve", lambda e: e.scalar_tensor_tensor(out=out_ap, in0=pv[:, 0:nch * 64], scalar=k.dnw.t[:, l:l + 1], in1=zs_ap,
                                                 op0=ALU.mult, op1=ALU.mult), reads=[psb, k.dnw, zs_res], writes=[out_buf])


def conv_silu(k, l, qi, h, psb, T, out_ap, out_buf, acc_buf):
    P = k.P
    ch = qi * 8 + h
    pre = k.pre.t
    P.op("dve", lambda e: e.tensor_copy(out=pre[:, qi, 0:3], in_=k.convhist.t[:, l, ch, :]), reads=[k.convhist], writes=[k.pre])
    P.op("act", lambda e: e.activation(out=pre[:, qi, 3:3 + T], in_=psb.t[:, 0:T], func=AF.Copy), reads=[psb], writes=[k.pre])
    acc = acc_buf.t[:, 0:T]
    w = k.convw.t
    P.op("dve", lambda e: e.tensor_scalar_mul(out=acc, in0=pre[:, qi, 0:T], scalar1=w[:, l, ch, 0:1]), reads=[k.pre, k.convw], writes=[acc_buf])
    for i in range(1, 4):
        P.op("dve", lambda e, i=i: e.scalar_tensor_tensor(out=acc, in0=pre[:, qi, i:i + T], scalar=w[:, l, ch, i:i + 1], in1=acc,
                                                           op0=ALU.mult, op1=ALU.add), reads=[k.pre, k.convw, acc_buf], writes=[acc_buf])
    P.op("act", lambda e: e.activation(out=out_ap, in_=acc, func=AF.Silu), reads=[acc_buf], writes=[out_buf])
    P.op("dve", lambda e: e.tensor_copy(out=k.convhist.t[:, l, ch, :], in_=pre[:, qi, T:T + 3]), reads=[k.pre], writes=[k.convhist])


def dn_head_prompt(k, l, h):
    P, pl = k.P, k.pool
    T = 512
    W = k.I["w_in"][l]
    if h == 0:
        for qi in range(4):
            proj_F(k, l, W, qi * 1024 + h * 128, 128, k.ps[qi], T)
    conv_silu(k, l, 0, h, k.ps[0], T, pl[0].t[:, 0:T], pl[0], pl[0])
    conv_silu(k, l, 1, h, k.ps[1], T, pl[1].t[:, 0:T], pl[1], pl[1])
    conv_silu(k, l, 2, h, k.ps[2], T, bfview(pl[2])[:, 0:T], pl[2], pl[3])
    zs = bfview(pl[2])[:, 512:512 + T]
    P.op("act", lambda e: e.activation(out=zs, in_=k.ps[3].t[:, 0:T], func=AF.Silu), reads=[k.ps[3]], writes=[pl[2]])
    dn_prep_head(k, l, h, 8)
    st = {"i": 0, "wv": None, "slot": None}

    def fill():
        i = st["i"]
        if h + 1 >= 8 or i >= 16:
            return
        qi, pc = i // 4, i % 4
        if pc == 0:
            st["slot"], st["wv"] = wload(k, W, 0, 16, qi * 1024 + (h + 1) * 128, 128)
        wv, slot, psb = st["wv"], st["slot"], k.ps[qi]

        def fn(e):
            ins = None
            for kc in range(4 * pc, 4 * pc + 4):
                ins = e.matmul(psb.t[:, 0:T], wv[:, kc, :], k.hT.t[:, kc, 0:T], start=(kc == 0), stop=(kc == 15))
            return ins
        P.op("pe", fn, reads=[slot, k.hT], writes=[psb])
        st["i"] = i + 1
    for c in range(8):
        dn_chunk(k, c, k.S.t[:, l, h, :], k.Sb.t[:, l, h, :], k.S, k.Sb, h, pl[7].t[:, c * 64 + 63:c * 64 + 64], None, fill=fill)
    dn_finish_head(k, l, h, 8, zs, pl[2], k.oT.t[:, h, 0:T], k.oT)


def gelu_glu_chunk(k, l, j, T, y_ps, uT, ubuf):
    P, pl = k.P, k.pool
    y = pl[12].t[:, 0:T]
    P.op("dve", lambda e: e.scalar_tensor_tensor(out=y, in0=uT, scalar=k.s5d.t[:, l, j:j + 1], in1=y_ps.t[:, 0:T], op0=ALU.mult, op1=ALU.add),
         reads=[ubuf, k.s5d, y_ps], writes=[pl[12]])
    t = k.pre.t[:, 0, 0:T]
    P.op("act", lambda e: e.activation(out=t, in_=y, func=AF.Square), reads=[pl[12]], writes=[k.pre])
    P.op("pool", lambda e: e.tensor_scalar(out=t, in0=t, scalar1=0.044715, scalar2=1.0, op0=ALU.mult, op1=ALU.add), reads=[k.pre], writes=[k.pre])
    P.op("pool", lambda e: e.tensor_mul(out=t, in0=t, in1=y), reads=[k.pre, pl[12]], writes=[k.pre])
    P.op("act", lambda e: e.activation(out=t, in_=t, func=AF.Sigmoid, scale=1.5957691216057308), reads=[k.pre], writes=[k.pre])
    P.op("pool", lambda e: e.tensor_mul(out=k.big16.t[:, j, 0:T], in0=t, in1=y), reads=[k.pre, pl[12]], writes=[k.big16])


def s5_prompt(k, l):
    P, pl = k.P, k.pool
    T = 512
    V = "dve"
    tabs = [k.tab[0], Buf(k.pool_t[:, 13:15, :], "tab1")]
    tabs[1].res = pl[13].res
    tab1_extra = pl[14]
    tsl = [[pl[i] for i in range(2, 10)], [pl[i] for i in range(15, 23)]]
    xbs = [pl[10], pl[23]]
    bus = [(k.ps[1], k.ps[2]), (k.ps[4], k.ps[5])]
    uTs = [pl[0], pl[1]]
    ctx = {}

    def prologue(j):
        bb = k.blk[j % 2]
        P.op("sp", lambda e: e.dma_start(out=bb.t[:], in_=k.blkd[l, j]), reads=[k.blkd_res], writes=[bb], dma=True)
        proj_F(k, l, k.I["w_in"][l], 4112 + j * 128, 128, k.ps[0], T)
        ub = uTs[j % 2]
        uT = ub.t[:, 0:T]
        uTb = bfview(pl[11])[:, (j % 2) * 512:(j % 2) * 512 + T]
        P.op("act", lambda e: e.activation(out=uT, in_=k.ps[0].t[:, 0:T], func=AF.Copy), reads=[k.ps[0]], writes=[ub])
        P.op("dve", lambda e: e.tensor_copy(out=uTb, in_=k.ps[0].t[:, 0:T]), reads=[k.ps[0]], writes=[pl[11]])
        ctx[j] = (bb, uT, ub, uTb)

    def stageA(mt):
        j, mm_, p = mt // 4, mt % 4, mt % 2
        bb, uT, ub, uTb = ctx[j]
        tb = tabs[p]
        tres = [tb] if p == 0 else [tb, tab1_extra]
        tv = tb.t if p == 0 else tb.t
        P.op("sp", lambda e: e.dma_start(out=tv[:] if p == 0 else tv, in_=k.tabd[l, mt]), reads=[k.tabd_res], writes=tres, dma=True)
        p1, p2 = bus[p]
        mm(k, p1, p1.t[:, 0:T], [(bb.t[:, mm_ * 4 + 0, :], uTb)], [bb, pl[11]])
        mm(k, p2, p2.t[:, 0:T], [(bb.t[:, mm_ * 4 + 1, :], uTb)], [bb, pl[11]])
        cT, sT = tv[:, 0, :], tv[:, 1, :]
        t = [x.t[:, 0:T] for x in tsl[p]]
        tr_ = tsl[p]
        b1, b2 = p1.t[:, 0:T], p2.t[:, 0:T]
        P.op(V, lambda e: e.tensor_mul(out=t[0], in0=cT, in1=b1), reads=tres + [p1], writes=[tr_[0]])
        P.op(V, lambda e: e.tensor_mul(out=t[1], in0=sT, in1=b2), reads=tres + [p2], writes=[tr_[1]])
        P.op(V, lambda e: e.tensor_mul(out=t[2], in0=cT, in1=b2), reads=tres + [p2], writes=[tr_[2]])
        P.op(V, lambda e: e.tensor_mul(out=t[3], in0=sT, in1=b1), reads=tres + [p1], writes=[tr_[3]])

    def stageB(mt):
        p = mt % 2
        t = [x.t[:, 0:T] for x in tsl[p]]
        tr_ = tsl[p]
        P.op("pool", lambda e: e.tensor_add(out=t[0], in0=t[0], in1=t[1]), reads=[tr_[0], tr_[1]], writes=[tr_[0]])
        P.op("pool", lambda e: e.tensor_sub(out=t[2], in0=t[2], in1=t[3]), reads=[tr_[2], tr_[3]], writes=[tr_[2]])
        magb = k.s5m.t[:, l, 0, mt:mt + 1].to_broadcast([128, T])
        P.op(V, lambda e: e.tensor_tensor_scan(out=t[4], data0=magb, data1=t[0], initial=k.s5x.t[:, l, 0, mt:mt + 1],
                                               op0=ALU.mult, op1=ALU.add), reads=[k.s5m, tr_[0], k.s5x], writes=[tr_[4]])
        P.op(V, lambda e: e.tensor_tensor_scan(out=t[5], data0=magb, data1=t[2], initial=k.s5x.t[:, l, 1, mt:mt + 1],
                                               op0=ALU.mult, op1=ALU.add), reads=[k.s5m, tr_[2], k.s5x], writes=[tr_[5]])

    def stageC(mt):
        j, mm_, p = mt // 4, mt % 4, mt % 2
        bb, uT, ub, uTb = ctx[j]
        tb = tabs[p]
        tres = [tb] if p == 0 else [tb, tab1_extra]
        cT, sT = tb.t[:, 0, :], tb.t[:, 1, :]
        t = [x.t[:, 0:T] for x in tsl[p]]
        tr_ = tsl[p]
        xb = xbs[p]
        xre = bfview(xb)[:, 0:T]
        xim = bfview(xb)[:, 512:512 + T]
        P.op("pool", lambda e: e.tensor_mul(out=t[6], in0=cT, in1=t[4]), reads=tres + [tr_[4]], writes=[tr_[6]])
        P.op(V, lambda e: e.tensor_mul(out=t[7], in0=sT, in1=t[5]), reads=tres + [tr_[5]], writes=[tr_[7]])
        P.op(V, lambda e: e.tensor_sub(out=xre, in0=t[6], in1=t[7]), reads=[tr_[6], tr_[7]], writes=[xb])
        sx = k.small.t[:, 32 + 2 * p:32 + 2 * p + 1]
        P.op(V, lambda e: e.tensor_sub(out=sx, in0=t[6][:, T - 1:T], in1=t[7][:, T - 1:T]), reads=[tr_[6], tr_[7]], writes=[k.small])
        P.op("pool", lambda e: e.tensor_mul(out=t[6], in0=cT, in1=t[5]), reads=tres + [tr_[5]], writes=[tr_[6]])
        P.op(V, lambda e: e.tensor_mul(out=t[7], in0=sT, in1=t[4]), reads=tres + [tr_[4]], writes=[tr_[7]])
        P.op(V, lambda e: e.tensor_add(out=xim, in0=t[6], in1=t[7]), reads=[tr_[6], tr_[7]], writes=[xb])
        P.op(V, lambda e: e.tensor_add(out=k.s5x.t[:, l, 1, mt:mt + 1], in0=t[6][:, T - 1:T], in1=t[7][:, T - 1:T]),
             reads=[tr_[6], tr_[7]], writes=[k.s5x])
        P.op(V, lambda e: e.tensor_copy(out=k.s5x.t[:, l, 0, mt:mt + 1], in_=sx), reads=[k.small], writes=[k.s5x])

        def fy(e):
            e.matmul(k.ps[3].t[:, 0:T], bb.t[:, mm_ * 4 + 2, :], xre, start=(mm_ == 0), stop=False)
            return e.matmul(k.ps[3].t[:, 0:T], bb.t[:, mm_ * 4 + 3, :], xim, start=False, stop=(mm_ == 3))
        P.op("pe", fy, reads=[bb, xb], writes=[k.ps[3]])
        if mm_ == 3:
            gelu_glu_chunk(k, l, j, T, k.ps[3], uT, ub)

    for step in range(32 + 2):
        if 0 <= step - 2 < 32:
            stageC(step - 2)
        if step < 32:
            if step % 4 == 0:
                prologue(step // 4)
            stageA(step)
        if 0 <= step - 1 < 32:
            stageB(step - 1)


def glu_stage(k, l, T):
    P, pl = k.P, k.pool
    for jo in range(8):
        psb = k.ps[4 + jo % 2]
        proj_F(k, l, k.I["w_glu"][l], jo * 128, 128, psb, T, rhs=k.big16.t, nk=8, rhs_buf=k.big16)
        sg = pl[13 + jo % 2]
        P.op("act", lambda e, sg=sg, psb=psb: e.activation(out=sg.t[:, 0:T], in_=psb.t[:, 0:T], func=AF.Sigmoid), reads=[psb], writes=[sg])
        P.op("dve", lambda e, sg=sg, jo=jo: e.tensor_mul(out=k.g5T.t[:, jo, 0:T], in0=sg.t[:, 0:T], in1=k.big16.t[:, jo, 0:T]),
             reads=[sg, k.big16], writes=[k.g5T])


def merge_stage(k, l, T):
    P, pl = k.P, k.pool
    for c2 in range(8):
        sdn, vdn = wload(k, k.I["w_br_dn"][l], 0, 8, c2 * 256, 256)
        ss5, vs5 = wload(k, k.I["w_br_s5"][l], 0, 8, c2 * 256, 256)
        sgd, vgd = wload(k, k.I["w_in"][l], 0, 16, 5136 + c2 * 256, 256)
        sgs, vgs = wload(k, k.I["w_in"][l], 0, 16, 7184 + c2 * 256, 256)
        for d in range(2):
            c = 2 * c2 + d
            cs = slice(d * 128, (d + 1) * 128)
            mm(k, k.ps[0], k.ps[0].t[:, 0:T], [(vdn[:, kc, cs], k.oT.t[:, kc, 0:T]) for kc in range(8)], [sdn, k.oT])
            mm(k, k.ps[1], k.ps[1].t[:, 0:T], [(vs5[:, kc, cs], k.g5T.t[:, kc, 0:T]) for kc in range(8)], [ss5, k.g5T])
            mm(k, k.ps[2], k.ps[2].t[:, 0:T], [(vgd[:, kc, cs], k.hT.t[:, kc, 0:T]) for kc in range(16)], [sgd, k.hT])
            mm(k, k.ps[3], k.ps[3].t[:, 0:T], [(vgs[:, kc, cs], k.hT.t[:, kc, 0:T]) for kc in range(16)], [sgs, k.hT])
            a, b = pl[0].t[:, 0:T], pl[1].t[:, 0:T]
            P.op("act", lambda e, a=a: e.activation(out=a, in_=k.ps[2].t[:, 0:T], func=AF.Sigmoid), reads=[k.ps[2]], writes=[pl[0]])
            P.op("act", lambda e, b=b: e.activation(out=b, in_=k.ps[3].t[:, 0:T], func=AF.Sigmoid), reads=[k.ps[3]], writes=[pl[1]])
            P.op("dve", lambda e, a=a: e.tensor_mul(out=a, in0=a, in1=k.ps[0].t[:, 0:T]), reads=[pl[0], k.ps[0]], writes=[pl[0]])
            P.op("dve", lambda e, b=b: e.tensor_mul(out=b, in0=b, in1=k.ps[1].t[:, 0:T]), reads=[pl[1], k.ps[1]], writes=[pl[1]])
            P.op("pool", lambda e, a=a, b=b, c=c: e.tensor_add(out=k.big16.t[:, c, 0:T], in0=a, in1=b), reads=[pl[0], pl[1]], writes=[k.big16])


def tokmajor_proj(k, G, wsrc, r0, nk_list, act_c0):
    P = k.P
    np_ = G.np
    for fb in range(4):
        slots = []
        rr = r0
        for nk in nk_list:
            slots.append(wload(k, wsrc, rr, nk, fb * 512, 512) + (nk,))
            rr += nk * 128
        for s in range(G.nsub):
            pairs = []
            ci = act_c0
            for slot, wv, nk in slots:
                for kc in range(nk):
                    pairs.append((k.big16.t[:, ci, s * np_:(s + 1) * np_], wv[:, kc, :]))
                    ci += 1
            psb = k.ps[s]
            mm(k, psb, psb.t[0:np_, :], pairs, [sl[0] for sl in slots] + [k.big16])
            xv = k.xres.t[0:np_, s, fb * 512:(fb + 1) * 512]
            P.op("dve", lambda e, xv=xv, psb=psb: e.tensor_tensor(out=xv, in0=xv, in1=psb.t[0:np_, :], op=ALU.add),
                 reads=[k.xres, psb], writes=[k.xres])


def ffn_stage(k, G, l):
    P, pl = k.P, k.pool
    T = G.T
    for qd in range(4):
        i = 0
        while i < 11:
            n2 = 2 if i + 1 < 11 else 1
            hc = 11 * qd + i
            sg_, vg = wload(k, k.I["w_ffn_gate"][l], 0, 16, hc * 128, 128 * n2)
            su_, vu = wload(k, k.I["w_ffn_up"][l], 0, 16, hc * 128, 128 * n2)
            for d in range(n2):
                ii = i + d
                pg, pu = k.ps[4 + (ii % 2) * 2], k.ps[5 + (ii % 2) * 2]
                mm(k, pg, pg.t[:, 0:T], [(vg[:, kc, d * 128:(d + 1) * 128], k.hT.t[:, kc, 0:T]) for kc in range(16)], [sg_, k.hT])
                mm(k, pu, pu.t[:, 0:T], [(vu[:, kc, d * 128:(d + 1) * 128], k.hT.t[:, kc, 0:T]) for kc in range(16)], [su_, k.hT])
                sg = pl[ii % 2]
                P.op("act", lambda e, sg=sg, pg=pg: e.activation(out=sg.t[:, 0:T], in_=pg.t[:, 0:T], func=AF.Silu), reads=[pg], writes=[sg])
                P.op("dve", lambda e, sg=sg, pu=pu, ii=ii: e.tensor_mul(out=k.big16.t[:, ii, 0:T], in0=sg.t[:, 0:T], in1=pu.t[:, 0:T]),
                     reads=[sg, pu], writes=[k.big16])
            i += n2
        tokmajor_proj(k, G, k.I["w_ffn_down"][l], 11 * qd * 128, [8, 3], 0)


def final_norm_store(k, G, out_ap_rows):
    P, pl = k.P, k.pool
    np_ = G.np
    for fb in range(4):
        P.op("sp", lambda e, fb=fb: e.dma_start(out=pl[fb].t[:], in_=k.I["norm_f"][fb * 512:(fb + 1) * 512].partition_broadcast(128)),
             writes=[pl[fb]], dma=True)
    for s in range(G.nsub):
        xs = k.xres.t[0:np_, s, :]
        c0 = k.small.t[0:np_, 2 * s:2 * s + 1]
        c1 = k.small.t[0:np_, 2 * s + 1:2 * s + 2]
        xn = k.xn.t[0:np_, :]
        P.op("dve", lambda e, c0=c0: e.memset(c0, 0.0), writes=[k.small])
        P.op("act", lambda e, xn=xn, xs=xs, c0=c0: e.activation(out=xn, in_=xs, func=AF.Square, accum_out=c0),
             reads=[k.xres, k.small], writes=[k.xn, k.small])
        P.op("dve", lambda e, c0=c0, c1=c1: e.tensor_scalar(out=c1, in0=c0, scalar1=1.0 / D, scalar2=1e-6, op0=ALU.mult, op1=ALU.add),
             reads=[k.small], writes=[k.small])
        P.op("act", lambda e, c1=c1: e.activation(out=c1, in_=c1, func=AF.Sqrt), reads=[k.small], writes=[k.small])
        P.op("dve", lambda e, c1=c1: e.reciprocal(out=c1, in_=c1), reads=[k.small], writes=[k.small])
        for fb in range(4):
            xv = k.xres.t[0:np_, s, fb * 512:(fb + 1) * 512]
            P.op("dve", lambda e, xv=xv, c1=c1, fb=fb: e.scalar_tensor_tensor(out=xv, in0=xv, scalar=c1, in1=pl[fb].t[0:np_, :],
                                                                             op0=ALU.mult, op1=ALU.mult), reads=[k.xres, k.small, pl[fb]], writes=[k.xres])
        dst = out_ap_rows(s)
        P.op("sp", lambda e, dst=dst, xs=xs: e.dma_start(out=dst, in_=xs), reads=[k.xres], writes=[k.onext()], dma=True)


def run_prompt_tile(k, ti, n_ptiles, n_layers):
    P = k.P
    G = Grp(512, 128, 4, False)
    t0 = ti * 512
    for s in range(4):
        P.op("sp", lambda e, s=s: e.dma_start(out=k.xres.t[:, s, :], in_=k.I["x_prompt"][t0 + s * 128:t0 + (s + 1) * 128, :]),
             writes=[k.xres], dma=True)
    import os
    STOP = int(os.environ.get("KSTOP", "99"))
    if STOP <= 0:
        return
    for l in range(n_layers):
        norm_stage(k, G, k.n1w, l)
        if STOP <= 1:
            return
        gates_stage(k, G, l)
        gates_derive(k, 8)
        if STOP <= 2:
            return
        for h in range(8):
            if os.environ.get("KSKIPDN"):
                break
            dn_head_prompt(k, l, h)
            if STOP <= 3:
                return
        if STOP <= 4:
            return
        if "oT" in k.DBG and ti == 0 and l == 0:
            P.op("pool", lambda e: e.dma_start(out=k.DBG["oT"], in_=k.oT.t[:]), reads=[k.oT], writes=[k.onext()], dma=True)
        s5_prompt(k, l)
        glu_stage(k, l, 512)
        if "g5T" in k.DBG and ti == 0 and l == 0:
            P.op("pool", lambda e: e.dma_start(out=k.DBG["g5T"], in_=k.g5T.t[:]), reads=[k.g5T], writes=[k.onext()], dma=True)
        merge_stage(k, l, 512)
        tokmajor_proj(k, G, k.I["w_out"][l], 0, [8, 8], 0)
        norm_stage(k, G, k.n2w, l)
        ffn_stage(k, G, l)
        if ti == n_ptiles - 1:
            store_prompt_states(k, l)
    final_norm_store(k, G, lambda s: k.O["y_prompt"][t0 + s * 128:t0 + (s + 1) * 128, :])


def store_prompt_states(k, l):
    P = k.P
    for r in range(3):
        P.op("sp", lambda e, r=r: e.dma_start(out=k.O["p_dn_conv"][l, r].rearrange("(c p) -> p c", p=128), in_=k.convhist.t[:, l, :, r],
                                              allow_slow_non_contiguous=True), reads=[k.convhist], writes=[k.onext()], dma=True)
    P.op("sp", lambda e: e.dma_start(out=k.O["p_dn_ssm"][l].rearrange("h a b -> a h b"), in_=k.S.t[:, l, :, :]),
         reads=[k.S], writes=[k.onext()], dma=True)
    for ri, nm in enumerate(("p_s5_re", "p_s5_im")):
        dst = bass.AP(k.O[nm].tensor, l * 4096, [[1, 128], [128, 32]])
        P.op("sp", lambda e, dst=dst, ri=ri: e.dma_start(out=dst, in_=k.s5x.t[:, l, ri, :], allow_slow_non_contiguous=True),
             reads=[k.s5x], writes=[k.onext()], dma=True)


def run_sample(k, n_layers):
    P, pl, I, O = k.P, k.pool, k.I, k.O
    G = Grp(16, 16, 1, True)
    T = 16
    c3 = lambda ap: ap.rearrange("p (c t) -> p c t", t=64)
    P.op("sp", lambda e: e.dma_start(out=k.xres.t[0:16, 0, :], in_=I["x_sample"]), writes=[k.xres], dma=True)
    for l_ in range(n_layers):
        sample_layer(k, G, l_)
    final_norm_store(k, G, lambda s: O["y_sample"][0:16, :])


def sample_layer(k, G, l):
    P, pl, I, O = k.P, k.pool, k.I, k.O
    T = 16
    c3 = lambda ap: ap.rearrange("p (c t) -> p c t", t=64)
    if True:
        W = I["w_in"][l]
        norm_stage(k, G, k.n1w, l)
        gates_stage(k, G, l)
        for r in range(2):
            P.op("dve", lambda e, r=r: e.tensor_copy(out=k.sg.t[:, r, :], in_=grow(k, r)[:, 0:16]), reads=[gbuf(k, r)], writes=[k.sg])
        P.op("sp", lambda e: e.dma_start(out=O["s_dn_conv"][l, :, 0:2, :], in_=I["state_dn_conv"][l, :, 1:3, :]),
             writes=[k.onext()], dma=True)
        for h_ in range(8):
            sample_head(k, G, l, h_)
        sample_rest(k, G, l)


def sample_head(k, G, l, h):
    P, pl, I, O = k.P, k.pool, k.I, k.O
    T = 16
    W = I["w_in"][l]
    c3 = lambda ap: ap.rearrange("p (c t) -> p c t", t=64)
    if True:
        if True:
            for qi in range(4):
                proj_F(k, l, W, qi * 1024 + h * 128, 128, k.ps[qi], T)
            for qi in range(3):
                ch = qi * 8 + h
                P.op("sp", lambda e, qi=qi, ch=ch: e.dma_start(
                    out=k.c48.t[:, qi, :], in_=I["state_dn_conv"][l, :, :, ch * 128:(ch + 1) * 128].rearrange("t r c -> (t r) c")),
                    writes=[k.c48], dma=True)
                psh = k.ps[4 + qi]
                tr(k, psh, psh.t[:, 0:48], k.c48.t[:, qi, :], k.ident.t[0:48, 0:48], [k.c48, k.ident])
                hv = psh.t[:, 0:48].rearrange("p (t r) -> p r t", r=3)
                acc = k.cacc.t[:, qi, :]
                new = k.ps[qi].t[:, 0:16]
                w = k.convw.t
                P.op("act", lambda e, ch=ch, new=new: e.activation(out=k.nr.t[:, ch, :], in_=new, func=AF.Copy), reads=[k.ps[qi]], writes=[k.nr])
                P.op("dve", lambda e, acc=acc, hv=hv, ch=ch: e.tensor_scalar_mul(out=acc, in0=hv[:, 0, :], scalar1=w[:, l, ch, 0:1]),
                     reads=[psh, k.convw], writes=[k.cacc])
                for i in (1, 2):
                    P.op("dve", lambda e, acc=acc, hv=hv, ch=ch, i=i: e.scalar_tensor_tensor(
                        out=acc, in0=hv[:, i, :], scalar=w[:, l, ch, i:i + 1], in1=acc, op0=ALU.mult, op1=ALU.add),
                        reads=[psh, k.convw, k.cacc], writes=[k.cacc])
                P.op("dve", lambda e, acc=acc, new=new, ch=ch: e.scalar_tensor_tensor(
                    out=acc, in0=new, scalar=w[:, l, ch, 3:4], in1=acc, op0=ALU.mult, op1=ALU.add),
                    reads=[k.ps[qi], k.convw, k.cacc], writes=[k.cacc])
            P.op("act", lambda e: e.activation(out=k.zs16.t[:], in_=k.ps[3].t[:, 0:16], func=AF.Silu), reads=[k.ps[3]], writes=[k.zs16])
            sample_dn_direct(k, l, h)


def sample_dn_direct(k, l, h):
    P, pl, I, O = k.P, k.pool, k.I, k.O
    V, A_ = "dve", "act"
    S16 = k.S.t[:].rearrange("p l h d -> p (l h) d")
    a, b, c, d = pl[0].t, pl[1].t, pl[2].t, pl[3].t
    q, sqq, qn = a[:, 0:16], a[:, 16:32], a[:, 32:48]
    kk_, sqk, kn = b[:, 0:16], b[:, 16:32], b[:, 32:48]
    v, bb_, egb, t1, vn, o, osq, prod = [c[:, i * 16:(i + 1) * 16] for i in range(8)]
    kq = d[:, 0:32].rearrange("p (t two) -> p t two", two=2)
    p4, p5, p6, p7 = k.ps[4], k.ps[5], k.ps[6], k.ps[7]
    P.op("sp", lambda e: e.dma_start(out=S16, in_=I["state_dn_ssm"][l, :, h].rearrange("t a b -> a t b")), writes=[k.S], dma=True)
    P.op(A_, lambda e: e.activation(out=q, in_=k.cacc.t[:, 0, :], func=AF.Silu), reads=[k.cacc], writes=[pl[0]])
    P.op(A_, lambda e: e.activation(out=kk_, in_=k.cacc.t[:, 1, :], func=AF.Silu), reads=[k.cacc], writes=[pl[1]])
    P.op(A_, lambda e: e.activation(out=v, in_=k.cacc.t[:, 2, :], func=AF.Silu), reads=[k.cacc], writes=[pl[2]])
    for x, sq, xn, buf, reg, scale, col in ((q, sqq, qn, pl[0], p5.t[:, 0:16], 128 ** -0.5, 1), (kk_, sqk, kn, pl[1], p5.t[:, 16:32], 1.0, 0)):
        P.op(A_, lambda e, x=x, sq=sq: e.activation(out=sq, in_=x, func=AF.Square), reads=[buf], writes=[buf])
        mm(k, p5, reg, [(k.ones.t[:], sq)], [k.ones, buf])
        P.op(A_, lambda e, sq=sq, reg=reg: e.activation(out=sq, in_=reg, func=AF.Sqrt, bias=k.epsc.t[:, 0:1]), reads=[p5, k.epsc], writes=[buf])
        P.op(V, lambda e, sq=sq: e.reciprocal(out=sq, in_=sq), reads=[buf], writes=[buf])
        P.op(V, lambda e, x=x, sq=sq, xn=xn, scale=scale: e.scalar_tensor_tensor(out=xn, in0=x, scalar=scale, in1=sq, op0=ALU.mult, op1=ALU.mult),
             reads=[buf], writes=[buf])
        P.op(V, lambda e, xn=xn, col=col: e.tensor_copy(out=kq[:, :, col], in_=xn), reads=[buf], writes=[pl[3]])
    gsel = d[0:8, 64:96]
    P.op(V, lambda e: e.tensor_scalar_mul(out=gsel, in0=k.sg.t[:].rearrange("p r t -> p (r t)"), scalar1=k.ident.t[0:8, h:h + 1]),
         reads=[k.sg, k.ident], writes=[pl[3]])
    mm(k, p6, p6.t[:, 0:32], [(k.ones.t[0:8, :], gsel)], [k.ones, pl[3]])
    P.op(V, lambda e: e.tensor_copy(out=bb_, in_=p6.t[:, 0:16]), reads=[p6], writes=[pl[2]])
    P.op(A_, lambda e: e.activation(out=egb, in_=p6.t[:, 16:32], func=AF.Exp), reads=[p6], writes=[pl[2]])

    def fkq(e):
        ins = None
        for tok in range(16):
            ins = e.matmul(p4.t[:, tok * 2:(tok + 1) * 2], S16[:, tok, :], kq[:, tok, :], start=True, stop=True)
        return ins
    P.op("pe", fkq, reads=[k.S, pl[3]], writes=[p4])
    kqv = p4.t[:, 0:32].rearrange("p (t two) -> p t two", two=2)
    kS, qS = kqv[:, :, 0], kqv[:, :, 1]
    P.op(V, lambda e: e.tensor_mul(out=t1, in0=egb, in1=kS), reads=[pl[2], p4], writes=[pl[2]])
    P.op(V, lambda e: e.tensor_sub(out=t1, in0=v, in1=t1), reads=[pl[2]], writes=[pl[2]])
    P.op(V, lambda e: e.tensor_mul(out=vn, in0=bb_, in1=t1), reads=[pl[2]], writes=[pl[2]])
    P.op(V, lambda e: e.tensor_mul(out=prod, in0=qn, in1=kn), reads=[pl[0], pl[1]], writes=[pl[2]])
    mm(k, p6, p6.t[:, 32:48], [(k.ones.t[:], prod)], [k.ones, pl[2]])
    P.op(V, lambda e: e.tensor_mul(out=o, in0=egb, in1=qS), reads=[pl[2], p4], writes=[pl[2]])
    P.op(V, lambda e: e.tensor_mul(out=t1, in0=vn, in1=p6.t[:, 32:48]), reads=[pl[2], p6], writes=[pl[2]])
    P.op(V, lambda e: e.tensor_add(out=o, in0=o, in1=t1), reads=[pl[2]], writes=[pl[2]])
    P.op(A_, lambda e: e.activation(out=osq, in_=o, func=AF.Square), reads=[pl[2]], writes=[pl[2]])
    mm(k, p6, p6.t[:, 48:64], [(k.ones.t[:], osq)], [k.ones, pl[2]])
    P.op(V, lambda e: e.tensor_scalar(out=osq, in0=p6.t[:, 48:64], scalar1=1.0 / 128, scalar2=1e-6, op0=ALU.mult, op1=ALU.add),
         reads=[p6], writes=[pl[2]])
    P.op(A_, lambda e: e.activation(out=osq, in_=osq, func=AF.Sqrt), reads=[pl[2]], writes=[pl[2]])
    P.op(V, lambda e: e.reciprocal(out=osq, in_=osq), reads=[pl[2]], writes=[pl[2]])
    P.op(V, lambda e: e.tensor_mul(out=o, in0=o, in1=osq), reads=[pl[2]], writes=[pl[2]])
    P.op(V, lambda e: e.scalar_tensor_tensor(out=k.oT.t[:, h, 0:16], in0=o, scalar=k.dnw.t[:, l:l + 1], in1=k.zs16.t[:], op0=ALU.mult, op1=ALU.mult),
         reads=[pl[2], k.dnw, k.zs16], writes=[k.oT])
    tr(k, p7, p7.t[0:16, 0:128], kn, k.ident.t[:], [pl[1], k.ident])
    tr(k, p7, p7.t[0:16, 128:256], vn, k.ident.t[:], [pl[2], k.ident])
    tok2 = pl[4].t[0:16, 0:256]
    P.op(V, lambda e: e.tensor_copy(out=tok2, in_=p7.t[0:16, 0:256]), reads=[p7], writes=[pl[4]])
    kexp = k.pool_t[0:16, 5:9, :].rearrange("p a b -> p (a b)").rearrange("p (t d) -> p t d", d=128)
    kex_res = [pl[5], pl[6], pl[7], pl[8]]
    P.op(V, lambda e: e.tensor_tensor(out=kexp, in0=tok2[:, 0:128].unsqueeze(1).to_broadcast([16, 16, 128]),
                                      in1=bcast_mid(k.ident.t[0:16, 0:16], 128), op=ALU.mult), reads=[pl[4], k.ident], writes=kex_res)
    for g4 in range(4):
        psb = k.ps[g4]

        def fo(e, g4=g4, psb=psb):
            ins = None
            for j in range(4):
                ins = e.matmul(psb.t[:, j * 128:(j + 1) * 128], kexp[:, 4 * g4 + j, :], tok2[:, 128:256], start=True, stop=True)
            return ins
        P.op("pe", fo, reads=kex_res + [pl[4]], writes=[psb])
        Sv = S16[:, 4 * g4:4 * g4 + 4, :]
        P.op(V, lambda e, g4=g4, Sv=Sv: e.tensor_tensor(out=Sv, in0=Sv, in1=bcast_mid(egb[:, 4 * g4:4 * g4 + 4], 128), op=ALU.mult),
             reads=[k.S, pl[2]], writes=[k.S])
        P.op(V, lambda e, Sv=Sv, psb=psb: e.tensor_tensor(out=Sv, in0=Sv, in1=psb.t[:, :].rearrange("p (t d) -> p t d", d=128), op=ALU.add),
             reads=[k.S, psb], writes=[k.S])
    P.op("sp", lambda e: e.dma_start(out=O["s_dn_ssm"][l, :, h].rearrange("t a b -> a t b"), in_=S16), reads=[k.S], writes=[k.onext()], dma=True)

def sample_rest(k, G, l):
    P, pl, I, O = k.P, k.pool, k.I, k.O
    T = 16
    W = I["w_in"][l]
    if True:
        stg_res = [pl[i] for i in range(8)]
        stg = k.pool_t[0:16, 0:8, :].rearrange("p a b -> p (a b)")
        for b_ in range(6):
            psb = k.ps[b_]

            def fnr(e, b_=b_, psb=psb):
                ins = None
                for c4 in range(4):
                    ch = 4 * b_ + c4
                    ins = e.transpose(psb.t[0:16, c4 * 128:(c4 + 1) * 128], k.nr.t[:, ch, :], k.ident.t[:])
                return ins
            P.op("pe", fnr, reads=[k.nr, k.ident], writes=[psb])
            P.op("act" if b_ % 2 else "dve", (lambda e, b_=b_, psb=psb: e.activation(out=stg[:, b_ * 512:(b_ + 1) * 512], in_=psb.t[0:16, :], func=AF.Copy))
                 if b_ % 2 else (lambda e, b_=b_, psb=psb: e.tensor_copy(out=stg[:, b_ * 512:(b_ + 1) * 512], in_=psb.t[0:16, :])),
                 reads=[psb], writes=[pl[b_]])
        P.op("sp", lambda e: e.dma_start(out=O["s_dn_conv"][l][:, 2, :], in_=stg[:, 0:3072]), reads=stg_res[0:6], writes=[k.onext()], dma=True)
        for ri, nm in enumerate(("state_s5_re", "state_s5_im")):
            P.op("sp", lambda e, nm=nm: e.dma_start(out=stg, in_=I[nm][l].rearrange("t g p -> t (g p)")), writes=stg_res, dma=True)
            psb = k.ps[6 + ri]

            def fnl(e, psb=psb):
                ins = None
                for m in range(32):
                    ins = e.transpose(psb.t[:, m * 16:(m + 1) * 16], stg[:, m * 128:(m + 1) * 128], k.ident.t[0:16, 0:16])
                return ins
            P.op("pe", fnl, reads=stg_res + [k.ident], writes=[psb])
            P.op("dve", lambda e, ri=ri, psb=psb: e.tensor_copy(out=pl[14 + ri].t[:, :], in_=psb.t[:, :]), reads=[psb], writes=[pl[14 + ri]])
        uT = pl[0].t[:, 0:128].rearrange("p (j t) -> p j t", t=16)
        uTb = bfview(pl[1])[:, 0:128].rearrange("p (j t) -> p j t", t=16)
        for j in range(8):
            proj_F(k, l, W, 4112 + j * 128, 128, k.ps[0], T)
            P.op("act", lambda e, j=j: e.activation(out=uT[:, j, :], in_=k.ps[0].t[:, 0:T], func=AF.Copy), reads=[k.ps[0]], writes=[pl[0]])
            P.op("dve", lambda e, j=j: e.tensor_copy(out=uTb[:, j, :], in_=k.ps[0].t[:, 0:T]), reads=[k.ps[0]], writes=[pl[1]])
        v3 = lambda b: b.t[:, :].rearrange("p (m t) -> p m t", t=16)
        for j in range(8):
            bb = k.blk[j % 2]
            P.op("sp", lambda e, bb=bb, j=j: e.dma_start(out=bb.t[:], in_=k.blkd[l, j]), reads=[k.blkd_res], writes=[bb], dma=True)
            for mm_ in range(4):
                mt = 4 * j + mm_
                mm(k, k.ps[1], k.ps[1].t[:, mt * 16:(mt + 1) * 16], [(bb.t[:, mm_ * 4 + 0, :], uTb[:, j, :])], [bb, pl[1]])
                mm(k, k.ps[2], k.ps[2].t[:, mt * 16:(mt + 1) * 16], [(bb.t[:, mm_ * 4 + 1, :], uTb[:, j, :])], [bb, pl[1]])
        arb = bcast_mid(k.s5m.t[:, l, 1, :], 16)
        aib = bcast_mid(k.s5m.t[:, l, 2, :], 16)
        x0r, x0i, t1, t2, x1r, x1i = v3(pl[14]), v3(pl[15]), v3(pl[2]), v3(pl[3]), v3(pl[16]), v3(pl[17])
        V = "dve"
        P.op(V, lambda e: e.tensor_tensor(out=t1, in0=x0i, in1=aib, op=ALU.mult), reads=[pl[15], k.s5m], writes=[pl[2]])
        P.op(V, lambda e: e.tensor_tensor(out=t2, in0=x0r, in1=arb, op=ALU.mult), reads=[pl[14], k.s5m], writes=[pl[3]])
        P.op(V, lambda e: e.tensor_sub(out=t2, in0=t2, in1=t1), reads=[pl[2], pl[3]], writes=[pl[3]])
        P.op(V, lambda e: e.tensor_tensor(out=x1r, in0=t2, in1=v3(k.ps[1]), op=ALU.add), reads=[pl[3], k.ps[1]], writes=[pl[16]])
        P.op(V, lambda e: e.tensor_tensor(out=t1, in0=x0r, in1=aib, op=ALU.mult), reads=[pl[14], k.s5m], writes=[pl[2]])
        P.op(V, lambda e: e.tensor_tensor(out=t2, in0=x0i, in1=arb, op=ALU.mult), reads=[pl[15], k.s5m], writes=[pl[3]])
        P.op(V, lambda e: e.tensor_add(out=t2, in0=t2, in1=t1), reads=[pl[2], pl[3]], writes=[pl[3]])
        P.op(V, lambda e: e.tensor_tensor(out=x1i, in0=t2, in1=v3(k.ps[2]), op=ALU.add), reads=[pl[3], k.ps[2]], writes=[pl[17]])
        xre = bfview(pl[10])[:, 0:512]
        xim = bfview(pl[10])[:, 512:1024]
        P.op(V, lambda e: e.tensor_copy(out=xre, in_=pl[16].t[:, :]), reads=[pl[16]], writes=[pl[10]])
        P.op(V, lambda e: e.tensor_copy(out=xim, in_=pl[17].t[:, :]), reads=[pl[17]], writes=[pl[10]])
        stg2_res = [pl[i] for i in range(2, 10)]
        stg2 = k.pool_t[0:16, 2:10, :].rearrange("p a b -> p (a b)")
        for ri, nm in enumerate(("s_s5_re", "s_s5_im")):
            for half in range(2):
                for b_ in range(4):
                    psb = k.ps[4 + b_]

                    def fns(e, ri=ri, half=half, b_=b_, psb=psb):
                        ins = None
                        for c4 in range(4):
                            m = half * 16 + b_ * 4 + c4
                            ins = e.transpose(psb.t[0:16, c4 * 128:(c4 + 1) * 128], pl[16 + ri].t[:, m * 16:(m + 1) * 16], k.ident.t[:])
                        return ins
                    P.op("pe", fns, reads=[pl[16 + ri], k.ident], writes=[psb])
                    col = (half * 4 + b_) * 512
                    if b_ % 2:
                        P.op("act", lambda e, col=col, psb=psb: e.activation(out=stg2[:, col:col + 512], in_=psb.t[0:16, :], func=AF.Copy),
                             reads=[psb], writes=[pl[2 + half * 4 + b_]])
                    else:
                        P.op("dve", lambda e, col=col, psb=psb: e.tensor_copy(out=stg2[:, col:col + 512], in_=psb.t[0:16, :]),
                             reads=[psb], writes=[pl[2 + half * 4 + b_]])
            P.op("sp", lambda e, nm=nm: e.dma_start(out=O[nm][l].rearrange("t g p -> t (g p)"), in_=stg2), reads=stg2_res, writes=[k.onext()], dma=True)
        for j in range(8):
            bb = k.blk[j % 2]
            P.op("sp", lambda e, bb=bb, j=j: e.dma_start(out=bb.t[:], in_=k.blkd[l, j]), reads=[k.blkd_res], writes=[bb], dma=True)

            def fy(e, j=j, bb=bb):
                ins = None
                for mm_ in range(4):
                    mt = 4 * j + mm_
                    e.matmul(k.ps[3].t[:, j * 16:(j + 1) * 16], bb.t[:, mm_ * 4 + 2, :], xre[:, mt * 16:(mt + 1) * 16], start=(mm_ == 0), stop=False)
                    ins = e.matmul(k.ps[3].t[:, j * 16:(j + 1) * 16], bb.t[:, mm_ * 4 + 3, :], xim[:, mt * 16:(mt + 1) * 16], start=False, stop=(mm_ == 3))
                return ins
            P.op("pe", fy, reads=[bb, pl[10]], writes=[k.ps[3]])
        y = pl[11].t[:, 0:128]
        y3 = y.rearrange("p (j t) -> p j t", t=16)
        P.op(V, lambda e: e.tensor_tensor(out=y3, in0=uT, in1=bcast_mid(k.s5d.t[:, l, :], 16), op=ALU.mult), reads=[pl[0], k.s5d], writes=[pl[11]])
        P.op(V, lambda e: e.tensor_add(out=y, in0=y, in1=k.ps[3].t[:, 0:128]), reads=[pl[11], k.ps[3]], writes=[pl[11]])
        t = pl[12].t[:, 0:128]
        P.op("act", lambda e: e.activation(out=t, in_=y, func=AF.Square), reads=[pl[11]], writes=[pl[12]])
        P.op(V, lambda e: e.tensor_scalar(out=t, in0=t, scalar1=0.044715, scalar2=1.0, op0=ALU.mult, op1=ALU.add), reads=[pl[12]], writes=[pl[12]])
        P.op(V, lambda e: e.tensor_mul(out=t, in0=t, in1=y), reads=[pl[12], pl[11]], writes=[pl[12]])
        P.op("act", lambda e: e.activation(out=t, in_=t, func=AF.Sigmoid, scale=1.5957691216057308), reads=[pl[12]], writes=[pl[12]])
        P.op(V, lambda e: e.tensor_tensor(out=k.big16.t[:, 0:8, 0:16], in0=t.rearrange("p (j t) -> p j t", t=16), in1=y3, op=ALU.mult),
             reads=[pl[12], pl[11]], writes=[k.big16])
        glu_stage(k, l, T)
        merge_stage(k, l, T)
        tokmajor_proj(k, G, I["w_out"][l], 0, [8, 8], 0)
        norm_stage(k, G, k.n2w, l)
        ffn_stage(k, G, l)


def make_in_maps(inputs):
    hc = host_consts()
    maps = []
    for c in range(8):
        m = {}
        b = c % 4
        m["x_prompt"] = np.ascontiguousarray(inputs["x_prompt"][b])
        sl = slice(c * 16, (c + 1) * 16)
        m["x_sample"] = np.ascontiguousarray(inputs["x_sample"][sl, 0])
        m["state_dn_conv"] = np.ascontiguousarray(inputs["state_dn_conv"][:, sl])
        m["state_dn_ssm"] = np.ascontiguousarray(inputs["state_dn_ssm"][:, sl])
        m["state_s5_re"] = np.ascontiguousarray(inputs["state_s5_re"][:, sl])
        m["state_s5_im"] = np.ascontiguousarray(inputs["state_s5_im"][:, sl])
        for nm in ("norm1", "w_in", "dn_conv_w", "dn_a_log", "dn_dt_bias", "dn_norm_w", "w_br_dn", "s5_lam_re", "s5_lam_im",
                   "s5_log_dt", "s5_b_re", "s5_b_im", "s5_c_re", "s5_c_im", "s5_d", "w_glu", "w_br_s5", "w_out", "norm2",
                   "w_ffn_gate", "w_ffn_up", "w_ffn_down", "norm_f"):
            m[nm] = np.ascontiguousarray(inputs[nm], dtype=np.float32)
        m.update(hc)
        maps.append(m)
    return maps


def kernel(**inputs):
    nc, k = build()
    maps = make_in_maps(inputs)
    res = run_bass_kernel_spmd(nc, maps, core_ids=list(range(8)))
    R = res.results
    y_prompt = np.stack([R[b]["y_prompt"] for b in range(4)], 0)
    y_sample = np.concatenate([R[c]["y_sample"] for c in range(8)], 0)[:, None, :]
    p_dn_conv = np.stack([R[b]["p_dn_conv"] for b in range(4)], 1)
    p_dn_ssm = np.stack([R[b]["p_dn_ssm"] for b in range(4)], 1)
    p_s5_re = np.stack([R[b]["p_s5_re"] for b in range(4)], 1)
    p_s5_im = np.stack([R[b]["p_s5_im"] for b in range(4)], 1)
    s_dn_conv = np.concatenate([R[c]["s_dn_conv"] for c in range(8)], 1)
    s_dn_ssm = np.concatenate([R[c]["s_dn_ssm"] for c in range(8)], 1)
    s_s5_re = np.concatenate([R[c]["s_s5_re"] for c in range(8)], 1)
    s_s5_im = np.concatenate([R[c]["s_s5_im"] for c in range(8)], 1)
    return (y_prompt, y_sample, p_dn_conv, p_dn_ssm, p_s5_re, p_s5_im, s_dn_conv, s_dn_ssm, s_s5_re, s_s5_im)
```

```python
import math
import numpy as np
import concourse.bass as bass
import concourse.mybir as mybir
from concourse.bass_utils import run_bass_kernel_spmd
from contextlib import ExitStack

F32 = mybir.dt.float32
BF16 = mybir.dt.bfloat16
AF = mybir.ActivationFunctionType
ALU = mybir.AluOpType
AX = mybir.AxisListType

SAME_ENG_SYNC = True
D = 2048
NH = 8
FFN = 5632
IN_DIM = 9232
BIG = 30000.0
PI = math.pi


class Res:
    __slots__ = ("name", "last_writer", "readers", "dsem")

    def __init__(self, name):
        self.name = name
        self.last_writer = None
        self.readers = []
        self.dsem = None


class Buf:
    def __init__(self, t, name):
        self.t = t
        self.res = Res(name)


class Op:
    __slots__ = ("eng", "fn", "deps", "signaled", "seq", "is_dma", "dsem", "dval")

    def __init__(self, eng, fn, is_dma=False):
        self.eng = eng
        self.fn = fn
        self.deps = []
        self.signaled = False
        self.seq = None
        self.is_dma = is_dma
        self.dsem = None
        self.dval = None


class Prog:
    ENGS = ("pe", "act", "dve", "pool", "sp")

    def __init__(self, nc):
        self.nc = nc
        self.ops = []
        self.stack = ExitStack()
        self.dsem_count = {}
        self.n_dsem = 0
        self.sb_bytes = 0

    def sbuf(self, name, shape, dtype):
        t = self.stack.enter_context(self.nc.sbuf_tensor(name, list(shape), dtype))
        n = 1
        for s in shape[1:]:
            n *= s
        self.sb_bytes += n * mybir.dt.size(dtype)
        return Buf(t, name)

    def psum(self, name, shape, dtype=F32):
        return Buf(self.stack.enter_context(self.nc.psum_tensor(name, list(shape), dtype)), name)

    def op(self, eng, fn, reads=(), writes=(), dma=False):
        o = Op(eng, fn, dma)
        deps = set()
        for r in reads:
            excl = getattr(r, "excl", False)
            r = r.res
            if r.last_writer is not None:
                deps.add(r.last_writer)
            if excl:
                for rd in r.readers:
                    if rd.eng != eng:
                        deps.add(rd)
        for w in writes:
            w = w.res
            if w.last_writer is not None:
                deps.add(w.last_writer)
            for rd in w.readers:
                deps.add(rd)
        for d in deps:
            if d.eng == eng and not d.is_dma:
                if eng == "pe" or not SAME_ENG_SYNC:
                    continue
            d.signaled = True
            o.deps.append(d)
        for r in reads:
            r.res.readers.append(o)
        for w in writes:
            w.res.last_writer = o
            w.res.readers = []
        if dma:
            key = writes[0].res
            if key.dsem is None:
                key.dsem = ("d", self.n_dsem)
                self.n_dsem += 1
                self.dsem_count[key.dsem] = 0
            self.dsem_count[key.dsem] += 1
            o.dsem = key.dsem
            o.dval = 16 * self.dsem_count[key.dsem]
            o.signaled = True
        self.ops.append(o)
        return o

    def emit(self):
        nc = self.nc
        cnt = {e: 0 for e in self.ENGS}
        for o in self.ops:
            if o.is_dma:
                continue
            if o.signaled:
                cnt[o.eng] += 1
                o.seq = cnt[o.eng]
        sems = {e: self.stack.enter_context(nc.semaphore("s_" + e)) for e in self.ENGS}
        dsems = {k: self.stack.enter_context(nc.semaphore("sd%d" % k[1])) for k in self.dsem_count}
        per_eng = {e: [] for e in self.ENGS}
        for o in self.ops:
            per_eng[o.eng].append(o)
        final_waits = [(dsems[k], 16 * v) for k, v in self.dsem_count.items()]

        def make(e, ops):
            def body(engobj):
                known = {}
                for o in ops:
                    need = {}
                    for d in o.deps:
                        k, v = (d.dsem, d.dval) if d.is_dma else (d.eng, d.seq)
                        if known.get(k, 0) >= v:
                            continue
                        if need.get(k, 0) < v:
                            need[k] = v
                    for k, v in need.items():
                        engobj.wait_ge(dsems[k] if isinstance(k, tuple) else sems[k], v)
                        known[k] = v
                    ins = o.fn(engobj)
                    if o.is_dma:
                        ins.then_inc(dsems[o.dsem], 16)
                    elif o.signaled:
                        ins.then_inc(sems[o.eng], 1)
                if e == "sp":
                    for s, v in final_waits:
                        engobj.wait_ge(s, v)
            return body

        with nc.Block() as block:
            deco = {"pe": block.tensor, "act": block.scalar, "dve": block.vector,
                    "pool": block.gpsimd, "sp": block.sync}
            for e in self.ENGS:
                if per_eng[e] or e == "sp":
                    deco[e](make(e, per_eng[e]))


def host_consts():
    c = {}
    c["c_ident"] = np.eye(128, dtype=np.float32)
    m = np.zeros((64, 3, 64), np.float32)
    p = np.arange(64)[:, None]
    f = np.arange(64)[None, :]
    m[:, 0, :] = np.where(f > p, BIG, 0.0)
    m[:, 1, :] = np.where(f < p, -BIG, 0.0)
    m[:, 2, :] = np.where(f < p, 1.0, 0.0)
    c["c_mask"] = m
    sel = np.zeros((8, 8, 128), np.float32)
    for h in range(8):
        sel[h, h, :] = 1.0
    c["c_sel"] = sel
    c["c_iota"] = np.broadcast_to(np.arange(1, 513, dtype=np.float32), (128, 512)).copy()
    mE = np.zeros((128, 4, 8), np.float32)
    for gl2 in range(2):
        for mm in range(4):
            mE[gl2 * 64:(gl2 + 1) * 64, mm, 2 * mm + gl2] = 1.0
    c["c_maskE"] = mE
    return c


class Ctx:
    pass


def build(n_ptiles=4, n_layers=2, do_sample=True, dbg=None):
    nc = bass.Bass("TRN2", target_bir_lowering=False)
    P = Prog(nc)
    dbg = dbg or []

    def din(name, shape):
        return nc.dram_tensor(name, list(shape), F32, kind="ExternalInput").ap()

    def dout(name, shape):
        return nc.dram_tensor(name, list(shape), F32, kind="ExternalOutput").ap()

    I = {}
    I["x_prompt"] = din("x_prompt", [2048, D])
    I["x_sample"] = din("x_sample", [16, D])
    I["state_dn_conv"] = din("state_dn_conv", [2, 16, 3, 3072])
    I["state_dn_ssm"] = din("state_dn_ssm", [2, 16, 8, 128, 128])
    I["state_s5_re"] = din("state_s5_re", [2, 16, 64, 64])
    I["state_s5_im"] = din("state_s5_im", [2, 16, 64, 64])
    for nm, sh in [("norm1", [2, D]), ("w_in", [2, D, IN_DIM]), ("dn_conv_w", [2, 4, 3072]), ("dn_a_log", [2, 8]),
                   ("dn_dt_bias", [2, 8]), ("dn_norm_w", [2, 128]), ("w_br_dn", [2, 1024, D]),
                   ("s5_lam_re", [2, 64, 64]), ("s5_lam_im", [2, 64, 64]), ("s5_log_dt", [2, 64]),
                   ("s5_b_re", [2, 64, 64, 16]), ("s5_b_im", [2, 64, 64, 16]), ("s5_c_re", [2, 64, 16, 64]),
                   ("s5_c_im", [2, 64, 16, 64]), ("s5_d", [2, 1024]), ("w_glu", [2, 1024, 1024]),
                   ("w_br_s5", [2, 1024, D]), ("w_out", [2, D, D]), ("norm2", [2, D]),
                   ("w_ffn_gate", [2, D, FFN]), ("w_ffn_up", [2, D, FFN]), ("w_ffn_down", [2, FFN, D]),
                   ("norm_f", [D])]:
        I[nm] = din(nm, sh)
    hc = host_consts()
    for k, v in hc.items():
        I[k] = din(k, v.shape)
    O = {}
    O["y_prompt"] = dout("y_prompt", [2048, D])
    O["y_sample"] = dout("y_sample", [16, D])
    O["p_dn_conv"] = dout("p_dn_conv", [2, 3, 3072])
    O["p_dn_ssm"] = dout("p_dn_ssm", [2, 8, 128, 128])
    O["p_s5_re"] = dout("p_s5_re", [2, 64, 64])
    O["p_s5_im"] = dout("p_s5_im", [2, 64, 64])
    O["s_dn_conv"] = dout("s_dn_conv", [2, 16, 3, 3072])
    O["s_dn_ssm"] = dout("s_dn_ssm", [2, 16, 8, 128, 128])
    O["s_s5_re"] = dout("s_s5_re", [2, 16, 64, 64])
    O["s_s5_im"] = dout("s_s5_im", [2, 16, 64, 64])
    DBG = {}
    for nm, sh in dbg:
        DBG[nm] = dout("dbg_" + nm, sh)
    tabd = nc.dram_tensor("tabd", [2, 32, 128, 2, 512], F32, kind="Internal").ap()
    blkd = nc.dram_tensor("blkd", [2, 8, 128, 16, 128], BF16, kind="Internal").ap()

    k = Ctx()
    k.nc, k.P, k.I, k.O, k.DBG = nc, P, I, O, DBG
    k.WC_NA, k.WC_NB = 220, 176
    k.wcacheA = nc.dram_tensor("wcacheA", [k.WC_NA, 128, 2048], BF16, kind="Internal").ap()
    k.wcacheB = nc.dram_tensor("wcacheB", [k.WC_NB, 128, 4096], BF16, kind="Internal").ap()
    k.wc_na, k.wc_nb = 0, 0
    k.wc_idx, k.wc_res, k.wc_last = {}, {}, {}
    k.wc_bufs = [Buf(None, "wcs%d" % i) for i in range(4)]
    k.wc_n = 0

    def wc_sem():
        k.wc_n += 1
        return k.wc_bufs[k.wc_n % 4]
    k.wc_sem = wc_sem
    k.tabd, k.blkd = tabd, blkd
    k.blkd_res = Buf(None, 'blkd_res')
    k.tabd_res = Buf(None, 'tabd_res')
    alloc_buffers(k)
    setup_consts(k)
    for l in range(n_layers):
        setup_layer(k, l)
    for l in range(n_layers):
        setup_s5(k, l)
    for ti in range(n_ptiles):
        run_prompt_tile(k, ti, n_ptiles, n_layers)
    if do_sample:
        run_sample(k, n_layers)
    P.emit()
    k.sb_bytes = P.sb_bytes
    P.stack.close()
    return nc, k


def alloc_buffers(k):
    P = k.P
    k.xres = P.sbuf("xres", [128, 4, D], F32)
    k.hT = P.sbuf("hT", [128, 16, 512], BF16)
    k.NSLOT = 4
    k.wring = [P.sbuf("wr%d" % i, [128, 4096], BF16) for i in range(k.NSLOT)]
    k.wnext = 0
    k.big16 = P.sbuf("big16", [128, 16, 512], BF16)
    k.oT = P.sbuf("oT", [128, 8, 512], BF16)
    k.g5T = P.sbuf("g5T", [128, 8, 512], BF16)
    k.NPOOL = 24
    pool_t = P.stack.enter_context(k.nc.sbuf_tensor("pool", [128, k.NPOOL, 512], F32))
    P.sb_bytes += k.NPOOL * 2048
    k.pool = []
    for i in range(k.NPOOL):
        b = Buf(pool_t[:, i, :], "pool%d" % i)
        k.pool.append(b)
    k.pool_t = pool_t
    k.pre = P.sbuf("pre", [128, 3, 516], F32)
    k.onb = P.sbuf("onb", [64, 8, 128], BF16)
    k.oraw = P.sbuf("oraw", [64, 8, 128], F32)
    k.blk = [P.sbuf("blk%d" % i, [128, 16, 128], BF16) for i in range(2)]
    k.tab = [P.sbuf("tab0", [128, 2, 512], F32)] * 2
    k.small = P.sbuf("small", [128, 64], F32)
    k.ps = [P.psum("ps%d" % i, [128, 512], F32) for i in range(8)]
    for b in k.ps:
        b.excl = True
    k.ident = P.sbuf("ident", [128, 128], F32)
    k.identb = P.sbuf("identb", [128, 128], BF16)
    k.ones = P.sbuf("ones", [128, 128], F32)
    k.mask = P.sbuf("mask", [64, 3, 64], F32)
    k.maskE = P.sbuf("maskE", [128, 4, 8], F32)
    k.n1w = P.sbuf("n1w", [128, 2, 16], F32)
    k.n2w = P.sbuf("n2w", [128, 2, 16], F32)
    k.convw = P.sbuf("convw", [128, 2, 24, 4], F32)
    k.alog = P.sbuf("alog", [8, 2, 2], F32)
    k.dnw = P.sbuf("dnw", [128, 2], F32)
    k.s5d = P.sbuf("s5d", [128, 2, 8], F32)
    k.convhist = P.sbuf("convhist", [128, 2, 24, 3], F32)
    k.S = P.sbuf("S", [128, 2, 8, 128], F32)
    k.Sb = P.sbuf("Sb", [128, 2, 8, 128], BF16)
    k.s5x = P.sbuf("s5x", [128, 2, 2, 32], F32)
    k.s5m = P.sbuf("s5m", [128, 2, 3, 32], F32)
    k.gcol = P.sbuf("gcol", [64, 4, 8, 8], F32)
    k.xn = Buf(k.pre.t[:].rearrange("p a b -> p (a b)").bitcast(BF16)[:, 0:D], "xn")
    k.xn.res = k.pre.res
    k.epsc = P.sbuf("epsc", [128, 1], F32)
    k.onec = P.sbuf("onec", [128, 1], F32)
    k.vnewb = [P.sbuf("vnew%d" % i, [64, 128], BF16) for i in range(2)]
    k.c48 = P.sbuf("c48", [48, 3, 128], F32)
    k.cacc = P.sbuf("cacc", [128, 3, 16], F32)
    k.nr = P.sbuf("nr", [128, 24, 16], F32)
    k.SS = Buf(k.S.t[:, 0, :, :], "SS")
    k.SS.res = k.S.res
    k.SSb = Buf(k.Sb.t[:, 0, :, :], "SSb")
    k.SSb.res = k.Sb.res
    k.sg = P.sbuf("sg", [8, 2, 16], F32)
    k.zs16 = P.sbuf("zs16", [128, 16], BF16)
    k.outs = [Buf(None, "out%d" % i) for i in range(8)]
    k.outn = 0

    def onext():
        k.outn += 1
        return k.outs[k.outn % 8]
    k.onext = onext


def setup_consts(k):
    P, I = k.P, k.I
    P.op("sp", lambda e: e.dma_start(out=k.ident.t[:], in_=I["c_ident"]), writes=[k.ident], dma=True)
    P.op("sp", lambda e: e.dma_start(out=k.mask.t[:], in_=I["c_mask"]), writes=[k.mask], dma=True)
    P.op("sp", lambda e: e.dma_start(out=k.maskE.t[:], in_=I["c_maskE"]), writes=[k.maskE], dma=True)
    P.op("dve", lambda e: e.tensor_copy(out=k.identb.t[:], in_=k.ident.t[:]), reads=[k.ident], writes=[k.identb])
    P.op("dve", lambda e: e.memset(k.ones.t[:], 1.0), writes=[k.ones])
    P.op("dve", lambda e: e.memset(k.convhist.t[:], 0.0), writes=[k.convhist])
    P.op("dve", lambda e: e.memset(k.S.t[:], 0.0), writes=[k.S])
    P.op("dve", lambda e: e.memset(k.s5x.t[:], 0.0), writes=[k.s5x])
    P.op("dve", lambda e: e.memset(k.Sb.t[:], 0.0), writes=[k.Sb])
    P.op("dve", lambda e: e.memset(k.epsc.t[:], 1e-6), writes=[k.epsc])
    P.op("dve", lambda e: e.memset(k.onec.t[:], 1.0), writes=[k.onec])


def setup_layer(k, l):
    P, I = k.P, k.I
    P.op("sp", lambda e: e.dma_start(out=k.n1w.t[:, l, :], in_=I["norm1"][l].rearrange("(c p) -> p c", p=128),
                                     allow_slow_non_contiguous=True), writes=[k.n1w], dma=True)
    P.op("sp", lambda e: e.dma_start(out=k.n2w.t[:, l, :], in_=I["norm2"][l].rearrange("(c p) -> p c", p=128),
                                     allow_slow_non_contiguous=True), writes=[k.n2w], dma=True)
    for i in range(4):
        P.op("sp", lambda e, i=i: e.dma_start(out=k.convw.t[:, l, :, i], in_=I["dn_conv_w"][l, i].rearrange("(c p) -> p c", p=128),
                                              allow_slow_non_contiguous=True), writes=[k.convw], dma=True)
    P.op("sp", lambda e: e.dma_start(out=k.alog.t[:, l, 0:1], in_=I["dn_a_log"][l].rearrange("(h o) -> h o", o=1)),
         writes=[k.alog], dma=True)
    P.op("sp", lambda e: e.dma_start(out=k.alog.t[:, l, 1:2], in_=I["dn_dt_bias"][l].rearrange("(h o) -> h o", o=1)),
         writes=[k.alog], dma=True)
    P.op("act", lambda e: e.activation(out=k.alog.t[:, l, 0:1], in_=k.alog.t[:, l, 0:1], func=AF.Exp),
         reads=[k.alog], writes=[k.alog])
    P.op("dve", lambda e: e.tensor_scalar_mul(out=k.alog.t[:, l, 0:1], in0=k.alog.t[:, l, 0:1], scalar1=-1.0),
         reads=[k.alog], writes=[k.alog])
    P.op("sp", lambda e: e.dma_start(out=k.dnw.t[:, l:l + 1], in_=I["dn_norm_w"][l].rearrange("(p o) -> p o", o=1)),
         writes=[k.dnw], dma=True)
    P.op("sp", lambda e: e.dma_start(out=k.s5d.t[:, l, :], in_=I["s5_d"][l].rearrange("(c p) -> p c", p=128),
                                     allow_slow_non_contiguous=True), writes=[k.s5d], dma=True)


def bcast_mid(ap2d, n):
    p, a = ap2d.shape
    return ap2d.unsqueeze(2).to_broadcast([p, a, n])


def wload(k, src2d, r0, nk, c0, ncols):
    slot = k.wring[k.wnext]
    k.wnext = (k.wnext + 1) % k.NSLOT
    n = nk * ncols
    assert n <= 4096
    flat = slot.t[:, 0:n]
    view = flat.rearrange("p (k n) -> p k n", k=nk)
    key = (src2d.tensor.name, int(src2d.offset), r0, nk, c0, ncols)
    idx = k.wc_idx.get(key)
    if idx is None:
        if n <= 2048:
            idx = ("A", k.wc_na)
            k.wc_na += 1
            assert k.wc_na <= k.WC_NA
        else:
            idx = ("B", k.wc_nb)
            k.wc_nb += 1
            assert k.wc_nb <= k.WC_NB
        k.wc_idx[key] = idx
        src = src2d[r0:r0 + nk * 128, c0:c0 + ncols].rearrange("(k p) n -> p k n", p=128)
        k.P.op("pool", lambda e: e.dma_start(out=view, in_=src), writes=[slot], dma=True)
        k.P.op("sp", lambda e: e.dma_start(out=(k.wcacheA if idx[0] == "A" else k.wcacheB)[idx[1], :, 0:n], in_=flat), reads=[slot], writes=[k.wc_sem()], dma=True)
        k.wc_last[idx] = k.P.ops[-1]
    else:
        st = k.wc_last[idx]
        op = k.P.op("sp", lambda e: e.dma_start(out=flat, in_=(k.wcacheA if idx[0] == "A" else k.wcacheB)[idx[1], :, 0:n]), writes=[slot], dma=True)
        if st is not None and st not in op.deps:
            op.deps.append(st)
    return slot, view


def mm(k, psb, out_ap, pairs, reads):
    def fn(e):
        n = len(pairs)
        ins = None
        for i, (l_, r_) in enumerate(pairs):
            ins = e.matmul(out_ap, l_, r_, start=(i == 0), stop=(i == n - 1))
        return ins
    k.P.op("pe", fn, reads=reads, writes=[psb])


def tr(k, psb, out_ap, in_ap, ident_ap, reads):
    k.P.op("pe", lambda e: e.transpose(out_ap, in_ap, ident_ap), reads=reads, writes=[psb])


def psbf(psb):
    return psb.t[:].bitcast(BF16)


def tokv(k, q):
    return k.pool_t[0:64, 8 + q, :].bitcast(BF16).rearrange("p (c d) -> p c d", d=128)


def bfview(buf):
    return buf.t.bitcast(BF16)


def setup_s5(k, l):
    P, I, pl = k.P, k.I, k.pool

    def v(i):
        return pl[i].t[:, 0:32]
    V, A_, Gp = "dve", "act", "pool"
    P.op("sp", lambda e: e.dma_start(out=v(0), in_=I["s5_lam_re"][l].rearrange("g p -> (g p)").rearrange("(m q) -> q m", q=128),
                                     allow_slow_non_contiguous=True), writes=[pl[0]], dma=True)
    P.op("sp", lambda e: e.dma_start(out=v(1), in_=I["s5_lam_im"][l].rearrange("g p -> (g p)").rearrange("(m q) -> q m", q=128),
                                     allow_slow_non_contiguous=True), writes=[pl[1]], dma=True)
    ldt = I["s5_log_dt"]
    for gl2 in range(2):
        src = bass.AP(ldt.tensor, l * 64 + gl2, [[0, 64], [2, 32]])
        P.op("sp", lambda e, src=src, gl2=gl2: e.dma_start(out=pl[2].t[gl2 * 64:(gl2 + 1) * 64, 0:32], in_=src,
                                                          allow_slow_non_contiguous=True), writes=[pl[2]], dma=True)
    P.op(A_, lambda e: e.activation(out=v(2), in_=v(2), func=AF.Exp), reads=[pl[2]], writes=[pl[2]])
    P.op(V, lambda e: e.tensor_mul(out=v(3), in0=v(0), in1=v(2)), reads=[pl[0], pl[2]], writes=[pl[3]])
    P.op(V, lambda e: e.tensor_mul(out=v(4), in0=v(1), in1=v(2)), reads=[pl[1], pl[2]], writes=[pl[4]])
    P.op(A_, lambda e: e.activation(out=v(5), in_=v(3), func=AF.Exp), reads=[pl[3]], writes=[pl[5]])
    P.op(A_, lambda e: e.activation(out=v(8), in_=v(4), func=AF.Sin, scale=0.125), reads=[pl[4]], writes=[pl[8]])
    P.op(A_, lambda e: e.activation(out=v(6), in_=v(4), func=AF.Sin, scale=0.0625), reads=[pl[4]], writes=[pl[6]])
    P.op(V, lambda e: e.tensor_mul(out=v(6), in0=v(6), in1=v(6)), reads=[pl[6]], writes=[pl[6]])
    P.op(V, lambda e: e.tensor_scalar(out=v(7), in0=v(6), scalar1=-2.0, scalar2=1.0, op0=ALU.mult, op1=ALU.add), reads=[pl[6]], writes=[pl[7]])
    for _ in range(3):
        P.op(V, lambda e: e.tensor_mul(out=v(6), in0=v(7), in1=v(8)), reads=[pl[7], pl[8]], writes=[pl[6]])
        P.op(V, lambda e: e.tensor_mul(out=v(7), in0=v(7), in1=v(7)), reads=[pl[7]], writes=[pl[7]])
        P.op(V, lambda e: e.tensor_mul(out=v(8), in0=v(8), in1=v(8)), reads=[pl[8]], writes=[pl[8]])
        P.op(V, lambda e: e.tensor_sub(out=v(7), in0=v(7), in1=v(8)), reads=[pl[7], pl[8]], writes=[pl[7]])
        P.op(V, lambda e: e.tensor_scalar_mul(out=v(8), in0=v(6), scalar1=2.0), reads=[pl[6]], writes=[pl[8]])
    P.op(V, lambda e: e.tensor_mul(out=v(9), in0=v(5), in1=v(7)), reads=[pl[5], pl[7]], writes=[pl[9]])
    P.op(V, lambda e: e.tensor_mul(out=v(10), in0=v(5), in1=v(8)), reads=[pl[5], pl[8]], writes=[pl[10]])
    P.op(V, lambda e: e.tensor_scalar_add(out=v(11), in0=v(9), scalar1=-1.0), reads=[pl[9]], writes=[pl[11]])
    P.op(V, lambda e: e.tensor_mul(out=v(12), in0=v(0), in1=v(0)), reads=[pl[0]], writes=[pl[12]])
    P.op(V, lambda e: e.tensor_mul(out=v(6), in0=v(1), in1=v(1)), reads=[pl[1]], writes=[pl[6]])
    P.op(V, lambda e: e.tensor_add(out=v(12), in0=v(12), in1=v(6)), reads=[pl[12], pl[6]], writes=[pl[12]])
    P.op(V, lambda e: e.reciprocal(out=v(12), in_=v(12)), reads=[pl[12]], writes=[pl[12]])
    P.op(V, lambda e: e.tensor_mul(out=v(13), in0=v(11), in1=v(0)), reads=[pl[11], pl[0]], writes=[pl[13]])
    P.op(V, lambda e: e.tensor_mul(out=v(6), in0=v(10), in1=v(1)), reads=[pl[10], pl[1]], writes=[pl[6]])
    P.op(V, lambda e: e.tensor_add(out=v(13), in0=v(13), in1=v(6)), reads=[pl[13], pl[6]], writes=[pl[13]])
    P.op(V, lambda e: e.tensor_mul(out=v(13), in0=v(13), in1=v(12)), reads=[pl[13], pl[12]], writes=[pl[13]])
    P.op(V, lambda e: e.tensor_mul(out=v(14), in0=v(10), in1=v(0)), reads=[pl[10], pl[0]], writes=[pl[14]])
    P.op(V, lambda e: e.tensor_mul(out=v(6), in0=v(11), in1=v(1)), reads=[pl[11], pl[1]], writes=[pl[6]])
    P.op(V, lambda e: e.tensor_sub(out=v(14), in0=v(14), in1=v(6)), reads=[pl[14], pl[6]], writes=[pl[14]])
    P.op(V, lambda e: e.tensor_mul(out=v(14), in0=v(14), in1=v(12)), reads=[pl[14], pl[12]], writes=[pl[14]])
    for qi, src in enumerate((5, 9, 10)):
        P.op(V, lambda e, qi=qi, src=src: e.tensor_copy(out=k.s5m.t[:, l, qi, :], in_=v(src)), reads=[pl[src]], writes=[k.s5m])

    def v3(i):
        return pl[i].t[:, :].rearrange("p (m c) -> p m c", c=16)
    for idx, nm in ((16, "s5_b_re"), (17, "s5_b_im")):
        src = bass.AP(I[nm].tensor, l * 65536, [[16, 128], [2048, 32], [1, 16]])
        P.op("sp", lambda e, idx=idx, src=src: e.dma_start(out=v3(idx), in_=src), writes=[pl[idx]], dma=True)
    frb = bcast_mid(v(13), 16)
    fib = bcast_mid(v(14), 16)
    P.op(V, lambda e: e.tensor_mul(out=v3(18), in0=v3(16), in1=frb), reads=[pl[16], pl[13]], writes=[pl[18]])
    P.op(V, lambda e: e.tensor_mul(out=v3(19), in0=v3(17), in1=fib), reads=[pl[17], pl[14]], writes=[pl[3]])
    P.op(V, lambda e: e.tensor_sub(out=v3(18), in0=v3(18), in1=v3(19)), reads=[pl[18], pl[3]], writes=[pl[18]])
    P.op(V, lambda e: e.tensor_mul(out=v3(19), in0=v3(17), in1=frb), reads=[pl[17], pl[13]], writes=[pl[3]])
    P.op(V, lambda e: e.tensor_mul(out=v3(20), in0=v3(16), in1=fib), reads=[pl[16], pl[14]], writes=[pl[3]])
    P.op(V, lambda e: e.tensor_add(out=v3(19), in0=v3(19), in1=v3(20)), reads=[pl[3], pl[3]], writes=[pl[3]])
    cn = k.pool_t[:, 0:2, :].rearrange("p a (j q) -> p (a j) q", q=128)
    for idx, nm in ((21, "s5_c_re"), (22, "s5_c_im")):
        srcn = I[nm][l].rearrange("g c p -> (g c) p").rearrange("(j q) p -> q j p", q=128)
        for dup in range(2):
            P.op("sp", lambda e, srcn=srcn, dup=dup: e.dma_start(out=cn[:, :, dup * 64:(dup + 1) * 64], in_=srcn),
                 writes=[pl[0], pl[1]], dma=True)
        for j in range(8):
            psb = k.ps[j % 4]
            tr(k, psb, psb.t[:, 0:128], cn[:, j, :], k.ident.t[:], [pl[0], pl[1], k.ident])
            for gl2 in range(2):
                srcv = psb.t[gl2 * 64:(gl2 + 1) * 64, 0:128].rearrange("p (mm g c) -> p mm g c", g=2, c=16)[:, :, gl2, :]
                dstv = pl[idx].t[gl2 * 64:(gl2 + 1) * 64, :].rearrange("p (m c) -> p m c", c=16)[:, 4 * j:4 * j + 4, :]
                P.op(A_, lambda e, srcv=srcv, dstv=dstv: e.activation(out=dstv, in_=srcv, func=AF.Copy), reads=[psb], writes=[pl[idx]])
    for j in range(8):
        bb = k.blk[j % 2]
        for mm_ in range(4):
            mt = 4 * j + mm_
            mk = k.maskE.t[:, mm_, :].unsqueeze(2).to_broadcast([128, 8, 16])
            for kind, srcslot in ((0, 18), (1, 19)):
                srcv = v3(srcslot)[:, mt, :].unsqueeze(1).to_broadcast([128, 8, 16])
                E = pl[23]
                Ev = E.t[:, 0:128].rearrange("p (g c) -> p g c", c=16)
                P.op(V, lambda e, Ev=Ev, srcv=srcv, mk=mk: e.tensor_tensor(out=Ev, in0=srcv, in1=mk, op=ALU.mult),
                     reads=[pl[srcslot], k.maskE], writes=[E])
                psb = k.ps[(mm_ * 2 + kind) % 8]
                tr(k, psb, psb.t[:, 0:128], E.t[:, 0:128], k.ident.t[:], [E, k.ident])
                P.op(A_, lambda e, psb=psb, b=mm_ * 4 + kind, bb=bb: e.activation(out=bb.t[:, b, :], in_=psb.t[:, 0:128], func=AF.Copy),
                     reads=[psb], writes=[bb])
            for kind, srcslot, sc in ((2, 21, 1.0), (3, 22, -1.0)):
                srcv = v3(srcslot)[:, mt, :].unsqueeze(1).to_broadcast([128, 8, 16])
                outv = bb.t[:, mm_ * 4 + kind, :].rearrange("p (g c) -> p g c", c=16)
                P.op(V, lambda e, outv=outv, srcv=srcv, mk=mk, sc=sc: e.scalar_tensor_tensor(
                    out=outv, in0=srcv, scalar=sc, in1=mk, op0=ALU.mult, op1=ALU.mult),
                    reads=[pl[srcslot], k.maskE], writes=[bb])
        P.op("sp", lambda e, bb=bb, j=j: e.dma_start(out=k.blkd[l, j], in_=bb.t[:]), reads=[bb], writes=[k.blkd_res], dma=True)
    Tc = k.pool_t[:, 16:20, :]
    Ts = k.pool_t[:, 20:24, :]
    t1 = k.pool_t[:, 9:11, :].rearrange("p a (g n) -> p (a g) n", g=2)
    t2 = k.pool_t[:, 11:13, :].rearrange("p a (g n) -> p (a g) n", g=2)
    rc, rs_, rt1, rt2 = [pl[i] for i in range(16, 20)], [pl[i] for i in range(20, 24)], [pl[9], pl[10]], [pl[11], pl[12]]
    for g4 in range(8):
        m0 = 4 * g4
        P.op(V, lambda e, m0=m0: e.tensor_copy(out=Tc[:, :, 0:1], in_=pl[7].t[:, m0:m0 + 4].unsqueeze(2)), reads=[pl[7]], writes=rc)
        P.op(V, lambda e, m0=m0: e.tensor_copy(out=Ts[:, :, 0:1], in_=pl[8].t[:, m0:m0 + 4].unsqueeze(2)), reads=[pl[8]], writes=rs_)
        n = 1
        while n < 512:
            cb = Tc[:, :, n - 1:n].to_broadcast([128, 4, n])
            sb = Ts[:, :, n - 1:n].to_broadcast([128, 4, n])
            P.op(V, lambda e, n=n, sb=sb: e.tensor_tensor(out=t1[:, :, 0:n], in0=Ts[:, :, 0:n], in1=sb, op=ALU.mult), reads=rs_, writes=rt1)
            P.op(V, lambda e, n=n, sb=sb: e.tensor_tensor(out=t2[:, :, 0:n], in0=Tc[:, :, 0:n], in1=sb, op=ALU.mult), reads=rc + rs_, writes=rt2)
            P.op(V, lambda e, n=n, cb=cb: e.tensor_tensor(out=Tc[:, :, n:2 * n], in0=Tc[:, :, 0:n], in1=cb, op=ALU.mult), reads=rc, writes=rc)
            P.op(V, lambda e, n=n, cb=cb: e.tensor_tensor(out=Ts[:, :, n:2 * n], in0=Ts[:, :, 0:n], in1=cb, op=ALU.mult), reads=rs_ + rc, writes=rs_)
            P.op(V, lambda e, n=n: e.tensor_sub(out=Tc[:, :, n:2 * n], in0=Tc[:, :, n:2 * n], in1=t1[:, :, 0:n]), reads=rc + rt1, writes=rc)
            P.op(V, lambda e, n=n: e.tensor_add(out=Ts[:, :, n:2 * n], in0=Ts[:, :, n:2 * n], in1=t2[:, :, 0:n]), reads=rs_ + rt2, writes=rs_)
            n *= 2
        for i in range(4):
            P.op("sp", lambda e, i=i, m0=m0: e.dma_start(out=k.tabd[l, m0 + i, :, 0, :], in_=Tc[:, i, :]), reads=rc, writes=[k.tabd_res], dma=True)
            P.op("sp", lambda e, i=i, m0=m0: e.dma_start(out=k.tabd[l, m0 + i, :, 1, :], in_=Ts[:, i, :]), reads=rs_, writes=[k.tabd_res], dma=True)

class Grp:
    def __init__(self, T, np_, nsub, sample):
        self.T, self.np, self.nsub, self.sample = T, np_, nsub, sample


def norm_stage(k, G, wbuf, l):
    P = k.P
    np_ = G.np
    for s in range(G.nsub):
        xs = k.xres.t[0:np_, s, :]
        c0 = k.small.t[0:np_, 2 * s:2 * s + 1]
        c1 = k.small.t[0:np_, 2 * s + 1:2 * s + 2]
        xn = k.xn.t[0:np_, :]
        P.op("dve", lambda e, c0=c0: e.memset(c0, 0.0), writes=[k.small])
        P.op("act", lambda e, xn=xn, xs=xs, c0=c0: e.activation(out=xn, in_=xs, func=AF.Square, accum_out=c0),
             reads=[k.xres, k.small], writes=[k.xn, k.small])
        P.op("dve", lambda e, c0=c0, c1=c1: e.tensor_scalar(out=c1, in0=c0, scalar1=1.0 / D, scalar2=1e-6, op0=ALU.mult, op1=ALU.add),
             reads=[k.small], writes=[k.small])
        P.op("act", lambda e, c1=c1: e.activation(out=c1, in_=c1, func=AF.Sqrt), reads=[k.small], writes=[k.small])
        P.op("dve", lambda e, c1=c1: e.reciprocal(out=c1, in_=c1), reads=[k.small], writes=[k.small])
        P.op("act", lambda e, xn=xn, xs=xs, c1=c1: e.activation(out=xn, in_=xs, func=AF.Copy, scale=c1),
             reads=[k.xres, k.small], writes=[k.xn])
        for half in range(2):
            psb = k.ps[half]
            pv = psbf(psb)

            def fn(e, half=half, pv=pv):
                ins = None
                for c in range(8):
                    ins = e.transpose(pv[:, c * np_:(c + 1) * np_], k.xn.t[0:np_, (half * 8 + c) * 128:(half * 8 + c + 1) * 128],
                                      k.identb.t[0:np_, 0:np_])
                return ins
            P.op("pe", fn, reads=[k.xn, k.identb], writes=[psb])
            outv = k.hT.t[:, half * 8:(half + 1) * 8, s * np_:(s + 1) * np_]
            inv = pv[:, 0:8 * np_].rearrange("p (c t) -> p c t", t=np_)
            wv = bcast_mid(wbuf.t[:, l, half * 8:(half + 1) * 8], np_)
            P.op("dve", lambda e, outv=outv, inv=inv, wv=wv: e.tensor_tensor(out=outv, in0=inv, in1=wv, op=ALU.mult),
                 reads=[psb, wbuf], writes=[k.hT])


def proj_F(k, l, wsrc, c0, ncols, psb, T, rhs=None, nk=16, rhs_buf=None, m0=0, mcols=None):
    slot, wv = wload(k, wsrc, 0, nk, c0, ncols)
    rb = rhs_buf or k.hT
    rt = rhs if rhs is not None else k.hT.t
    mcols = mcols or ncols
    pairs = [(wv[:, kc, m0:m0 + mcols], rt[:, kc, 0:T]) for kc in range(nk)]
    mm(k, psb, psb.t[0:mcols, 0:T], pairs, [slot, rb])


def grow(k, r):
    return k.pool[19 + r].t[0:8, :]


def gbuf(k, r):
    return k.pool[19 + r]


def gates_stage(k, G, l):
    P = k.P
    T = G.T
    slot, wv = wload(k, k.I["w_in"][l], 0, 16, 4096, 16)
    mm(k, k.ps[2], k.ps[2].t[0:8, 0:T], [(wv[:, kc, 0:8], k.hT.t[:, kc, 0:T]) for kc in range(16)], [slot, k.hT])
    mm(k, k.ps[3], k.ps[3].t[0:8, 0:T], [(wv[:, kc, 8:16], k.hT.t[:, kc, 0:T]) for kc in range(16)], [slot, k.hT])
    P.op("act", lambda e: e.activation(out=grow(k, 0)[:, 0:T], in_=k.ps[2].t[0:8, 0:T], func=AF.Sigmoid), reads=[k.ps[2]], writes=[gbuf(k, 0)])
    P.op("act", lambda e: e.activation(out=grow(k, 1)[:, 0:T], in_=k.ps[3].t[0:8, 0:T], func=AF.Exp, bias=k.alog.t[:, l, 1:2]),
         reads=[k.ps[3], k.alog], writes=[gbuf(k, 1)])
    P.op("act", lambda e: e.activation(out=grow(k, 1)[:, 0:T], in_=grow(k, 1)[:, 0:T], func=AF.Ln, bias=k.onec.t[0:8, 0:1]), reads=[gbuf(k, 1), k.onec], writes=[gbuf(k, 1)])
    P.op("dve", lambda e: e.tensor_scalar_mul(out=grow(k, 1)[:, 0:T], in0=grow(k, 1)[:, 0:T], scalar1=k.alog.t[:, l, 0:1]),
         reads=[gbuf(k, 1), k.alog], writes=[gbuf(k, 1)])


def gates_derive(k, nch):
    P = k.P
    T = nch * 64
    for c in range(nch):
        P.op("dve", lambda e, c=c: e.tensor_tensor_scan(out=grow(k, 2)[:, c * 64:(c + 1) * 64], data0=k.ones.t[0:8, 0:64],
                                                         data1=grow(k, 1)[:, c * 64:(c + 1) * 64], initial=0.0, op0=ALU.mult, op1=ALU.add),
             reads=[gbuf(k, 1), k.ones], writes=[gbuf(k, 2)])
    P.op("act", lambda e: e.activation(out=grow(k, 3)[:, 0:T], in_=grow(k, 2)[:, 0:T], func=AF.Exp), reads=[gbuf(k, 2)], writes=[gbuf(k, 3)])
    P.op("dve", lambda e: e.tensor_mul(out=grow(k, 3)[:, 0:T], in0=grow(k, 0)[:, 0:T], in1=grow(k, 3)[:, 0:T]), reads=[gbuf(k, 0), gbuf(k, 3)], writes=[gbuf(k, 3)])
    g3 = grow(k, 2)[:, 0:T].rearrange("p (c t) -> p c t", t=64)
    P.op("dve", lambda e: e.tensor_tensor(out=grow(k, 4)[:, 0:T].rearrange("p (c t) -> p c t", t=64), in0=g3,
                                          in1=g3[:, :, 63:64].to_broadcast([8, nch, 64]), op=ALU.subtract),
         reads=[gbuf(k, 2)], writes=[gbuf(k, 4)])
    P.op("act", lambda e: e.activation(out=grow(k, 4)[:, 0:T], in_=grow(k, 4)[:, 0:T], func=AF.Exp, scale=-1.0), reads=[gbuf(k, 4)], writes=[gbuf(k, 4)])
    psb = k.ps[4]

    def fn(e):
        ins = None
        for qi, row in enumerate((2, 0, 3, 4)):
            for c in range(nch):
                ins = e.transpose(psb.t[0:64, (qi * 8 + c) * 8:(qi * 8 + c + 1) * 8], grow(k, row)[:, c * 64:(c + 1) * 64], k.ident.t[0:8, 0:8])
        return ins
    P.op("pe", fn, reads=[gbuf(k, 0), gbuf(k, 2), gbuf(k, 3), gbuf(k, 4), k.ident], writes=[psb])
    P.op("dve", lambda e: e.tensor_copy(out=k.gcol.t[:].rearrange("p q c h -> p (q c h)"), in_=psb.t[0:64, 0:256]),
         reads=[psb], writes=[k.gcol])


def dn_prep_head(k, l, h, nch, ident_T=False):
    P, pl = k.P, k.pool
    T = nch * 64
    V, A_ = "dve", "act"
    qT, kT = pl[0].t[:, 0:T], pl[1].t[:, 0:T]
    vTb = bfview(pl[2])[:, 0:T]
    knT, qnT = bfview(pl[4])[:, 0:T], bfview(pl[4])[:, 512:512 + T]
    qdT, nWT = bfview(pl[5])[:, 0:T], bfview(pl[5])[:, 512:512 + T]
    for src, srcb, psb, dst, scale in ((qT, pl[0], k.ps[0], qnT, 128 ** -0.5), (kT, pl[1], k.ps[1], knT, 1.0)):
        sq = pl[3].t[:, 0:T]
        P.op(A_, lambda e, sq=sq, src=src: e.activation(out=sq, in_=src, func=AF.Square), reads=[srcb], writes=[pl[3]])
        mm(k, psb, psb.t[:, 0:T], [(k.ones.t[:], sq)], [k.ones, pl[3]])
        P.op(A_, lambda e, sq=sq, psb=psb: e.activation(out=sq, in_=psb.t[:, 0:T], func=AF.Ln, bias=k.epsc.t[:, 0:1]),
             reads=[psb, k.epsc], writes=[pl[3]])
        P.op(A_, lambda e, sq=sq: e.activation(out=sq, in_=sq, func=AF.Exp, scale=-0.5), reads=[pl[3]], writes=[pl[3]])
        P.op(V, lambda e, sq=sq, src=src, dst=dst, scale=scale: e.scalar_tensor_tensor(
            out=dst, in0=src, scalar=scale, in1=sq, op0=ALU.mult, op1=ALU.mult), reads=[srcb, pl[3]], writes=[pl[4]])
    import os
    KP = int(os.environ.get("KP", "9"))
    if KP <= 1:
        return
    gsel = pl[8].t[0:8, 0:T]
    P.op(V, lambda e: e.tensor_scalar_mul(out=gsel, in0=grow(k, 2)[:, 0:T], scalar1=k.ident.t[0:8, h:h + 1]),
         reads=[gbuf(k, 2), k.ident], writes=[pl[8]])
    mm(k, k.ps[2], k.ps[2].t[:, 0:T], [(k.ones.t[0:8, :], gsel)], [k.ones, pl[8]])
    P.op(V, lambda e: e.tensor_copy(out=pl[6].t[:, 0:T], in_=k.ps[2].t[:, 0:T]), reads=[k.ps[2]], writes=[pl[6]])
    P.op(A_, lambda e: e.activation(out=pl[7].t[:, 0:T], in_=k.ps[2].t[:, 0:T], func=AF.Exp), reads=[k.ps[2]], writes=[pl[7]])
    P.op(V, lambda e: e.tensor_mul(out=qdT, in0=qnT, in1=pl[7].t[:, 0:T]), reads=[pl[4], pl[7]], writes=[pl[5]])
    if KP <= 2:
        return
    mm_chunks = []

    def fnG(e):
        if ident_T:
            return e.matmul(k.ps[3].t[0:64, 0:64], knT[:, 0:64], knT[:, 0:64], start=True, stop=True)
        ins = None
        for c in range(nch):
            ins = e.matmul(k.ps[3].t[0:64, c * 64:(c + 1) * 64], knT[:, c * 64:(c + 1) * 64], knT[:, c * 64:(c + 1) * 64], start=True, stop=True)
        return ins
    P.op("pe", fnG, reads=[pl[4]], writes=[k.ps[3]])

    def fnQ(e):
        ins = None
        for c in range(nch):
            ins = e.matmul(k.ps[4].t[0:64, c * 64:(c + 1) * 64], knT[:, c * 64:(c + 1) * 64], qnT[:, c * 64:(c + 1) * 64], start=True, stop=True)
        return ins
    P.op("pe", fnQ, reads=[pl[4]], writes=[k.ps[4]])
    v3 = lambda ap: ap.rearrange("p (c t) -> p c t", t=64)
    negD, mA, mB = pl[8].t[0:64, 0:T], pl[9].t[0:64, 0:T], pl[10].t[0:64, 0:T]
    P.op(V, lambda e: e.tensor_tensor(out=v3(negD), in0=v3(pl[6].t[0:64, 0:T]), in1=bcast_mid(k.gcol.t[:, 0, 0:nch, h], 64), op=ALU.subtract),
         reads=[pl[6], k.gcol], writes=[pl[8]])
    P.op(V, lambda e: e.tensor_tensor(out=v3(mA), in0=v3(negD), in1=k.mask.t[:, 0:1, :].to_broadcast([64, nch, 64]), op=ALU.add),
         reads=[pl[8], k.mask], writes=[pl[9]])
    P.op(A_, lambda e: e.activation(out=mA, in_=mA, func=AF.Exp, scale=-1.0), reads=[pl[9]], writes=[pl[9]])
    P.op(V, lambda e: e.tensor_tensor(out=v3(mB), in0=v3(negD), in1=k.mask.t[:, 1:2, :].to_broadcast([64, nch, 64]), op=ALU.add),
         reads=[pl[8], k.mask], writes=[pl[10]])
    P.op(A_, lambda e: e.activation(out=mB, in_=mB, func=AF.Exp), reads=[pl[10]], writes=[pl[10]])
    if not ident_T:
        A0 = pl[11].t[0:64, 0:T]
        P.op(V, lambda e: e.tensor_tensor(out=v3(A0), in0=v3(k.ps[3].t[0:64, 0:T]), in1=bcast_mid(k.gcol.t[:, 1, 0:nch, h], 64), op=ALU.mult),
             reads=[k.ps[3], k.gcol], writes=[pl[11]])
        P.op(V, lambda e: e.tensor_mul(out=A0, in0=A0, in1=mA), reads=[pl[11], pl[9]], writes=[pl[11]])
        P.op(V, lambda e: e.tensor_tensor(out=v3(A0), in0=v3(A0), in1=k.mask.t[:, 2:3, :].to_broadcast([64, nch, 64]), op=ALU.mult),
             reads=[pl[11], k.mask], writes=[pl[11]])
    QKTm = bfview(pl[18])[0:64, 512:512 + T]
    P.op(V, lambda e: e.tensor_mul(out=QKTm, in0=k.ps[4].t[0:64, 0:T], in1=mB), reads=[k.ps[4], pl[10]], writes=[pl[18]])
    if KP <= 3:
        return
    for src, srcb, psb, outs in ((knT, pl[4], k.ps[5], ((0, 2), (1, 3))), (vTb, pl[2], k.ps[6], ((2, 1),))):
        pv = psbf(psb)

        def fn(e, src=src, pv=pv):
            ins = None
            for c in range(nch):
                ins = e.transpose(pv[0:64, c * 128:(c + 1) * 128], src[:, c * 64:(c + 1) * 64], k.identb.t[:])
            return ins
        P.op("pe", fn, reads=[srcb, k.identb], writes=[psb])
        for oi, qi in outs:
            P.op(V, lambda e, oi=oi, qi=qi, pv=pv: e.tensor_tensor(
                out=tokv(k, oi)[:, 0:nch, :], in0=pv[0:64, 0:nch * 128].rearrange("p (c d) -> p c d", d=128),
                in1=bcast_mid(k.gcol.t[:, qi, 0:nch, h], 128), op=ALU.mult), reads=[psb, k.gcol], writes=[pl[8 + oi]])
    if not ident_T:
        AT0 = pl[12].t[0:64, 0:T]

        def fnT(e):
            ins = None
            for c in range(nch):
                ins = e.transpose(k.ps[0].t[0:64, c * 64:(c + 1) * 64], A0[:, c * 64:(c + 1) * 64], k.ident.t[0:64, 0:64])
            return ins
        P.op("pe", fnT, reads=[pl[11], k.ident], writes=[k.ps[0]])
        P.op(A_, lambda e: e.activation(out=AT0, in_=k.ps[0].t[0:64, 0:T], func=AF.Copy), reads=[k.ps[0]], writes=[pl[12]])
        PT = [pl[16], pl[17]]
        I3 = k.ident.t[0:64, 0:64].unsqueeze(1).to_broadcast([64, nch, 64])
        P.op(V, lambda e: e.scalar_tensor_tensor(out=v3(PT[0].t[0:64, 0:T]), in0=v3(AT0), scalar=-1.0, in1=I3, op0=ALU.mult, op1=ALU.add),
             reads=[pl[12], k.ident], writes=[PT[0]])
        X = [pl[11], pl[13]]
        XT = [pl[12], pl[14]]
        IX = pl[15]
        cur = 0
        for lev in range(5):
            nxt = 1 - cur
            lowp = lev >= 1
            sel_ = (lambda b_: bfview(b_)[0:64, 0:T]) if lowp else (lambda b_: b_.t[0:64, 0:T])
            nsel = lambda b_: bfview(b_)[0:64, 0:T]
            Xc, XTc = sel_(X[cur]), sel_(XT[cur])

            def fnX(e, Xc=Xc, XTc=XTc):
                ins = None
                for c in range(nch):
                    sl = slice(c * 64, (c + 1) * 64)
                    ins = e.matmul(k.ps[0].t[0:64, sl], XTc[:, sl], Xc[:, sl], start=True, stop=True)
                return ins
            P.op("pe", fnX, reads=[X[cur], XT[cur]], writes=[k.ps[0]])
            if lev < 4:
                def fnXT(e, Xc=Xc, XTc=XTc):
                    ins = None
                    for c in range(nch):
                        sl = slice(c * 64, (c + 1) * 64)
                        ins = e.matmul(k.ps[1].t[0:64, sl], Xc[:, sl], XTc[:, sl], start=True, stop=True)
                    return ins
                P.op("pe", fnXT, reads=[X[cur], XT[cur]], writes=[k.ps[1]])
                P.op(A_, lambda e, nxt=nxt: e.activation(out=nsel(X[nxt]), in_=k.ps[0].t[0:64, 0:T], func=AF.Copy),
                     reads=[k.ps[0]], writes=[X[nxt]])
                P.op(A_, lambda e, nxt=nxt: e.activation(out=nsel(XT[nxt]), in_=k.ps[1].t[0:64, 0:T], func=AF.Copy),
                     reads=[k.ps[1]], writes=[XT[nxt]])
            IXv = sel_(IX)
            P.op(V, lambda e, IXv=IXv: e.tensor_tensor(out=v3(IXv), in0=v3(k.ps[0].t[0:64, 0:T]), in1=I3, op=ALU.add),
                 reads=[k.ps[0], k.ident], writes=[IX])
            pin, pout = PT[lev % 2], PT[(lev + 1) % 2]
            pinv = sel_(pin)

            def fnP(e, pinv=pinv, IXv=IXv):
                ins = None
                for c in range(nch):
                    sl = slice(c * 64, (c + 1) * 64)
                    ins = e.matmul(k.ps[2].t[0:64, sl], IXv[:, sl], pinv[:, sl], start=True, stop=True)
                return ins
            P.op("pe", fnP, reads=[IX, pin], writes=[k.ps[2]])
            if lev < 4:
                P.op(V, lambda e, pout=pout: e.tensor_copy(out=nsel(pout), in_=k.ps[2].t[0:64, 0:T]), reads=[k.ps[2]], writes=[pout])
            cur = nxt
    TTb = bfview(pl[18])[0:64, 0:T]
    if ident_T:
        P.op(V, lambda e: e.tensor_copy(out=v3(TTb), in_=k.identb.t[0:64, 0:64].unsqueeze(1).to_broadcast([64, nch, 64])),
             reads=[k.identb], writes=[pl[18]])
    else:
        P.op(V, lambda e: e.tensor_copy(out=TTb, in_=k.ps[2].t[0:64, 0:T]), reads=[k.ps[2]], writes=[pl[18]])

    def fnW(e):
        ins = None
        for c in range(nch):
            ins = e.matmul(k.ps[3].t[:, c * 64:(c + 1) * 64], tokv(k, 0)[:, c, :], TTb[:, c * 64:(c + 1) * 64], start=True, stop=True)
        return ins
    P.op("pe", fnW, reads=[pl[8], pl[18]], writes=[k.ps[3]])
    P.op(A_, lambda e: e.activation(out=nWT, in_=k.ps[3].t[:, 0:T], func=AF.Copy, scale=-1.0), reads=[k.ps[3]], writes=[pl[5]])


def dn_chunk(k, c, S_ap, Sb_ap, Sres, Sbres, h, egl_ap, T_cols, fill=None):
    P, pl = k.P, k.pool
    sl = slice(c * 64, (c + 1) * 64)
    TTb = bfview(pl[18])[0:64, sl]
    QKTm = bfview(pl[18])[0:64, 512 + c * 64:512 + (c + 1) * 64]
    qdT = bfview(pl[5])[:, sl]
    nWT = bfview(pl[5])[:, 512 + c * 64:512 + (c + 1) * 64]
    psv = k.ps[4]
    mm(k, psv, psv.t[0:64, 0:128], [(TTb, tokv(k, 2)[:, c, :]), (nWT, Sb_ap)], [pl[18], pl[10], pl[5], Sbres])
    if fill:
        fill()
    vb_ = k.vnewb[c % 2]
    vnew = vb_.t[:]
    P.op("act", lambda e: e.activation(out=vnew, in_=psv.t[0:64, 0:128], func=AF.Copy), reads=[psv], writes=[vb_])
    pso = k.ps[5 + (c % 8) // 4]
    oreg = pso.t[0:64, (c % 4) * 128:(c % 4 + 1) * 128]
    mm(k, pso, oreg, [(qdT, Sb_ap), (QKTm, vnew)], [pl[5], Sbres, pl[18], vb_])
    pss = k.ps[7]
    mm(k, pss, pss.t[:, 0:128], [(tokv(k, 1)[:, c, :], vnew)], [pl[9], vb_])
    if fill:
        fill()
    P.op("dve", lambda e: e.scalar_tensor_tensor(out=Sb_ap, in0=S_ap, scalar=egl_ap, in1=pss.t[:, 0:128], op0=ALU.mult, op1=ALU.add),
         reads=[Sres, pss, pl[7]], writes=[Sbres])
    P.op("dve", lambda e: e.scalar_tensor_tensor(out=S_ap, in0=S_ap, scalar=egl_ap, in1=pss.t[:, 0:128], op0=ALU.mult, op1=ALU.add),
         reads=[Sres, pss, pl[7]], writes=[Sres])


def dn_finish_head(k, l, h, nch, zs_ap, zs_res, out_ap, out_buf, stride_tok=None):
    P, pl = k.P, k.pool
    for b in range((nch + 3) // 4):
        n = min(4, nch - 4 * b)
        P.op("dve", lambda e, b=b, n=n: e.tensor_copy(out=k.oraw.t[:, 4 * b:4 * b + n, :].rearrange("p c d -> p (c d)"),
                                                       in_=k.ps[5 + b].t[0:64, 0:n * 128]), reads=[k.ps[5 + b]], writes=[k.oraw])
    sq = pl[3].t[0:64, :]
    T2 = nch * 128
    sqv = (sq if nch <= 4 else None)
    ssq = k.small.t[0:64, 16:16 + nch]
    for b in range((nch + 3) // 4):
        n = min(4, nch - 4 * b)
        P.op("act", lambda e, b=b, n=n: e.activation(out=sq[:, 0:n * 128], in_=k.oraw.t[:, 4 * b:4 * b + n, :].rearrange("p c d -> p (c d)"),
                                                      func=AF.Square), reads=[k.oraw], writes=[pl[3]])
        P.op("dve", lambda e, b=b, n=n: e.tensor_reduce(out=k.small.t[0:64, 16 + 4 * b:16 + 4 * b + n],
                                                         in_=sq[:, 0:n * 128].rearrange("p (c d) -> p c d", d=128),
                                                         axis=AX.X, op=ALU.add), reads=[pl[3]], writes=[k.small])
    P.op("dve", lambda e: e.tensor_scalar(out=ssq, in0=ssq, scalar1=1.0 / 128, scalar2=1e-6, op0=ALU.mult, op1=ALU.add),
         reads=[k.small], writes=[k.small])
    P.op("act", lambda e: e.activation(out=ssq, in_=ssq, func=AF.Ln), reads=[k.small], writes=[k.small])
    P.op("act", lambda e: e.activation(out=ssq, in_=ssq, func=AF.Exp, scale=-0.5), reads=[k.small], writes=[k.small])
    P.op("dve", lambda e: e.tensor_tensor(out=k.onb.t[:, 0:nch, :], in0=k.oraw.t[:, 0:nch, :], in1=bcast_mid(ssq, 128), op=ALU.mult),
         reads=[k.oraw, k.small], writes=[k.onb])
    psb = k.ps[7]
    pv = psbf(psb)

    def fn(e):
        ins = None
        for c in range(nch):
            ins = e.transpose(pv[:, c * 64:(c + 1) * 64], k.onb.t[:, c, :], k.identb.t[0:64, 0:64])
        return ins
    P.op("pe", fn, reads=[k.onb, k.identb], writes=[psb])
    P.op("dve", lambda e: e.scalar_tensor_tensor(out=out_ap, in0=pv[:, 0:nch * 64], scalar=k.dnw.t[:, l:l + 1], in1=zs_ap,
                                                 op0=ALU.mult, op1=ALU.mult), reads=[psb, k.dnw, zs_res], writes=[out_buf])


def conv_silu(k, l, qi, h, psb, T, out_ap, out_buf, acc_buf):
    P = k.P
    ch = qi * 8 + h
    pre = k.pre.t
    P.op("dve", lambda e: e.tensor_copy(out=pre[:, qi, 0:3], in_=k.convhist.t[:, l, ch, :]), reads=[k.convhist], writes=[k.pre])
    P.op("act", lambda e: e.activation(out=pre[:, qi, 3:3 + T], in_=psb.t[:, 0:T], func=AF.Copy), reads=[psb], writes=[k.pre])
    acc = acc_buf.t[:, 0:T]
    w = k.convw.t
    P.op("dve", lambda e: e.tensor_scalar_mul(out=acc, in0=pre[:, qi, 0:T], scalar1=w[:, l, ch, 0:1]), reads=[k.pre, k.convw], writes=[acc_buf])
    for i in range(1, 4):
        P.op("dve", lambda e, i=i: e.scalar_tensor_tensor(out=acc, in0=pre[:, qi, i:i + T], scalar=w[:, l, ch, i:i + 1], in1=acc,
                                                           op0=ALU.mult, op1=ALU.add), reads=[k.pre, k.convw, acc_buf], writes=[acc_buf])
    P.op("act", lambda e: e.activation(out=out_ap, in_=acc, func=AF.Silu), reads=[acc_buf], writes=[out_buf])
    P.op("dve", lambda e: e.tensor_copy(out=k.convhist.t[:, l, ch, :], in_=pre[:, qi, T:T + 3]), reads=[k.pre], writes=[k.convhist])


def dn_head_prompt(k, l, h):
    P, pl = k.P, k.pool
    T = 512
    W = k.I["w_in"][l]
    if h == 0:
        for qi in range(4):
            proj_F(k, l, W, qi * 1024 + h * 128, 128, k.ps[qi], T)
    conv_silu(k, l, 0, h, k.ps[0], T, pl[0].t[:, 0:T], pl[0], pl[0])
    conv_silu(k, l, 1, h, k.ps[1], T, pl[1].t[:, 0:T], pl[1], pl[1])
    conv_silu(k, l, 2, h, k.ps[2], T, bfview(pl[2])[:, 0:T], pl[2], pl[3])
    zs = bfview(pl[2])[:, 512:512 + T]
    P.op("act", lambda e: e.activation(out=zs, in_=k.ps[3].t[:, 0:T], func=AF.Silu), reads=[k.ps[3]], writes=[pl[2]])
    dn_prep_head(k, l, h, 8)
    st = {"i": 0, "wv": None, "slot": None}

    def fill():
        i = st["i"]
        if h + 1 >= 8 or i >= 16:
            return
        qi, pc = i // 4, i % 4
        if pc == 0:
            st["slot"], st["wv"] = wload(k, W, 0, 16, qi * 1024 + (h + 1) * 128, 128)
        wv, slot, psb = st["wv"], st["slot"], k.ps[qi]

        def fn(e):
            ins = None
            for kc in range(4 * pc, 4 * pc + 4):
                ins = e.matmul(psb.t[:, 0:T], wv[:, kc, :], k.hT.t[:, kc, 0:T], start=(kc == 0), stop=(kc == 15))
            return ins
        P.op("pe", fn, reads=[slot, k.hT], writes=[psb])
        st["i"] = i + 1
    for c in range(8):
        dn_chunk(k, c, k.S.t[:, l, h, :], k.Sb.t[:, l, h, :], k.S, k.Sb, h, pl[7].t[:, c * 64 + 63:c * 64 + 64], None, fill=fill)
    dn_finish_head(k, l, h, 8, zs, pl[2], k.oT.t[:, h, 0:T], k.oT)


def gelu_glu_chunk(k, l, j, T, y_ps, uT, ubuf):
    P, pl = k.P, k.pool
    y = pl[12].t[:, 0:T]
    P.op("dve", lambda e: e.scalar_tensor_tensor(out=y, in0=uT, scalar=k.s5d.t[:, l, j:j + 1], in1=y_ps.t[:, 0:T], op0=ALU.mult, op1=ALU.add),
         reads=[ubuf, k.s5d, y_ps], writes=[pl[12]])
    t = k.pre.t[:, 0, 0:T]
    P.op("act", lambda e: e.activation(out=t, in_=y, func=AF.Square), reads=[pl[12]], writes=[k.pre])
    P.op("pool", lambda e: e.tensor_scalar(out=t, in0=t, scalar1=0.044715, scalar2=1.0, op0=ALU.mult, op1=ALU.add), reads=[k.pre], writes=[k.pre])
    P.op("pool", lambda e: e.tensor_mul(out=t, in0=t, in1=y), reads=[k.pre, pl[12]], writes=[k.pre])
    P.op("act", lambda e: e.activation(out=t, in_=t, func=AF.Sigmoid, scale=1.5957691216057308), reads=[k.pre], writes=[k.pre])
    P.op("pool", lambda e: e.tensor_mul(out=k.big16.t[:, j, 0:T], in0=t, in1=y), reads=[k.pre, pl[12]], writes=[k.big16])


def s5_prompt(k, l):
    P, pl = k.P, k.pool
    T = 512
    V = "dve"
    tabs = [k.tab[0], Buf(k.pool_t[:, 13:15, :], "tab1")]
    tabs[1].res = pl[13].res
    tab1_extra = pl[14]
    tsl = [[pl[i] for i in range(2, 10)], [pl[i] for i in range(15, 23)]]
    xbs = [pl[10], pl[23]]
    bus = [(k.ps[1], k.ps[2]), (k.ps[4], k.ps[5])]
    uTs = [pl[0], pl[1]]
    ctx = {}

    def prologue(j):
        bb = k.blk[j % 2]
        P.op("sp", lambda e: e.dma_start(out=bb.t[:], in_=k.blkd[l, j]), reads=[k.blkd_res], writes=[bb], dma=True)
        proj_F(k, l, k.I["w_in"][l], 4112 + j * 128, 128, k.ps[0], T)
        ub = uTs[j % 2]
        uT = ub.t[:, 0:T]
        uTb = bfview(pl[11])[:, (j % 2) * 512:(j % 2) * 512 + T]
        P.op("act", lambda e: e.activation(out=uT, in_=k.ps[0].t[:, 0:T], func=AF.Copy), reads=[k.ps[0]], writes=[ub])
        P.op("dve", lambda e: e.tensor_copy(out=uTb, in_=k.ps[0].t[:, 0:T]), reads=[k.ps[0]], writes=[pl[11]])
        ctx[j] = (bb, uT, ub, uTb)

    def stageA(mt):
        j, mm_, p = mt // 4, mt % 4, mt % 2
        bb, uT, ub, uTb = ctx[j]
        tb = tabs[p]
        tres = [tb] if p == 0 else [tb, tab1_extra]
        tv = tb.t if p == 0 else tb.t
        P.op("sp", lambda e: e.dma_start(out=tv[:] if p == 0 else tv, in_=k.tabd[l, mt]), reads=[k.tabd_res], writes=tres, dma=True)
        p1, p2 = bus[p]
        mm(k, p1, p1.t[:, 0:T], [(bb.t[:, mm_ * 4 + 0, :], uTb)], [bb, pl[11]])
        mm(k, p2, p2.t[:, 0:T], [(bb.t[:, mm_ * 4 + 1, :], uTb)], [bb, pl[11]])
        cT, sT = tv[:, 0, :], tv[:, 1, :]
        t = [x.t[:, 0:T] for x in tsl[p]]
        tr_ = tsl[p]
        b1, b2 = p1.t[:, 0:T], p2.t[:, 0:T]
        P.op(V, lambda e: e.tensor_mul(out=t[0], in0=cT, in1=b1), reads=tres + [p1], writes=[tr_[0]])
        P.op(V, lambda e: e.tensor_mul(out=t[1], in0=sT, in1=b2), reads=tres + [p2], writes=[tr_[1]])
        P.op(V, lambda e: e.tensor_mul(out=t[2], in0=cT, in1=b2), reads=tres + [p2], writes=[tr_[2]])
        P.op(V, lambda e: e.tensor_mul(out=t[3], in0=sT, in1=b1), reads=tres + [p1], writes=[tr_[3]])

    def stageB(mt):
        p = mt % 2
        t = [x.t[:, 0:T] for x in tsl[p]]
        tr_ = tsl[p]
        P.op("pool", lambda e: e.tensor_add(out=t[0], in0=t[0], in1=t[1]), reads=[tr_[0], tr_[1]], writes=[tr_[0]])
        P.op("pool", lambda e: e.tensor_sub(out=t[2], in0=t[2], in1=t[3]), reads=[tr_[2], tr_[3]], writes=[tr_[2]])
        magb = k.s5m.t[:, l, 0, mt:mt + 1].to_broadcast([128, T])
        P.op(V, lambda e: e.tensor_tensor_scan(out=t[4], data0=magb, data1=t[0], initial=k.s5x.t[:, l, 0, mt:mt + 1],
                                               op0=ALU.mult, op1=ALU.add), reads=[k.s5m, tr_[0], k.s5x], writes=[tr_[4]])
        P.op(V, lambda e: e.tensor_tensor_scan(out=t[5], data0=magb, data1=t[2], initial=k.s5x.t[:, l, 1, mt:mt + 1],
                                               op0=ALU.mult, op1=ALU.add), reads=[k.s5m, tr_[2], k.s5x], writes=[tr_[5]])

    def stageC(mt):
        j, mm_, p = mt // 4, mt % 4, mt % 2
        bb, uT, ub, uTb = ctx[j]
        tb = tabs[p]
        tres = [tb] if p == 0 else [tb, tab1_extra]
        cT, sT = tb.t[:, 0, :], tb.t[:, 1, :]
        t = [x.t[:, 0:T] for x in tsl[p]]
        tr_ = tsl[p]
        xb = xbs[p]
        xre = bfview(xb)[:, 0:T]
        xim = bfview(xb)[:, 512:512 + T]
        P.op("pool", lambda e: e.tensor_mul(out=t[6], in0=cT, in1=t[4]), reads=tres + [tr_[4]], writes=[tr_[6]])
        P.op(V, lambda e: e.tensor_mul(out=t[7], in0=sT, in1=t[5]), reads=tres + [tr_[5]], writes=[tr_[7]])
        P.op(V, lambda e: e.tensor_sub(out=xre, in0=t[6], in1=t[7]), reads=[tr_[6], tr_[7]], writes=[xb])
        sx = k.small.t[:, 32 + 2 * p:32 + 2 * p + 1]
        P.op("pool", lambda e: e.tensor_sub(out=sx, in0=t[6][:, T - 1:T], in1=t[7][:, T - 1:T]), reads=[tr_[6], tr_[7]], writes=[k.small])
        P.op("pool", lambda e: e.tensor_mul(out=t[6], in0=cT, in1=t[5]), reads=tres + [tr_[5]], writes=[tr_[6]])
        P.op(V, lambda e: e.tensor_mul(out=t[7], in0=sT, in1=t[4]), reads=tres + [tr_[4]], writes=[tr_[7]])
        P.op(V, lambda e: e.tensor_add(out=xim, in0=t[6], in1=t[7]), reads=[tr_[6], tr_[7]], writes=[xb])
        P.op("pool", lambda e: e.tensor_add(out=k.s5x.t[:, l, 1, mt:mt + 1], in0=t[6][:, T - 1:T], in1=t[7][:, T - 1:T]),
             reads=[tr_[6], tr_[7]], writes=[k.s5x])
        P.op("pool", lambda e: e.tensor_copy(out=k.s5x.t[:, l, 0, mt:mt + 1], in_=sx), reads=[k.small], writes=[k.s5x])

        def fy(e):
            e.matmul(k.ps[3].t[:, 0:T], bb.t[:, mm_ * 4 + 2, :], xre, start=(mm_ == 0), stop=False)
            return e.matmul(k.ps[3].t[:, 0:T], bb.t[:, mm_ * 4 + 3, :], xim, start=False, stop=(mm_ == 3))
        P.op("pe", fy, reads=[bb, xb], writes=[k.ps[3]])
        if mm_ == 3:
            gelu_glu_chunk(k, l, j, T, k.ps[3], uT, ub)

    for step in range(32 + 2):
        if 0 <= step - 2 < 32:
            stageC(step - 2)
        if step < 32:
            if step % 4 == 0:
                prologue(step // 4)
            stageA(step)
        if 0 <= step - 1 < 32:
            stageB(step - 1)


def glu_stage(k, l, T):
    P, pl = k.P, k.pool
    for jo in range(8):
        psb = k.ps[4 + jo % 2]
        proj_F(k, l, k.I["w_glu"][l], jo * 128, 128, psb, T, rhs=k.big16.t, nk=8, rhs_buf=k.big16)
        sg = pl[13 + jo % 2]
        P.op("act", lambda e, sg=sg, psb=psb: e.activation(out=sg.t[:, 0:T], in_=psb.t[:, 0:T], func=AF.Sigmoid), reads=[psb], writes=[sg])
        P.op("dve", lambda e, sg=sg, jo=jo: e.tensor_mul(out=k.g5T.t[:, jo, 0:T], in0=sg.t[:, 0:T], in1=k.big16.t[:, jo, 0:T]),
             reads=[sg, k.big16], writes=[k.g5T])


def merge_stage(k, l, T):
    P, pl = k.P, k.pool
    for c2 in range(8):
        sdn, vdn = wload(k, k.I["w_br_dn"][l], 0, 8, c2 * 256, 256)
        ss5, vs5 = wload(k, k.I["w_br_s5"][l], 0, 8, c2 * 256, 256)
        sgd, vgd = wload(k, k.I["w_in"][l], 0, 16, 5136 + c2 * 256, 256)
        sgs, vgs = wload(k, k.I["w_in"][l], 0, 16, 7184 + c2 * 256, 256)
        for d in range(2):
            c = 2 * c2 + d
            cs = slice(d * 128, (d + 1) * 128)
            mm(k, k.ps[0], k.ps[0].t[:, 0:T], [(vdn[:, kc, cs], k.oT.t[:, kc, 0:T]) for kc in range(8)], [sdn, k.oT])
            mm(k, k.ps[1], k.ps[1].t[:, 0:T], [(vs5[:, kc, cs], k.g5T.t[:, kc, 0:T]) for kc in range(8)], [ss5, k.g5T])
            mm(k, k.ps[2], k.ps[2].t[:, 0:T], [(vgd[:, kc, cs], k.hT.t[:, kc, 0:T]) for kc in range(16)], [sgd, k.hT])
            mm(k, k.ps[3], k.ps[3].t[:, 0:T], [(vgs[:, kc, cs], k.hT.t[:, kc, 0:T]) for kc in range(16)], [sgs, k.hT])
            a, b = pl[0].t[:, 0:T], pl[1].t[:, 0:T]
            P.op("act", lambda e, a=a: e.activation(out=a, in_=k.ps[2].t[:, 0:T], func=AF.Sigmoid), reads=[k.ps[2]], writes=[pl[0]])
            P.op("act", lambda e, b=b: e.activation(out=b, in_=k.ps[3].t[:, 0:T], func=AF.Sigmoid), reads=[k.ps[3]], writes=[pl[1]])
            P.op("dve", lambda e, a=a: e.tensor_mul(out=a, in0=a, in1=k.ps[0].t[:, 0:T]), reads=[pl[0], k.ps[0]], writes=[pl[0]])
            P.op("dve", lambda e, b=b: e.tensor_mul(out=b, in0=b, in1=k.ps[1].t[:, 0:T]), reads=[pl[1], k.ps[1]], writes=[pl[1]])
            P.op("pool", lambda e, a=a, b=b, c=c: e.tensor_add(out=k.big16.t[:, c, 0:T], in0=a, in1=b), reads=[pl[0], pl[1]], writes=[k.big16])


def tokmajor_proj(k, G, wsrc, r0, nk_list, act_c0):
    P = k.P
    np_ = G.np
    for fb in range(4):
        slots = []
        rr = r0
        for nk in nk_list:
            slots.append(wload(k, wsrc, rr, nk, fb * 512, 512) + (nk,))
            rr += nk * 128
        for s in range(G.nsub):
            pairs = []
            ci = act_c0
            for slot, wv, nk in slots:
                for kc in range(nk):
                    pairs.append((k.big16.t[:, ci, s * np_:(s + 1) * np_], wv[:, kc, :]))
                    ci += 1
            psb = k.ps[s]
            mm(k, psb, psb.t[0:np_, :], pairs, [sl[0] for sl in slots] + [k.big16])
            xv = k.xres.t[0:np_, s, fb * 512:(fb + 1) * 512]
            P.op("dve", lambda e, xv=xv, psb=psb: e.tensor_tensor(out=xv, in0=xv, in1=psb.t[0:np_, :], op=ALU.add),
                 reads=[k.xres, psb], writes=[k.xres])


def ffn_stage(k, G, l):
    P, pl = k.P, k.pool
    T = G.T
    for qd in range(4):
        i = 0
        while i < 11:
            n2 = 2 if i + 1 < 11 else 1
            hc = 11 * qd + i
            sg_, vg = wload(k, k.I["w_ffn_gate"][l], 0, 16, hc * 128, 128 * n2)
            su_, vu = wload(k, k.I["w_ffn_up"][l], 0, 16, hc * 128, 128 * n2)
            for d in range(n2):
                ii = i + d
                pg, pu = k.ps[4 + (ii % 2) * 2], k.ps[5 + (ii % 2) * 2]
                mm(k, pg, pg.t[:, 0:T], [(vg[:, kc, d * 128:(d + 1) * 128], k.hT.t[:, kc, 0:T]) for kc in range(16)], [sg_, k.hT])
                mm(k, pu, pu.t[:, 0:T], [(vu[:, kc, d * 128:(d + 1) * 128], k.hT.t[:, kc, 0:T]) for kc in range(16)], [su_, k.hT])
                sg = pl[ii % 2]
                P.op("act", lambda e, sg=sg, pg=pg: e.activation(out=sg.t[:, 0:T], in_=pg.t[:, 0:T], func=AF.Silu), reads=[pg], writes=[sg])
                P.op("dve", lambda e, sg=sg, pu=pu, ii=ii: e.tensor_mul(out=k.big16.t[:, ii, 0:T], in0=sg.t[:, 0:T], in1=pu.t[:, 0:T]),
                     reads=[sg, pu], writes=[k.big16])
            i += n2
        tokmajor_proj(k, G, k.I["w_ffn_down"][l], 11 * qd * 128, [8, 3], 0)


def final_norm_store(k, G, out_ap_rows):
    P, pl = k.P, k.pool
    np_ = G.np
    for fb in range(4):
        P.op("sp", lambda e, fb=fb: e.dma_start(out=pl[fb].t[:], in_=k.I["norm_f"][fb * 512:(fb + 1) * 512].partition_broadcast(128)),
             writes=[pl[fb]], dma=True)
    for s in range(G.nsub):
        xs = k.xres.t[0:np_, s, :]
        c0 = k.small.t[0:np_, 2 * s:2 * s + 1]
        c1 = k.small.t[0:np_, 2 * s + 1:2 * s + 2]
        xn = k.xn.t[0:np_, :]
        P.op("dve", lambda e, c0=c0: e.memset(c0, 0.0), writes=[k.small])
        P.op("act", lambda e, xn=xn, xs=xs, c0=c0: e.activation(out=xn, in_=xs, func=AF.Square, accum_out=c0),
             reads=[k.xres, k.small], writes=[k.xn, k.small])
        P.op("dve", lambda e, c0=c0, c1=c1: e.tensor_scalar(out=c1, in0=c0, scalar1=1.0 / D, scalar2=1e-6, op0=ALU.mult, op1=ALU.add),
             reads=[k.small], writes=[k.small])
        P.op("act", lambda e, c1=c1: e.activation(out=c1, in_=c1, func=AF.Sqrt), reads=[k.small], writes=[k.small])
        P.op("dve", lambda e, c1=c1: e.reciprocal(out=c1, in_=c1), reads=[k.small], writes=[k.small])
        for fb in range(4):
            xv = k.xres.t[0:np_, s, fb * 512:(fb + 1) * 512]
            P.op("dve", lambda e, xv=xv, c1=c1, fb=fb: e.scalar_tensor_tensor(out=xv, in0=xv, scalar=c1, in1=pl[fb].t[0:np_, :],
                                                                             op0=ALU.mult, op1=ALU.mult), reads=[k.xres, k.small, pl[fb]], writes=[k.xres])
        dst = out_ap_rows(s)
        P.op("sp", lambda e, dst=dst, xs=xs: e.dma_start(out=dst, in_=xs), reads=[k.xres], writes=[k.onext()], dma=True)


def run_prompt_tile(k, ti, n_ptiles, n_layers):
    P = k.P
    G = Grp(512, 128, 4, False)
    t0 = ti * 512
    for s in range(4):
        P.op("sp", lambda e, s=s: e.dma_start(out=k.xres.t[:, s, :], in_=k.I["x_prompt"][t0 + s * 128:t0 + (s + 1) * 128, :]),
             writes=[k.xres], dma=True)
    import os
    STOP = int(os.environ.get("KSTOP", "99"))
    if STOP <= 0:
        return
    for l in range(n_layers):
        norm_stage(k, G, k.n1w, l)
        if STOP <= 1:
            return
        gates_stage(k, G, l)
        gates_derive(k, 8)
        if STOP <= 2:
            return
        for h in range(8):
            if os.environ.get("KSKIPDN"):
                break
            dn_head_prompt(k, l, h)
            if STOP <= 3:
                return
        if STOP <= 4:
            return
        if "oT" in k.DBG and ti == 0 and l == 0:
            P.op("pool", lambda e: e.dma_start(out=k.DBG["oT"], in_=k.oT.t[:]), reads=[k.oT], writes=[k.onext()], dma=True)
        s5_prompt(k, l)
        glu_stage(k, l, 512)
        if "g5T" in k.DBG and ti == 0 and l == 0:
            P.op("pool", lambda e: e.dma_start(out=k.DBG["g5T"], in_=k.g5T.t[:]), reads=[k.g5T], writes=[k.onext()], dma=True)
        merge_stage(k, l, 512)
        tokmajor_proj(k, G, k.I["w_out"][l], 0, [8, 8], 0)
        norm_stage(k, G, k.n2w, l)
        ffn_stage(k, G, l)
        if ti == n_ptiles - 1:
            store_prompt_states(k, l)
    final_norm_store(k, G, lambda s: k.O["y_prompt"][t0 + s * 128:t0 + (s + 1) * 128, :])


def store_prompt_states(k, l):
    P = k.P
    for r in range(3):
        P.op("sp", lambda e, r=r: e.dma_start(out=k.O["p_dn_conv"][l, r].rearrange("(c p) -> p c", p=128), in_=k.convhist.t[:, l, :, r],
                                              allow_slow_non_contiguous=True), reads=[k.convhist], writes=[k.onext()], dma=True)
    P.op("sp", lambda e: e.dma_start(out=k.O["p_dn_ssm"][l].rearrange("h a b -> a h b"), in_=k.S.t[:, l, :, :]),
         reads=[k.S], writes=[k.onext()], dma=True)
    for ri, nm in enumerate(("p_s5_re", "p_s5_im")):
        dst = bass.AP(k.O[nm].tensor, l * 4096, [[1, 128], [128, 32]])
        P.op("sp", lambda e, dst=dst, ri=ri: e.dma_start(out=dst, in_=k.s5x.t[:, l, ri, :], allow_slow_non_contiguous=True),
             reads=[k.s5x], writes=[k.onext()], dma=True)


def run_sample(k, n_layers):
    P, pl, I, O = k.P, k.pool, k.I, k.O
    G = Grp(16, 16, 1, True)
    T = 16
    c3 = lambda ap: ap.rearrange("p (c t) -> p c t", t=64)
    P.op("sp", lambda e: e.dma_start(out=k.xres.t[0:16, 0, :], in_=I["x_sample"]), writes=[k.xres], dma=True)
    for l_ in range(n_layers):
        sample_layer(k, G, l_)
    final_norm_store(k, G, lambda s: O["y_sample"][0:16, :])


def sample_layer(k, G, l):
    P, pl, I, O = k.P, k.pool, k.I, k.O
    T = 16
    c3 = lambda ap: ap.rearrange("p (c t) -> p c t", t=64)
    if True:
        W = I["w_in"][l]
        norm_stage(k, G, k.n1w, l)
        gates_stage(k, G, l)
        for r in range(2):
            P.op("dve", lambda e, r=r: e.tensor_copy(out=k.sg.t[:, r, :], in_=grow(k, r)[:, 0:16]), reads=[gbuf(k, r)], writes=[k.sg])
        P.op("sp", lambda e: e.dma_start(out=O["s_dn_conv"][l, :, 0:2, :], in_=I["state_dn_conv"][l, :, 1:3, :]),
             writes=[k.onext()], dma=True)
        for h_ in range(8):
            sample_head(k, G, l, h_)
        sample_rest(k, G, l)


def sample_head(k, G, l, h):
    P, pl, I, O = k.P, k.pool, k.I, k.O
    T = 16
    W = I["w_in"][l]
    c3 = lambda ap: ap.rearrange("p (c t) -> p c t", t=64)
    if True:
        if True:
            for qi in range(4):
                proj_F(k, l, W, qi * 1024 + h * 128, 128, k.ps[qi], T)
            for qi in range(3):
                ch = qi * 8 + h
                P.op("sp", lambda e, qi=qi, ch=ch: e.dma_start(
                    out=k.c48.t[:, qi, :], in_=I["state_dn_conv"][l, :, :, ch * 128:(ch + 1) * 128].rearrange("t r c -> (t r) c")),
                    writes=[k.c48], dma=True)
                psh = k.ps[4 + qi]
                tr(k, psh, psh.t[:, 0:48], k.c48.t[:, qi, :], k.ident.t[0:48, 0:48], [k.c48, k.ident])
                hv = psh.t[:, 0:48].rearrange("p (t r) -> p r t", r=3)
                acc = k.cacc.t[:, qi, :]
                new = k.ps[qi].t[:, 0:16]
                w = k.convw.t
                P.op("act", lambda e, ch=ch, new=new: e.activation(out=k.nr.t[:, ch, :], in_=new, func=AF.Copy), reads=[k.ps[qi]], writes=[k.nr])
                P.op("dve", lambda e, acc=acc, hv=hv, ch=ch: e.tensor_scalar_mul(out=acc, in0=hv[:, 0, :], scalar1=w[:, l, ch, 0:1]),
                     reads=[psh, k.convw], writes=[k.cacc])
                for i in (1, 2):
                    P.op("dve", lambda e, acc=acc, hv=hv, ch=ch, i=i: e.scalar_tensor_tensor(
                        out=acc, in0=hv[:, i, :], scalar=w[:, l, ch, i:i + 1], in1=acc, op0=ALU.mult, op1=ALU.add),
                        reads=[psh, k.convw, k.cacc], writes=[k.cacc])
                P.op("dve", lambda e, acc=acc, new=new, ch=ch: e.scalar_tensor_tensor(
                    out=acc, in0=new, scalar=w[:, l, ch, 3:4], in1=acc, op0=ALU.mult, op1=ALU.add),
                    reads=[k.ps[qi], k.convw, k.cacc], writes=[k.cacc])
            P.op("act", lambda e: e.activation(out=k.zs16.t[:], in_=k.ps[3].t[:, 0:16], func=AF.Silu), reads=[k.ps[3]], writes=[k.zs16])
            sample_dn_direct(k, l, h)


def sample_dn_direct(k, l, h):
    P, pl, I, O = k.P, k.pool, k.I, k.O
    V, A_ = "dve", "act"
    S16 = k.S.t[:].rearrange("p l h d -> p (l h) d")
    a, b, c, d = pl[0].t, pl[1].t, pl[2].t, pl[3].t
    q, sqq, qn = a[:, 0:16], a[:, 16:32], a[:, 32:48]
    kk_, sqk, kn = b[:, 0:16], b[:, 16:32], b[:, 32:48]
    v, bb_, egb, t1, vn, o, osq, prod = [c[:, i * 16:(i + 1) * 16] for i in range(8)]
    kq = d[:, 0:32].rearrange("p (t two) -> p t two", two=2)
    p4, p5, p6, p7 = k.ps[4], k.ps[5], k.ps[6], k.ps[7]
    P.op("sp", lambda e: e.dma_start(out=S16, in_=I["state_dn_ssm"][l, :, h].rearrange("t a b -> a t b")), writes=[k.S], dma=True)
    P.op(A_, lambda e: e.activation(out=q, in_=k.cacc.t[:, 0, :], func=AF.Silu), reads=[k.cacc], writes=[pl[0]])
    P.op(A_, lambda e: e.activation(out=kk_, in_=k.cacc.t[:, 1, :], func=AF.Silu), reads=[k.cacc], writes=[pl[1]])
    P.op(A_, lambda e: e.activation(out=v, in_=k.cacc.t[:, 2, :], func=AF.Silu), reads=[k.cacc], writes=[pl[2]])
    for x, sq, xn, buf, reg, scale, col in ((q, sqq, qn, pl[0], p5.t[:, 0:16], 128 ** -0.5, 1), (kk_, sqk, kn, pl[1], p5.t[:, 16:32], 1.0, 0)):
        P.op(A_, lambda e, x=x, sq=sq: e.activation(out=sq, in_=x, func=AF.Square), reads=[buf], writes=[buf])
        mm(k, p5, reg, [(k.ones.t[:], sq)], [k.ones, buf])
        P.op(A_, lambda e, sq=sq, reg=reg: e.activation(out=sq, in_=reg, func=AF.Sqrt, bias=k.epsc.t[:, 0:1]), reads=[p5, k.epsc], writes=[buf])
        P.op(V, lambda e, sq=sq: e.reciprocal(out=sq, in_=sq), reads=[buf], writes=[buf])
        P.op(V, lambda e, x=x, sq=sq, xn=xn, scale=scale: e.scalar_tensor_tensor(out=xn, in0=x, scalar=scale, in1=sq, op0=ALU.mult, op1=ALU.mult),
             reads=[buf], writes=[buf])
        P.op(V, lambda e, xn=xn, col=col: e.tensor_copy(out=kq[:, :, col], in_=xn), reads=[buf], writes=[pl[3]])
    gsel = d[0:8, 64:96]
    P.op(V, lambda e: e.tensor_scalar_mul(out=gsel, in0=k.sg.t[:].rearrange("p r t -> p (r t)"), scalar1=k.ident.t[0:8, h:h + 1]),
         reads=[k.sg, k.ident], writes=[pl[3]])
    mm(k, p6, p6.t[:, 0:32], [(k.ones.t[0:8, :], gsel)], [k.ones, pl[3]])
    P.op(V, lambda e: e.tensor_copy(out=bb_, in_=p6.t[:, 0:16]), reads=[p6], writes=[pl[2]])
    P.op(A_, lambda e: e.activation(out=egb, in_=p6.t[:, 16:32], func=AF.Exp), reads=[p6], writes=[pl[2]])

    def fkq(e):
        ins = None
        for tok in range(16):
            ins = e.matmul(p4.t[:, tok * 2:(tok + 1) * 2], S16[:, tok, :], kq[:, tok, :], start=True, stop=True)
        return ins
    P.op("pe", fkq, reads=[k.S, pl[3]], writes=[p4])
    kqv = p4.t[:, 0:32].rearrange("p (t two) -> p t two", two=2)
    kS, qS = kqv[:, :, 0], kqv[:, :, 1]
    P.op(V, lambda e: e.tensor_mul(out=t1, in0=egb, in1=kS), reads=[pl[2], p4], writes=[pl[2]])
    P.op(V, lambda e: e.tensor_sub(out=t1, in0=v, in1=t1), reads=[pl[2]], writes=[pl[2]])
    P.op(V, lambda e: e.tensor_mul(out=vn, in0=bb_, in1=t1), reads=[pl[2]], writes=[pl[2]])
    P.op(V, lambda e: e.tensor_mul(out=prod, in0=qn, in1=kn), reads=[pl[0], pl[1]], writes=[pl[2]])
    mm(k, p6, p6.t[:, 32:48], [(k.ones.t[:], prod)], [k.ones, pl[2]])
    P.op(V, lambda e: e.tensor_mul(out=o, in0=egb, in1=qS), reads=[pl[2], p4], writes=[pl[2]])
    P.op(V, lambda e: e.tensor_mul(out=t1, in0=vn, in1=p6.t[:, 32:48]), reads=[pl[2], p6], writes=[pl[2]])
    P.op(V, lambda e: e.tensor_add(out=o, in0=o, in1=t1), reads=[pl[2]], writes=[pl[2]])
    P.op(A_, lambda e: e.activation(out=osq, in_=o, func=AF.Square), reads=[pl[2]], writes=[pl[2]])
    mm(k, p6, p6.t[:, 48:64], [(k.ones.t[:], osq)], [k.ones, pl[2]])
    P.op(V, lambda e: e.tensor_scalar(out=osq, in0=p6.t[:, 48:64], scalar1=1.0 / 128, scalar2=1e-6, op0=ALU.mult, op1=ALU.add),
         reads=[p6], writes=[pl[2]])
    P.op(A_, lambda e: e.activation(out=osq, in_=osq, func=AF.Sqrt), reads=[pl[2]], writes=[pl[2]])
    P.op(V, lambda e: e.reciprocal(out=osq, in_=osq), reads=[pl[2]], writes=[pl[2]])
    P.op(V, lambda e: e.tensor_mul(out=o, in0=o, in1=osq), reads=[pl[2]], writes=[pl[2]])
    P.op(V, lambda e: e.scalar_tensor_tensor(out=k.oT.t[:, h, 0:16], in0=o, scalar=k.dnw.t[:, l:l + 1], in1=k.zs16.t[:], op0=ALU.mult, op1=ALU.mult),
         reads=[pl[2], k.dnw, k.zs16], writes=[k.oT])
    tr(k, p7, p7.t[0:16, 0:128], kn, k.ident.t[:], [pl[1], k.ident])
    tr(k, p7, p7.t[0:16, 128:256], vn, k.ident.t[:], [pl[2], k.ident])
    tok2 = pl[4].t[0:16, 0:256]
    P.op(V, lambda e: e.tensor_copy(out=tok2, in_=p7.t[0:16, 0:256]), reads=[p7], writes=[pl[4]])
    kexp = k.pool_t[0:16, 5:9, :].rearrange("p a b -> p (a b)").rearrange("p (t d) -> p t d", d=128)
    kex_res = [pl[5], pl[6], pl[7], pl[8]]
    P.op(V, lambda e: e.tensor_tensor(out=kexp, in0=tok2[:, 0:128].unsqueeze(1).to_broadcast([16, 16, 128]),
                                      in1=bcast_mid(k.ident.t[0:16, 0:16], 128), op=ALU.mult), reads=[pl[4], k.ident], writes=kex_res)
    for g4 in range(4):
        psb = k.ps[g4]

        def fo(e, g4=g4, psb=psb):
            ins = None
            for j in range(4):
                ins = e.matmul(psb.t[:, j * 128:(j + 1) * 128], kexp[:, 4 * g4 + j, :], tok2[:, 128:256], start=True, stop=True)
            return ins
        P.op("pe", fo, reads=kex_res + [pl[4]], writes=[psb])
        Sv = S16[:, 4 * g4:4 * g4 + 4, :]
        P.op(V, lambda e, g4=g4, Sv=Sv: e.tensor_tensor(out=Sv, in0=Sv, in1=bcast_mid(egb[:, 4 * g4:4 * g4 + 4], 128), op=ALU.mult),
             reads=[k.S, pl[2]], writes=[k.S])
        P.op(V, lambda e, Sv=Sv, psb=psb: e.tensor_tensor(out=Sv, in0=Sv, in1=psb.t[:, :].rearrange("p (t d) -> p t d", d=128), op=ALU.add),
             reads=[k.S, psb], writes=[k.S])
    P.op("sp", lambda e: e.dma_start(out=O["s_dn_ssm"][l, :, h].rearrange("t a b -> a t b"), in_=S16), reads=[k.S], writes=[k.onext()], dma=True)

def sample_rest(k, G, l):
    P, pl, I, O = k.P, k.pool, k.I, k.O
    T = 16
    W = I["w_in"][l]
    if True:
        stg_res = [pl[i] for i in range(8)]
        stg = k.pool_t[0:16, 0:8, :].rearrange("p a b -> p (a b)")
        for b_ in range(6):
            psb = k.ps[b_]

            def fnr(e, b_=b_, psb=psb):
                ins = None
                for c4 in range(4):
                    ch = 4 * b_ + c4
                    ins = e.transpose(psb.t[0:16, c4 * 128:(c4 + 1) * 128], k.nr.t[:, ch, :], k.ident.t[:])
                return ins
            P.op("pe", fnr, reads=[k.nr, k.ident], writes=[psb])
            P.op("act" if b_ % 2 else "dve", (lambda e, b_=b_, psb=psb: e.activation(out=stg[:, b_ * 512:(b_ + 1) * 512], in_=psb.t[0:16, :], func=AF.Copy))
                 if b_ % 2 else (lambda e, b_=b_, psb=psb: e.tensor_copy(out=stg[:, b_ * 512:(b_ + 1) * 512], in_=psb.t[0:16, :])),
                 reads=[psb], writes=[pl[b_]])
        P.op("sp", lambda e: e.dma_start(out=O["s_dn_conv"][l][:, 2, :], in_=stg[:, 0:3072]), reads=stg_res[0:6], writes=[k.onext()], dma=True)
        for ri, nm in enumerate(("state_s5_re", "state_s5_im")):
            P.op("sp", lambda e, nm=nm: e.dma_start(out=stg, in_=I[nm][l].rearrange("t g p -> t (g p)")), writes=stg_res, dma=True)
            psb = k.ps[6 + ri]

            def fnl(e, psb=psb):
                ins = None
                for m in range(32):
                    ins = e.transpose(psb.t[:, m * 16:(m + 1) * 16], stg[:, m * 128:(m + 1) * 128], k.ident.t[0:16, 0:16])
                return ins
            P.op("pe", fnl, reads=stg_res + [k.ident], writes=[psb])
            P.op("dve", lambda e, ri=ri, psb=psb: e.tensor_copy(out=pl[14 + ri].t[:, :], in_=psb.t[:, :]), reads=[psb], writes=[pl[14 + ri]])
        uT = pl[0].t[:, 0:128].rearrange("p (j t) -> p j t", t=16)
        uTb = bfview(pl[1])[:, 0:128].rearrange("p (j t) -> p j t", t=16)
        for j in range(8):
            proj_F(k, l, W, 4112 + j * 128, 128, k.ps[0], T)
            P.op("act", lambda e, j=j: e.activation(out=uT[:, j, :], in_=k.ps[0].t[:, 0:T], func=AF.Copy), reads=[k.ps[0]], writes=[pl[0]])
            P.op("dve", lambda e, j=j: e.tensor_copy(out=uTb[:, j, :], in_=k.ps[0].t[:, 0:T]), reads=[k.ps[0]], writes=[pl[1]])
        v3 = lambda b: b.t[:, :].rearrange("p (m t) -> p m t", t=16)
        for j in range(8):
            bb = k.blk[j % 2]
            P.op("sp", lambda e, bb=bb, j=j: e.dma_start(out=bb.t[:], in_=k.blkd[l, j]), reads=[k.blkd_res], writes=[bb], dma=True)
            for mm_ in range(4):
                mt = 4 * j + mm_
                mm(k, k.ps[1], k.ps[1].t[:, mt * 16:(mt + 1) * 16], [(bb.t[:, mm_ * 4 + 0, :], uTb[:, j, :])], [bb, pl[1]])
                mm(k, k.ps[2], k.ps[2].t[:, mt * 16:(mt + 1) * 16], [(bb.t[:, mm_ * 4 + 1, :], uTb[:, j, :])], [bb, pl[1]])
        arb = bcast_mid(k.s5m.t[:, l, 1, :], 16)
        aib = bcast_mid(k.s5m.t[:, l, 2, :], 16)
        x0r, x0i, t1, t2, x1r, x1i = v3(pl[14]), v3(pl[15]), v3(pl[2]), v3(pl[3]), v3(pl[16]), v3(pl[17])
        V = "dve"
        P.op(V, lambda e: e.tensor_tensor(out=t1, in0=x0i, in1=aib, op=ALU.mult), reads=[pl[15], k.s5m], writes=[pl[2]])
        P.op(V, lambda e: e.tensor_tensor(out=t2, in0=x0r, in1=arb, op=ALU.mult), reads=[pl[14], k.s5m], writes=[pl[3]])
        P.op(V, lambda e: e.tensor_sub(out=t2, in0=t2, in1=t1), reads=[pl[2], pl[3]], writes=[pl[3]])
        P.op(V, lambda e: e.tensor_tensor(out=x1r, in0=t2, in1=v3(k.ps[1]), op=ALU.add), reads=[pl[3], k.ps[1]], writes=[pl[16]])
        P.op(V, lambda e: e.tensor_tensor(out=t1, in0=x0r, in1=aib, op=ALU.mult), reads=[pl[14], k.s5m], writes=[pl[2]])
        P.op(V, lambda e: e.tensor_tensor(out=t2, in0=x0i, in1=arb, op=ALU.mult), reads=[pl[15], k.s5m], writes=[pl[3]])
        P.op(V, lambda e: e.tensor_add(out=t2, in0=t2, in1=t1), reads=[pl[2], pl[3]], writes=[pl[3]])
        P.op(V, lambda e: e.tensor_tensor(out=x1i, in0=t2, in1=v3(k.ps[2]), op=ALU.add), reads=[pl[3], k.ps[2]], writes=[pl[17]])
        xre = bfview(pl[10])[:, 0:512]
        xim = bfview(pl[10])[:, 512:1024]
        P.op(V, lambda e: e.tensor_copy(out=xre, in_=pl[16].t[:, :]), reads=[pl[16]], writes=[pl[10]])
        P.op(V, lambda e: e.tensor_copy(out=xim, in_=pl[17].t[:, :]), reads=[pl[17]], writes=[pl[10]])
        stg2_res = [pl[i] for i in range(2, 10)]
        stg2 = k.pool_t[0:16, 2:10, :].rearrange("p a b -> p (a b)")
        for ri, nm in enumerate(("s_s5_re", "s_s5_im")):
            for half in range(2):
                for b_ in range(4):
                    psb = k.ps[4 + b_]

                    def fns(e, ri=ri, half=half, b_=b_, psb=psb):
                        ins = None
                        for c4 in range(4):
                            m = half * 16 + b_ * 4 + c4
                            ins = e.transpose(psb.t[0:16, c4 * 128:(c4 + 1) * 128], pl[16 + ri].t[:, m * 16:(m + 1) * 16], k.ident.t[:])
                        return ins
                    P.op("pe", fns, reads=[pl[16 + ri], k.ident], writes=[psb])
                    col = (half * 4 + b_) * 512
                    if b_ % 2:
                        P.op("act", lambda e, col=col, psb=psb: e.activation(out=stg2[:, col:col + 512], in_=psb.t[0:16, :], func=AF.Copy),
                             reads=[psb], writes=[pl[2 + half * 4 + b_]])
                    else:
                        P.op("dve", lambda e, col=col, psb=psb: e.tensor_copy(out=stg2[:, col:col + 512], in_=psb.t[0:16, :]),
                             reads=[psb], writes=[pl[2 + half * 4 + b_]])
            P.op("sp", lambda e, nm=nm: e.dma_start(out=O[nm][l].rearrange("t g p -> t (g p)"), in_=stg2), reads=stg2_res, writes=[k.onext()], dma=True)
        for j in range(8):
            bb = k.blk[j % 2]
            P.op("sp", lambda e, bb=bb, j=j: e.dma_start(out=bb.t[:], in_=k.blkd[l, j]), reads=[k.blkd_res], writes=[bb], dma=True)

            def fy(e, j=j, bb=bb):
                ins = None
                for mm_ in range(4):
                    mt = 4 * j + mm_
                    e.matmul(k.ps[3].t[:, j * 16:(j + 1) * 16], bb.t[:, mm_ * 4 + 2, :], xre[:, mt * 16:(mt + 1) * 16], start=(mm_ == 0), stop=False)
                    ins = e.matmul(k.ps[3].t[:, j * 16:(j + 1) * 16], bb.t[:, mm_ * 4 + 3, :], xim[:, mt * 16:(mt + 1) * 16], start=False, stop=(mm_ == 3))
                return ins
            P.op("pe", fy, reads=[bb, pl[10]], writes=[k.ps[3]])
        y = pl[11].t[:, 0:128]
        y3 = y.rearrange("p (j t) -> p j t", t=16)
        P.op(V, lambda e: e.tensor_tensor(out=y3, in0=uT, in1=bcast_mid(k.s5d.t[:, l, :], 16), op=ALU.mult), reads=[pl[0], k.s5d], writes=[pl[11]])
        P.op(V, lambda e: e.tensor_add(out=y, in0=y, in1=k.ps[3].t[:, 0:128]), reads=[pl[11], k.ps[3]], writes=[pl[11]])
        t = pl[12].t[:, 0:128]
        P.op("act", lambda e: e.activation(out=t, in_=y, func=AF.Square), reads=[pl[11]], writes=[pl[12]])
        P.op(V, lambda e: e.tensor_scalar(out=t, in0=t, scalar1=0.044715, scalar2=1.0, op0=ALU.mult, op1=ALU.add), reads=[pl[12]], writes=[pl[12]])
        P.op(V, lambda e: e.tensor_mul(out=t, in0=t, in1=y), reads=[pl[12], pl[11]], writes=[pl[12]])
        P.op("act", lambda e: e.activation(out=t, in_=t, func=AF.Sigmoid, scale=1.5957691216057308), reads=[pl[12]], writes=[pl[12]])
        P.op(V, lambda e: e.tensor_tensor(out=k.big16.t[:, 0:8, 0:16], in0=t.rearrange("p (j t) -> p j t", t=16), in1=y3, op=ALU.mult),
             reads=[pl[12], pl[11]], writes=[k.big16])
        glu_stage(k, l, T)
        merge_stage(k, l, T)
        tokmajor_proj(k, G, I["w_out"][l], 0, [8, 8], 0)
        norm_stage(k, G, k.n2w, l)
        ffn_stage(k, G, l)


def make_in_maps(inputs):
    hc = host_consts()
    maps = []
    for c in range(8):
        m = {}
        b = c % 4
        m["x_prompt"] = np.ascontiguousarray(inputs["x_prompt"][b])
        sl = slice(c * 16, (c + 1) * 16)
        m["x_sample"] = np.ascontiguousarray(inputs["x_sample"][sl, 0])
        m["state_dn_conv"] = np.ascontiguousarray(inputs["state_dn_conv"][:, sl])
        m["state_dn_ssm"] = np.ascontiguousarray(inputs["state_dn_ssm"][:, sl])
        m["state_s5_re"] = np.ascontiguousarray(inputs["state_s5_re"][:, sl])
        m["state_s5_im"] = np.ascontiguousarray(inputs["state_s5_im"][:, sl])
        for nm in ("norm1", "w_in", "dn_conv_w", "dn_a_log", "dn_dt_bias", "dn_norm_w", "w_br_dn", "s5_lam_re", "s5_lam_im",
                   "s5_log_dt", "s5_b_re", "s5_b_im", "s5_c_re", "s5_c_im", "s5_d", "w_glu", "w_br_s5", "w_out", "norm2",
                   "w_ffn_gate", "w_ffn_up", "w_ffn_down", "norm_f"):
            m[nm] = np.ascontiguousarray(inputs[nm], dtype=np.float32)
        m.update(hc)
        maps.append(m)
    return maps


def kernel(**inputs):
    nc, k = build()
    maps = make_in_maps(inputs)
    res = run_bass_kernel_spmd(nc, maps, core_ids=list(range(8)))
    R = res.results
    y_prompt = np.stack([R[b]["y_prompt"] for b in range(4)], 0)
    y_sample = np.concatenate([R[c]["y_sample"] for c in range(8)], 0)[:, None, :]
    p_dn_conv = np.stack([R[b]["p_dn_conv"] for b in range(4)], 1)
    p_dn_ssm = np.stack([R[b]["p_dn_ssm"] for b in range(4)], 1)
    p_s5_re = np.stack([R[b]["p_s5_re"] for b in range(4)], 1)
    p_s5_im = np.stack([R[b]["p_s5_im"] for b in range(4)], 1)
    s_dn_conv = np.concatenate([R[c]["s_dn_conv"] for c in range(8)], 1)
    s_dn_ssm = np.concatenate([R[c]["s_dn_ssm"] for c in range(8)], 1)
    s_s5_re = np.concatenate([R[c]["s_s5_re"] for c in range(8)], 1)
    s_s5_im = np.concatenate([R[c]["s_s5_im"] for c in range(8)], 1)
    return (y_prompt, y_sample, p_dn_conv, p_dn_ssm, p_s5_re, p_s5_im, s_dn_conv, s_dn_ssm, s_s5_re, s_s5_im)
```
